# Optimizing a Trainium2 kernel written in Bass

```python
import jax
import jax.numpy as jnp
from jax import lax
import numpy as np

D_MODEL = 1024
BATCH = 2
SEQ = 8192
DEPTH = 1

DSA_PATTERNS = ((128, 1), (512, 4), (2048, 16))
DSA_GROUPS = 3
DSA_HEADS = 8
DSA_HEAD_DIM = 64
DSA_WIDTH = DSA_HEADS * DSA_HEAD_DIM
DSA_BLOCK = 128
ROPE_THETA = 10000.0

GDN_HEADS = 8
GDN_KEY_DIM = 64
GDN_VAL_DIM = 64
GDN_K_WIDTH = GDN_HEADS * GDN_KEY_DIM
GDN_V_WIDTH = GDN_HEADS * GDN_VAL_DIM
GDN_CONV = 4
GDN_CHUNK = 64

NORM_EPS = 1e-6

IN_SIZES = (
    DSA_GROUPS * 3 * DSA_WIDTH,
    DSA_WIDTH,
    2 * GDN_K_WIDTH + GDN_V_WIDTH,
    GDN_V_WIDTH,
    GDN_HEADS,
    GDN_HEADS,
    D_MODEL,
    D_MODEL,
)
IN_WIDTH = sum(IN_SIZES)

kernel_name = 'hybrid_dilated_attn_gated_deltanet_block'


def _rmsnorm(x, w):
    xf = x.astype(jnp.float32)
    y = xf * lax.rsqrt(jnp.mean(xf * xf, axis=-1, keepdims=True) + NORM_EPS)
    return (y * w.astype(jnp.float32)).astype(x.dtype)


def _l2norm(t):
    return t * lax.rsqrt(jnp.sum(t * t, axis=-1, keepdims=True) + NORM_EPS)


def _split_columns(h):
    parts, start = [], 0
    for size in IN_SIZES:
        parts.append(h[..., start:start + size])
        start += size
    return parts


def _rope_tables(seq, dim):
    inv_freq = ROPE_THETA ** (-jnp.arange(0, dim, 2, dtype=jnp.float32) / dim)
    ang = jnp.arange(seq, dtype=jnp.float32)[:, None] * inv_freq[None, :]
    ang = jnp.concatenate([ang, ang], axis=-1)
    return jnp.cos(ang), jnp.sin(ang)


def _apply_rope(t, cos, sin):
    half = t.shape[-1] // 2
    rot = jnp.concatenate([-t[..., half:], t[..., :half]], axis=-1)
    return t * cos[:, None, None, :] + rot * sin[:, None, None, :]


def _dilated_window_attention(q, k, v, window, dilation):
    b, s, h, dh = q.shape
    n_back = window // dilation
    sub_len = s // dilation
    n_blk = -(-sub_len // DSA_BLOCK)
    pad = n_blk * DSA_BLOCK - sub_len

    def to_blocks(t):
        t = t.reshape(b, sub_len, dilation, h, dh).transpose(0, 2, 1, 3, 4)
        t = jnp.pad(t, ((0, 0), (0, 0), (0, pad), (0, 0), (0, 0)))
        return t.reshape(b, dilation, n_blk, DSA_BLOCK, h, dh)

    def with_previous_block(t):
        prev = jnp.pad(t, ((0, 0), (0, 0), (1, 0), (0, 0), (0, 0), (0, 0)))[:, :, :-1]
        return jnp.concatenate([prev, t], axis=3)

    qb = to_blocks(q)
    kb = with_previous_block(to_blocks(k))
    vb = with_previous_block(to_blocks(v))
    scores = jnp.einsum('brnqhd,brnkhd->brnhqk', qb, kb) * (dh ** -0.5)
    qi = jnp.arange(DSA_BLOCK)[:, None]
    kj = jnp.arange(2 * DSA_BLOCK)[None, :]
    dist = qi + DSA_BLOCK - kj
    key_idx = jnp.arange(n_blk)[:, None, None] * DSA_BLOCK + kj - DSA_BLOCK
    valid = (dist >= 0) & (dist <= n_back) & (key_idx >= 0)
    scores = jnp.where(valid[:, None], scores, -jnp.inf)
    m = jnp.max(scores, axis=-1, keepdims=True)
    p = jnp.exp(scores - m)
    den = jnp.sum(p, axis=-1, keepdims=True)
    o = jnp.einsum('brnhqk,brnkhd->brnqhd', p / den, vb)
    lse = (m + jnp.log(den))[..., 0].transpose(0, 1, 2, 4, 3)

    def from_blocks(t):
        t = t.reshape(b, dilation, n_blk * DSA_BLOCK, *t.shape[4:])[:, :, :sub_len]
        t = jnp.moveaxis(t, 1, 2)
        return t.reshape(b, s, *t.shape[3:])

    return from_blocks(o), from_blocks(lse)


def _dilated_mixture(q, k, v):
    outs, lses = [], []
    for g, (window, dilation) in enumerate(DSA_PATTERNS):
        o, lse = _dilated_window_attention(q[:, :, g], k[:, :, g], v[:, :, g], window, dilation)
        outs.append(o)
        lses.append(lse)
    wts = jax.nn.softmax(jnp.stack(lses), axis=0)
    o = jnp.einsum('gbsh,gbshd->bshd', wts, jnp.stack(outs))
    return o.reshape(o.shape[0], o.shape[1], -1)


def _causal_depthwise_conv(x, w):
    k, c = w.shape
    return lax.conv_general_dilated(
        x, w[:, None, :].astype(x.dtype), window_strides=(1,), padding=((k - 1, 0),),
        dimension_numbers=('NWC', 'WIO', 'NWC'), feature_group_count=c)


def _gated_delta_rule(q, k, v, g, beta):
    b, s, h, dk = q.shape
    dv = v.shape[-1]
    c = GDN_CHUNK
    n = s // c

    def chunk(t):
        return t.reshape(b, n, c, h, -1).transpose(0, 3, 1, 2, 4)

    qc, kc, vc = chunk(q), chunk(k), chunk(v)
    gc = g.reshape(b, n, c, h).transpose(0, 3, 1, 2)
    bc = beta.reshape(b, n, c, h).transpose(0, 3, 1, 2)
    G = jnp.cumsum(gc, axis=-1)
    causal = jnp.tril(jnp.ones((c, c), dtype=bool))
    strict = jnp.tril(jnp.ones((c, c), dtype=bool), k=-1)
    decay_incl = jnp.exp(jnp.where(causal, G[..., :, None] - G[..., None, :], -jnp.inf))
    decay_strict = jnp.where(strict, decay_incl, 0.0)
    k_beta = kc * bc[..., None]
    a = jnp.einsum('bhnid,bhnjd->bhnij', k_beta, kc) * decay_strict
    eye = jnp.eye(c, dtype=a.dtype)
    t_inv = lax.linalg.triangular_solve(eye + a, jnp.broadcast_to(eye, a.shape),
                                        left_side=True, lower=True, unit_diagonal=True)
    u = t_inv @ (vc * bc[..., None])
    w = t_inv @ (k_beta * jnp.exp(G)[..., None])
    attn = jnp.einsum('bhnid,bhnjd->bhnij', qc, kc) * decay_incl
    q_dec = qc * jnp.exp(G)[..., None]
    g_last = G[..., -1:]
    k_dec = kc * jnp.exp(g_last - G)[..., None]
    chunk_decay = jnp.exp(g_last[..., 0])
    xs = tuple(jnp.moveaxis(t, 2, 0) for t in (q_dec, attn, u, w, k_dec, chunk_decay))

    def step(state, inp):
        q_e, at, u_c, w_c, k_d, dec = inp
        v_new = u_c - jnp.einsum('bhck,bhkv->bhcv', w_c, state)
        o = jnp.einsum('bhck,bhkv->bhcv', q_e, state) + jnp.einsum('bhij,bhjv->bhiv', at, v_new)
        state = state * dec[..., None, None] + jnp.einsum('bhck,bhcv->bhkv', k_d, v_new)
        return state, o

    state0 = jnp.zeros((b, h, dk, dv), dtype=q.dtype)
    _, o = lax.scan(step, state0, xs)
    return o.transpose(1, 0, 3, 2, 4).reshape(b, s, h, dv)


def setup_inputs(seed: int = 0) -> dict:
    key = jax.random.key(seed)
    ks = jax.random.split(key, 12)
    f32 = jnp.float32
    x = jax.random.normal(ks[0], (BATCH, SEQ, D_MODEL), f32)
    norm_w = 1.0 + 0.05 * jax.random.normal(ks[1], (DEPTH, D_MODEL), f32)
    w_in = jax.random.normal(ks[2], (DEPTH, D_MODEL, IN_WIDTH), f32) * D_MODEL ** -0.5
    conv_w = jax.random.normal(ks[3], (DEPTH, GDN_CONV, 2 * GDN_K_WIDTH + GDN_V_WIDTH), f32) * GDN_CONV ** -0.5
    a_log = jnp.log(jax.random.uniform(ks[4], (DEPTH, GDN_HEADS), f32, 1.0, 16.0))
    dt_bias = 0.5 * jax.random.normal(ks[5], (DEPTH, GDN_HEADS), f32)
    gdn_norm_w = 1.0 + 0.05 * jax.random.normal(ks[6], (DEPTH, GDN_VAL_DIM), f32)
    w_up_a = jax.random.normal(ks[7], (DEPTH, DSA_WIDTH, D_MODEL), f32) * DSA_WIDTH ** -0.5
    w_up_b = jax.random.normal(ks[8], (DEPTH, GDN_V_WIDTH, D_MODEL), f32) * GDN_V_WIDTH ** -0.5
    w_out = jax.random.normal(ks[9], (DEPTH, D_MODEL, D_MODEL), f32) * D_MODEL ** -0.5
    final_norm_w = 1.0 + 0.05 * jax.random.normal(ks[10], (D_MODEL,), f32)
    return {'x': x, 'norm_w': norm_w, 'w_in': w_in, 'conv_w': conv_w, 'a_log': a_log,
            'dt_bias': dt_bias, 'gdn_norm_w': gdn_norm_w, 'w_up_a': w_up_a, 'w_up_b': w_up_b,
            'w_out': w_out, 'final_norm_w': final_norm_w}


def reference(x, norm_w, w_in, conv_w, a_log, dt_bias, gdn_norm_w, w_up_a, w_up_b, w_out, final_norm_w):
    f32 = jnp.float32
    b, s, _ = x.shape
    cos, sin = _rope_tables(s, DSA_HEAD_DIM)
    for layer in range(DEPTH):
        h = _rmsnorm(x, norm_w[layer])
        proj = h @ w_in[layer]
        dsa_qkv, dsa_z, gdn_qkv, gdn_z, gdn_b, gdn_a, gate_a, gate_b = _split_columns(proj)

        qkv = dsa_qkv.astype(f32).reshape(b, s, DSA_GROUPS, 3, DSA_HEADS, DSA_HEAD_DIM)
        q_a = _apply_rope(qkv[:, :, :, 0], cos, sin)
        k_a = _apply_rope(qkv[:, :, :, 1], cos, sin)
        v_a = qkv[:, :, :, 2]
        o_a = _dilated_mixture(q_a, k_a, v_a).astype(x.dtype)
        y_a = (o_a * jax.nn.silu(dsa_z)) @ w_up_a[layer]

        cqkv = jax.nn.silu(_causal_depthwise_conv(gdn_qkv, conv_w[layer])).astype(f32)
        gq, gk, gv = jnp.split(cqkv, [GDN_K_WIDTH, 2 * GDN_K_WIDTH], axis=-1)
        gq = _l2norm(gq.reshape(b, s, GDN_HEADS, GDN_KEY_DIM)) * GDN_KEY_DIM ** -0.5
        gk = _l2norm(gk.reshape(b, s, GDN_HEADS, GDN_KEY_DIM))
        gv = gv.reshape(b, s, GDN_HEADS, GDN_VAL_DIM)
        beta = jax.nn.sigmoid(gdn_b.astype(f32))
        g = -jnp.exp(a_log[layer].astype(f32)) * jax.nn.softplus(gdn_a.astype(f32) + dt_bias[layer].astype(f32))
        o_b = _gated_delta_rule(gq, gk, gv, g, beta)
        o_b = _rmsnorm(o_b, gdn_norm_w[layer]).reshape(b, s, GDN_V_WIDTH).astype(x.dtype)
        y_b = (o_b * jax.nn.silu(gdn_z)) @ w_up_b[layer]

        merged = jax.nn.sigmoid(gate_a) * y_a + jax.nn.sigmoid(gate_b) * y_b
        x = x + merged @ w_out[layer]
    return _rmsnorm(x, final_norm_w)
```

```python
from contextlib import ExitStack
import numpy as np
import ml_dtypes
import concourse.bass as bass
import concourse.mybir as mybir
from concourse.bass_utils import run_bass_kernel_spmd

F32 = mybir.dt.float32
BF16 = mybir.dt.bfloat16
ALU = mybir.AluOpType
AF = mybir.ActivationFunctionType
AX = mybir.AxisListType

S = 8192
TT = 256
NTT = S // TT
ST = 2048
TPS = ST // TT
NCH = TT // 64
DIL = (1, 4, 16)
EPS = 1e-6
NB1 = 15
(ZA, GQ, GK, GV, ZB, BA) = (9, 10, 11, 12, 13, 14)


class Buf:
    __slots__ = ("n", "psum")

    def __init__(self, n="", psum=False):
        self.n = n
        self.psum = psum


class Op:
    __slots__ = ("eng", "fn", "deps", "sig", "cnt", "dkey", "dcnt", "idx", "alldeps", "dur", "st", "fin", "nun", "users")


class _FakeEng:
    def __init__(self):
        self.rec = None

    def __getattr__(self, name):
        def f(*a, **k):
            self.rec = (name, a, k)
            return self
        return f


def _free_elems(ap):
    try:
        sh = list(ap.shape)
        n = 1
        for v in sh[1:]:
            n *= int(v)
        return n
    except Exception:
        return 0


def _estimate(op):
    if op.fn is None:
        return 0.0
    if op.dkey is not None:
        return 0.1
    fe = _FakeEng()
    try:
        op.fn(fe)
    except Exception:
        return 0.5
    name, a, k = fe.rec if fe.rec else ("", (), {})
    args = list(a) + list(k.values())
    n = max([_free_elems(x) for x in args if hasattr(x, "shape")] + [1])
    if op.eng == "pe":
        if name == "matmul":
            rhs = a[2] if len(a) > 2 else k.get("rhs")
            n = _free_elems(rhs)
            f32 = str(getattr(rhs, "dtype", "")).endswith("float32")
            small = 0.0
            try:
                if int(a[1].shape[0]) < 128 or _free_elems(a[1]) < 128:
                    small = 0.08
            except Exception:
                pass
            return 0.05 + small + n * (0.0016 if f32 else 0.0004)
        return 0.11
    if op.eng == "act":
        return 0.20 + n * 0.0008
    if op.eng == "dve":
        return 0.20 + n * 0.0010
    if op.eng == "pool":
        return 0.35 + n * 0.0015
    return 0.1


class Sched:
    ENGS = ("pe", "act", "dve", "pool", "sp")

    def __init__(self):
        self.ops = {e: [] for e in self.ENGS}
        self.last_w = {}
        self.readers = {}
        self.dma_n = {}
        self.nops = 0
        self.last_key = {}

    def add(self, eng, fn, r=(), w=(), dkey=None):
        op = Op()
        op.eng, op.fn, op.sig, op.cnt, op.dkey, op.dcnt = eng, fn, False, 0, dkey, 0
        op.idx = self.nops
        self.nops += 1
        if dkey is not None:
            self.dma_n[dkey] = self.dma_n.get(dkey, 0) + 1
            op.dcnt = self.dma_n[dkey]
        w = tuple(w) + tuple(b for b in r if b.psum and b not in w)
        deps = {}
        for b in r:
            lw = self.last_w.get(b)
            if lw is not None:
                deps[id(lw)] = lw
        for b in w:
            lw = self.last_w.get(b)
            if lw is not None:
                deps[id(lw)] = lw
            for o in self.readers.get(b, {}).values():
                deps[id(o)] = o
        op.alldeps = list(deps.values())
        if dkey is not None:
            lk = self.last_key.get(dkey)
            if lk is not None:
                op.alldeps.append(lk)
            self.last_key[dkey] = op
        op.deps = [d for d in deps.values()
                   if not (d.eng == "pe" and eng == "pe" and d.dkey is None and dkey is None)]
        for d in op.deps:
            d.sig = True
        rk = eng if dkey is None else ("dma", dkey)
        for b in r:
            self.readers.setdefault(b, {})[rk] = op
        for b in w:
            self.last_w[b] = op
            self.readers[b] = {}
        self.ops[eng].append(op)
        return op

    def link(self, new, old):
        lw = self.last_w.get(old)
        if lw is not None:
            self.last_w[new] = lw
        self.readers[new] = dict(self.readers.get(old, {}))

    def barrier(self):
        lasts = [self.ops[e][-1] for e in self.ENGS if self.ops[e]]
        b = Buf("barrier")
        for o in lasts:
            self.last_w.pop(b, None)
        for e in self.ENGS:
            op = self.add(e, None, (), ())
            op.deps = [o for o in lasts]
            op.alldeps = [o for o in lasts]
            for o in lasts:
                o.sig = True

    def reorder(self, window=40, lat=0.12, dma_lat=2.5):
        allops = []
        for e in self.ENGS:
            allops.extend(self.ops[e])
        for op in allops:
            op.dur = _estimate(op)
            op.st = op.fin = None
            op.users = []
        for op in allops:
            op.nun = 0
        for e in self.ENGS:
            prev = None
            fence = None
            for op in self.ops[e]:
                if prev is not None and op.fn is None:
                    op.alldeps = list(op.alldeps) + [prev]
                if fence is not None:
                    op.alldeps = list(op.alldeps) + [fence]
                if op.fn is None:
                    fence = op
                prev = op
        for op in allops:
            seen = set()
            dd = []
            for d in op.alldeps:
                if id(d) not in seen:
                    seen.add(id(d))
                    dd.append(d)
            op.alldeps = dd
            op.nun = len(dd)
            for d in dd:
                d.users.append(op)
        pend = {e: list(self.ops[e]) for e in self.ENGS}
        head = {e: 0 for e in self.ENGS}
        tfree = {e: 0.0 for e in self.ENGS}
        order = {e: [] for e in self.ENGS}
        remaining = len(allops)
        while remaining:
            best = None
            for e in self.ENGS:
                lst = pend[e]
                i = head[e]
                cnt = 0
                while i < len(lst) and cnt < window:
                    op = lst[i]
                    i += 1
                    if op is None:
                        continue
                    cnt += 1
                    if op.nun:
                        continue
                    rt = tfree[e]
                    for d in op.alldeps:
                        v = d.fin + lat
                        if v > rt:
                            rt = v
                    key = (rt, op.idx)
                    if best is None or key < best[0]:
                        best = (key, e, i - 1, op)
                    if rt <= tfree[e]:
                        break
            (rt, _), e, pos, op = best
            op.st = rt
            busy = op.dur
            op.fin = rt + ((100.0 if op.dkey.startswith('cc') else (8.0 if op.dkey.startswith('go') else dma_lat)) if op.dkey is not None else busy)
            tfree[e] = rt + busy
            pend[e][pos] = None
            while head[e] < len(pend[e]) and pend[e][head[e]] is None:
                head[e] += 1
            order[e].append(op)
            for u in op.users:
                u.nun -= 1
            remaining -= 1
        for e in self.ENGS:
            self.ops[e] = order[e]
        self.est_total = max(tfree.values())

    def emit(self, nc, es):
        engobj = {"pe": nc.tensor, "act": nc.scalar, "dve": nc.vector, "pool": nc.gpsimd, "sp": nc.sync}
        sems = {e: es.enter_context(nc.semaphore("s_" + e)) for e in self.ENGS}
        dsems = {k: es.enter_context(nc.semaphore("d_%s" % (k,))) for k in self.dma_n}
        for e in self.ENGS:
            c = 0
            for op in self.ops[e]:
                if op.dkey is None and op.sig:
                    c += 1
                op.cnt = c
        block = es.enter_context(nc.Block())

        def run(e, eng):
            waited = {}
            for op in self.ops[e]:
                for d in op.deps:
                    if d.dkey is not None and d.dkey.startswith("cc"):
                        sem, val, key = dsems[d.dkey], 1, ("d", d.dkey)
                    elif d.dkey is not None:
                        sem, val, key = dsems[d.dkey], 16 * d.dcnt, ("d", d.dkey)
                    else:
                        sem, val, key = sems[d.eng], d.cnt, ("c", d.eng)
                    if waited.get(key, 0) >= val:
                        continue
                    waited[key] = val
                    eng.wait_ge(sem, val)
                if op.fn is None:
                    continue
                ins = op.fn(eng)
                if op.dkey is not None and op.dkey.startswith("cc"):
                    ins.then_inc(dsems[op.dkey])
                elif op.dkey is not None:
                    ins.then_inc(dsems[op.dkey], 16)
                elif op.sig:
                    ins.then_inc(sems[e], 1)

        @block.tensor
        def _(eng):
            run("pe", eng)

        @block.scalar
        def _(eng):
            run("act", eng)

        @block.vector
        def _(eng):
            run("dve", eng)

        @block.gpsimd
        def _(eng):
            run("pool", eng)

        @block.sync
        def _(eng):
            run("sp", eng)


class T:
    def __init__(self, nc, es, name, shape, dt, psum=False):
        self.t = es.enter_context((nc.psum_tensor if psum else nc.sbuf_tensor)("t_" + name, list(shape), dt))
        self.b = Buf(name, psum)

    def __getitem__(self, k):
        return self.t[k]


def build_program():
    nc = bass.Bass("TRN2", target_bir_lowering=False)
    es = ExitStack()
    sc = Sched()

    def dram_in(name, shape, dt=F32):
        return nc.dram_tensor(name, list(shape), dt, kind="ExternalInput").ap()

    xT = dram_in("xT", [128, 8, S])
    xTo = dram_in("xTo", [128, 8, ST])
    xo = dram_in("xo", [ST, 1024])
    w1 = dram_in("w1", [128, 8, NB1 * 128])
    wg = dram_in("wg", [128, 8, 2048])
    wua = dram_in("wua", [128, 4, 1024])
    wub = dram_in("wub", [128, 4, 1024])
    wo = dram_in("wo", [128, 8, 1024])
    normw_d = dram_in("normw", [128, 8])
    fnw_d = dram_in("fnw", [128, 1024])
    convw_d = dram_in("convw", [128, 12])
    alog_d = dram_in("alog", [128, 1])
    dtb_d = dram_in("dtb", [128, 1])
    gnw_d = dram_in("gnw", [128, 64])
    cos_d = dram_in("cosT", [128, S])
    sin_d = dram_in("sinT", [128, S])
    perm_d = dram_in("perm", [128, 128], BF16)
    ident_d = dram_in("ident", [128, 128], BF16)
    onesbd_d = dram_in("onesbd", [128, 128], BF16)
    onesbdf_d = dram_in("onesbdf", [128, 128])
    ones_d = dram_in("ones", [128, 128], BF16)
    lbd_d = dram_in("lbd", [128, 128])
    uaug_d = dram_in("uaug", [128, 130])
    slneg_d = dram_in("slneg", [128, 128])
    li_d = dram_in("li", [128, 128])
    dmask_d = dram_in("dmask", [128, 512], BF16)
    out_d = nc.dram_tensor("out", [ST, 1024], F32, kind="ExternalOutput").ap()
    bin0 = nc.dram_tensor("bin0", [256, ST], BF16)
    bout0 = nc.dram_tensor("bout0", [1024, ST], BF16)
    bin_ = [bin0] * 4
    bout = [bout0] * 4
    bb_, bo_ = Buf("bin"), Buf("bout")
    bin_b = [bb_] * 4
    bout_b = [bo_] * 4
    gown = [nc.dram_tensor("gown%d" % i, [1024, 512], BF16) for i in range(4)]
    gown_b = [Buf("gown%d" % i) for i in range(4)]
    cid_cache = {}

    def cidx(eng):
        if "v" not in cid_cache:
            cid_cache["v"] = eng.snap((eng.partition_id() % 4) * 512, min_val=0, max_val=1536)
        return cid_cache["v"]

    def copy_out(st):
        sc.add("sp", lambda e, st=st: e.dma_start(out=gown[st][:, :], in_=bout0[:, bass.ds(cidx(e), 512)]),
               (bo_,), (gown_b[st],), dkey="go%d" % st)

    def sb(name, shape, dt=F32):
        return T(nc, es, name, shape, dt)

    banks = [T(nc, es, "ps%d" % i, [128, 512], F32, psum=True) for i in range(8)]
    POOLS = {"ALL": [0, 1, 2, 3, 4, 5, 6, 7], "A": [0, 1, 2, 7], "B": [3, 4, 5, 6]}
    pool_i = {"ALL": 0, "A": 0, "B": 0}
    cur_pool = ["ALL"]

    def ps():
        k = cur_pool[0]
        lst = POOLS[k]
        b = banks[lst[pool_i[k] % len(lst)]]
        pool_i[k] += 1
        return b

    def step(gen, pool):
        cur_pool[0] = pool
        try:
            next(gen)
            return True
        except StopIteration:
            return False
        finally:
            cur_pool[0] = "ALL"

    def interleave(g1, g2):
        a1, a2 = g1 is not None, g2 is not None
        while a1 or a2:
            if a1:
                a1 = step(g1, "A")
            if a2:
                a2 = step(g2, "B")

    kid = [0]

    def load_const(dst, src, eng="sp"):
        kid[0] += 1
        sc.add(eng, lambda e, d=dst, s=src: e.dma_start(out=d[:], in_=s), (), (dst.b,), dkey="c%d" % kid[0])

    normw = sb("normw", [128, 8]); load_const(normw, normw_d)
    convw = sb("convw", [128, 12]); load_const(convw, convw_d)
    alog = sb("alog", [128, 1]); load_const(alog, alog_d)
    dtb = sb("dtb", [128, 1]); load_const(dtb, dtb_d)
    gnw = sb("gnw", [128, 64]); load_const(gnw, gnw_d)
    perm = sb("perm", [128, 128], BF16); load_const(perm, perm_d)
    ident = sb("ident", [128, 128], BF16); load_const(ident, ident_d)
    onesbd = sb("onesbd", [128, 128], BF16); load_const(onesbd, onesbd_d)
    onesbdf = sb("onesbdf", [128, 128]); load_const(onesbdf, onesbdf_d)
    ones = sb("ones", [128, 128], BF16); load_const(ones, ones_d)
    lbd = sb("lbd", [128, 128]); load_const(lbd, lbd_d)
    uaug = sb("uaug", [128, 130]); load_const(uaug, uaug_d)
    slneg = sb("slneg", [128, 128]); load_const(slneg, slneg_d)
    li = sb("li", [128, 128]); load_const(li, li_d)
    dmask = sb("dmask", [128, 512], BF16); load_const(dmask, dmask_d)
    nalog = sb("nalog", [128, 1])
    sc.add("act", lambda e: e.activation(nalog[:], alog[:], AF.Exp), (alog.b,), (nalog.b,))
    sc.add("dve", lambda e: e.tensor_scalar(nalog[:], nalog[:], -1.0, None, ALU.mult), (nalog.b,), (nalog.b,))

    XR = 12
    xs = [sb("xs%d" % i, [128, TT]) for i in range(XR)]
    xs_i = [0]

    def ring():
        b = xs[xs_i[0] % XR]
        k = "x%d" % (xs_i[0] % XR)
        xs_i[0] += 1
        return b, k

    cast_i = [0]

    def cast_to(dst_ap, dst_b, src_ap, src_b):
        cast_i[0] += 1
        if cast_i[0] % 2 == 0:
            sc.add("dve", lambda e: e.tensor_copy(dst_ap, src_ap), (src_b,), (dst_b,))
        else:
            sc.add("act", lambda e: e.copy(dst_ap, src_ap), (src_b,), (dst_b,))

    W1 = sb("W1", [128, 8, (NB1 - 1) * 128], BF16)
    W1b = [Buf("W1b%d" % i) for i in range(NB1 - 1)]
    for j in range((NB1 - 1) * 128 // TT):
        for c in range(8):
            stb, k = ring()
            sc.add("sp", lambda e, stb=stb, j=j, c=c: e.dma_start(out=stb[:, :], in_=w1[:, c, j * TT:(j + 1) * TT]),
                   (), (stb.b,), dkey=k)
            for bi in (2 * j, 2 * j + 1):
                cast_i[0] += (bi == 2 * j)
                dst_ap = W1[:, c, bi * 128:(bi + 1) * 128]
                src_ap = stb[:, (bi - 2 * j) * 128:(bi - 2 * j + 1) * 128]
                if cast_i[0] % 2 == 0:
                    sc.add("dve", lambda e, dst_ap=dst_ap, src_ap=src_ap: e.tensor_copy(dst_ap, src_ap), (stb.b,), (W1b[bi],))
                else:
                    sc.add("act", lambda e, dst_ap=dst_ap, src_ap=src_ap: e.copy(dst_ap, src_ap), (stb.b,), (W1b[bi],))
    Wba = sb("Wba", [128, 8, 4], BF16)
    stb, k = ring()
    stv = stb[:, 0:32].rearrange("p (a b) -> p a b", a=8)
    sc.add("sp", lambda e, stv=stv: e.dma_start(out=stv, in_=w1[:, :, BA * 128:BA * 128 + 4]), (), (stb.b,), dkey=k)
    cast_to(Wba[:], Wba.b, stv, stb.b)

    sqb = [sb("sq%d" % i, [128, TT], BF16) for i in range(3)]
    rstd = sb("rstd", [128, TT])
    hT2 = [sb("hT%d" % i, [128, 8, TT], BF16) for i in range(2)]
    cur_hT = [hT2[0]]
    cosb = [sb("cos%d" % i, [128, TT]) for i in range(2)]
    sinb = [sb("sin%d" % i, [128, TT]) for i in range(2)]
    Kt = [[sb("Kt%d_%d" % (g, s), [128, ST], BF16) for s in range(2)] for g in range(3)]
    Qt = [sb("Qt%d" % g, [128, ST], BF16) for g in range(3)]
    Vt = [sb("Vt%d" % g, [128, ST], BF16) for g in range(3)]
    Vs = [[sb("Vs%d_%d" % (g, s), [128, 16, 128], BF16) for s in range(2)] for g in range(3)]
    ndacc = sb("ndacc", [128, 2, ST])
    zaS = sb("zaS", [128, ST], BF16)
    gaT = sb("gaT", [128, ST], BF16)
    gbT = sb("gbT", [128, ST], BF16)
    qraw = [sb("qraw%d" % i, [128, TT], BF16) for i in range(2)]
    rt1 = [sb("rt1_%d" % i, [128, TT]) for i in range(2)]
    rt2 = [sb("rt2_%d" % i, [128, TT]) for i in range(2)]
    Pt = [sb("Pt%d" % i, [128, 512], BF16) for i in range(3)]
    Qx = [sb("Qx%d" % i, [128, 256], BF16) for i in range(2)]
    xpad = [sb("xpad%d" % i, [128, TT + 3], BF16) for i in range(3)]
    cacc = [sb("cacc%d" % i, [128, TT]) for i in range(2)]
    diagw = sb("diagw", [128, 12, 128], BF16)
    gsq = sb("gsq", [128, TT], BF16)
    grn = sb("grn", [128, TT])
    Qbd2 = [sb("Qbd%d" % i, [128, NCH, 128], BF16) for i in range(2)]
    Kbd2 = [sb("Kbd%d" % i, [128, NCH, 128], BF16) for i in range(2)]
    Vbd2 = [sb("Vbd%d" % i, [128, NCH, 128], BF16) for i in range(2)]
    zbS2 = [sb("zbS%d" % i, [128, TT], BF16) for i in range(2)]
    batok = sb("batok", [128, TT // 128, 4])
    bastk2 = [sb("bastk%d" % i, [128, NCH, 2]) for i in range(2)]
    beta2 = [sb("beta%d" % i, [128, NCH]) for i in range(2)]
    gg2 = [sb("gg%d" % i, [128, NCH]) for i in range(2)]
    Rbd = sb("Rbd", [128, NCH, 128])
    E1 = sb("E1", [128, NCH, 128])
    MM2 = sb("MM2", [128, NCH, 128])
    expG = sb("expG", [128, NCH])
    glast = sb("glast", [128, NCH])
    decf = sb("decf", [128, NCH])
    kdsc = sb("kdsc", [128, NCH])
    bsc = sb("bsc", [128, NCH])
    Cb = [sb("Cb%d" % i, [128, NCH, 128], BF16) for i in range(2)]
    Bb = [sb("Bb%d" % i, [128, NCH, 128], BF16) for i in range(2)]
    Xb = [sb("Xb%d" % i, [128, NCH, 128], BF16) for i in range(2)]
    ufin = sb("ufin", [128, NCH, 64])
    Wbd2 = sb("Wbd2", [128, NCH, 128], BF16)
    Wt = sb("Wt", [128, NCH, 128], BF16)
    attn = sb("attn", [128, NCH, 128], BF16)
    attnT = sb("attnT", [128, NCH, 128], BF16)
    Kdec = sb("Kdec", [128, NCH, 128], BF16)
    Sst = sb("Sst", [128, 64])
    Sbf = sb("Sbf", [128, 64], BF16)
    vnew = sb("vnew", [128, 64], BF16)
    oB = sb("oB", [128, 64])
    otile = sb("otile", [128, NCH, 64])
    oss = sb("oss", [128, NCH])
    onbd = sb("onbd", [128, NCH, 128], BF16)

    for k_ in range(12):
        sc.add("dve", lambda e, k_=k_: e.tensor_scalar(diagw[:, k_, :], ident[:], convw[:, k_:k_ + 1], None, ALU.mult),
               (ident.b, convw.b), (diagw.b,))
    for t_ in Qx:
        sc.add("pool", lambda e, t_=t_: e.memset(t_[:], 0.0), (), (t_.b,))
    for t_, eng in ((Sst, "dve"), (Sbf, "dve"), (Wbd2, "pool"), (onbd, "pool"), (Qbd2[0], "pool"), (Kbd2[0], "pool"),
                    (Vbd2[0], "pool"), (Qbd2[1], "pool"), (Kbd2[1], "pool"), (Vbd2[1], "pool"), (xpad[0], "dve"), (xpad[1], "dve"), (xpad[2], "dve")):
        sc.add(eng, lambda e, t_=t_: e.memset(t_[:], 0.0), (), (t_.b,))

    def rmsnorm_tile(src_dram, col0, hdst):
        bufs = []
        for c in range(8):
            xb = xs[xs_i[0] % XR]
            k = "x%d" % (xs_i[0] % XR)
            xs_i[0] += 1
            sc.add("sp", lambda e, xb=xb, c=c: e.dma_start(out=xb[:], in_=src_dram[:, c, col0:col0 + TT]),
                   (), (xb.b,), dkey=k)
            bufs.append(xb)
        pss = ps()
        for c in range(8):
            sq = sqb[c % 3]
            sc.add("act", lambda e, sq=sq, xb=bufs[c]: e.activation(sq[:], xb[:], AF.Square), (bufs[c].b,), (sq.b,))
            sc.add("pe", lambda e, sq=sq, c=c: e.matmul(pss[:, 0:TT], ones[:], sq[:], start=(c == 0), stop=(c == 7)),
                   (sq.b, ones.b), (pss.b,))
        sc.add("act", lambda e: e.activation(rstd[:], pss[:, 0:TT], AF.Ln, bias=EPS, scale=1.0 / 1024.0),
               (pss.b,), (rstd.b,))
        sc.add("act", lambda e: e.activation(rstd[:], rstd[:], AF.Exp, scale=-0.5), (rstd.b,), (rstd.b,))
        for c in range(8):
            sc.add("dve", lambda e, c=c, xb=bufs[c]: e.scalar_tensor_tensor(
                hdst[:, c, :], xb[:], normw[:, c:c + 1], rstd[:], ALU.mult, ALU.mult),
                (bufs[c].b, rstd.b, normw.b), (hdst.b,))

    def proj_fm(blk):
        p = ps()
        hT = cur_hT[0]
        for c in range(8):
            sc.add("pe", lambda e, c=c, p=p, hT=hT: e.matmul(p[:, 0:TT], W1[:, c, blk * 128:(blk + 1) * 128], hT[:, c, :],
                                                            start=(c == 0), stop=(c == 7)), (W1b[blk], hT.b), (p.b,))
        return p

    stmp = [sb("stmp%d" % i, [128, TT]) for i in range(2)]
    stmp_i = [0]

    def sigmoid_to(src_ap, src_b, shape_cols=TT):
        t = stmp[stmp_i[0] % 2]
        stmp_i[0] += 1
        tv = t[:, 0:shape_cols]
        sc.add("act", lambda e: e.activation(tv, src_ap, AF.Exp, scale=-1.0), (src_b,), (t.b,))
        sc.add("act", lambda e: e.activation(tv, tv, AF.Ln, bias=1.0), (t.b,), (t.b,))
        sc.add("act", lambda e: e.activation(tv, tv, AF.Exp, scale=-1.0), (t.b,), (t.b,))
        return t

    def perm_view(t_ap_tile, g, tt):
        d = DIL[g]
        n = TT // d
        v = t_ap_tile[:, :].rearrange("p (r m) -> p r m", r=d)
        return v[:, :, tt * n:(tt + 1) * n]

    def src_view(ap, g):
        d = DIL[g]
        return ap.rearrange("p (m r) -> p r m", r=d)

    rope_i = [0]

    def prenorm(ti):
        col0 = ti * TT
        cb, sb_ = cosb[ti % 2], sinb[ti % 2]
        sc.add("sp", lambda e: e.dma_start(out=cb[:], in_=cos_d[:, col0:col0 + TT]), (), (cb.b,), dkey="cos%d" % (ti % 2))
        sc.add("sp", lambda e: e.dma_start(out=sb_[:], in_=sin_d[:, col0:col0 + TT]), (), (sb_.b,), dkey="sin%d" % (ti % 2))
        rmsnorm_tile(xT, col0, hT2[ti % 2])

    prenorm(0)
    for st in range(4):
        slot = st % 2
        def tileA(st, slot, tt):
            ti = st * TPS + tt
            col0 = ti * TT
            par = ti % 2
            Qbd, Kbd, Vbd, zbS, bastk, beta, gg = Qbd2[par], Kbd2[par], Vbd2[par], zbS2[par], bastk2[par], beta2[par], gg2[par]
            cb, sb_ = cosb[ti % 2], sinb[ti % 2]
            hT = hT2[ti % 2]
            cur_hT[0] = hT

            def dsa_qk(g, qk):
                p = proj_fm(3 * g + qk)
                qr = qraw[rope_i[0] % 2]
                t1 = rt1[rope_i[0] % 2]
                t2 = rt2[rope_i[0] % 2]
                rope_i[0] += 1
                sc.add("act", lambda e, p=p, qr=qr: e.copy(qr[:], p[:, 0:TT]), (p.b,), (qr.b,))
                p2 = ps()
                sc.add("pe", lambda e, p2=p2, qr=qr: e.matmul(p2[:, 0:TT], perm[:], qr[:], start=True, stop=True),
                       (perm.b, qr.b), (p2.b,))
                sc.add("dve", lambda e, p=p, t1=t1, cb=cb: e.tensor_tensor(t1[:], p[:, 0:TT], cb[:], ALU.mult),
                       (p.b, cb.b), (t1.b,))
                sc.add("dve", lambda e, p2=p2, t2=t2, sb_=sb_: e.tensor_tensor(t2[:], p2[:, 0:TT], sb_[:], ALU.mult),
                       (p2.b, sb_.b), (t2.b,))
                dst = Qt[g] if qk == 0 else Kt[g][slot]
                sc.add("pool", lambda e, dst=dst, t1=t1, t2=t2, g=g, tt=tt: e.tensor_tensor(
                    perm_view(dst, g, tt), src_view(t1[:, :], g), src_view(t2[:, :], g), ALU.add),
                    (t1.b, t2.b), (dst.b,))

            def dsa_v(g):
                p = proj_fm(3 * g + 2)
                sc.add("act", lambda e, p=p, g=g, tt=tt: e.copy(perm_view(Vt[g], g, tt), src_view(p[:, 0:TT], g)),
                       (p.b,), (Vt[g].b,))

            def z_a():
                p = proj_fm(ZA)
                sg = sigmoid_to(p[:, 0:TT], p.b)
                sc.add("dve", lambda e, p=p, tt=tt, sg=sg: e.tensor_tensor(zaS[:, tt * TT:(tt + 1) * TT], p[:, 0:TT], sg[:], ALU.mult),
                       (p.b, sg.b), (zaS.b,))

            def z_b():
                p = proj_fm(ZB)
                sg = sigmoid_to(p[:, 0:TT], p.b)
                sc.add("dve", lambda e, p=p, sg=sg: e.tensor_tensor(zbS[:], p[:, 0:TT], sg[:], ALU.mult), (p.b, sg.b), (zbS.b,))

            def gdn_in(j):
                blk = (GQ, GK, GV)[j]
                p = proj_fm(blk)
                xp = xpad[j]
                ca = cacc[j % 2]
                sc.add("act", lambda e, xp=xp: e.copy(xp[:, 0:3], xp[:, TT:TT + 3]), (xp.b,), (xp.b,))
                sc.add("act", lambda e, xp=xp, p=p: e.copy(xp[:, 3:TT + 3], p[:, 0:TT]), (p.b, xp.b), (xp.b,))
                pc = ps()
                for tap in range(4):
                    sc.add("pe", lambda e, xp=xp, pc=pc, j=j, tap=tap: e.matmul(
                        pc[:, 0:TT], diagw[:, j * 4 + tap, :], xp[:, tap:tap + TT], start=(tap == 0), stop=(tap == 3)),
                        (diagw.b, xp.b), (pc.b,))
                if j < 2:
                    sg = sigmoid_to(pc[:, 0:TT], pc.b)
                    sc.add("dve", lambda e, ca=ca, pc=pc, sg=sg: e.tensor_tensor(ca[:], pc[:, 0:TT], sg[:], ALU.mult), (pc.b, sg.b), (ca.b,))
                    sc.add("act", lambda e, ca=ca: e.activation(gsq[:], ca[:], AF.Square), (ca.b,), (gsq.b,))
                    p3 = ps()
                    sc.add("pe", lambda e, p3=p3: e.matmul(p3[:, 0:TT], onesbd[:], gsq[:], start=True, stop=True),
                           (onesbd.b, gsq.b), (p3.b,))
                    scl = 64.0 if j == 0 else 1.0
                    sc.add("act", lambda e, p3=p3, scl=scl: e.activation(grn[:], p3[:, 0:TT], AF.Ln, bias=EPS * scl, scale=scl),
                           (p3.b,), (grn.b,))
                    sc.add("act", lambda e: e.activation(grn[:], grn[:], AF.Exp, scale=-0.5), (grn.b,), (grn.b,))
                    dstb = Qbd if j == 0 else Kbd
                    for h in range(2):
                        hs = slice(h * 64, (h + 1) * 64)
                        sc.add("dve", lambda e, dstb=dstb, hs=hs, h=h, ca=ca: e.tensor_tensor(
                            dstb[hs, :, h * 64:(h + 1) * 64], ca[hs, :].rearrange("p (c i) -> p c i", i=64),
                            grn[hs, :].rearrange("p (c i) -> p c i", i=64), ALU.mult), (ca.b, grn.b), (dstb.b,))
                else:
                    sg = sigmoid_to(pc[:, 0:TT], pc.b)
                    for h in range(2):
                        hs = slice(h * 64, (h + 1) * 64)
                        sc.add("dve", lambda e, hs=hs, h=h, pc=pc, sg=sg: e.tensor_tensor(
                            Vbd[hs, :, h * 64:(h + 1) * 64], pc[hs, 0:TT].rearrange("p (c i) -> p c i", i=64),
                            sg[hs, :].rearrange("p (c i) -> p c i", i=64), ALU.mult), (pc.b, sg.b), (Vbd.b,))

            def beta_part():
                pb = ps()
                for tb in range(TT // 128):
                    for c in range(8):
                        sc.add("pe", lambda e, tb=tb, c=c, pb=pb: e.matmul(
                            pb[:, tb * 4:(tb + 1) * 4], hT[:, c, tb * 128:(tb + 1) * 128], Wba[:, c, :],
                            start=(c == 0), stop=(c == 7)), (hT.b, Wba.b), (pb.b,))
                sc.add("dve", lambda e, pb=pb: e.tensor_copy(batok[:], pb[:, 0:(TT // 128) * 4].rearrange("p (b k) -> p b k", k=4)),
                       (pb.b,), (batok.b,))
                for h in range(2):
                    for cp in range(2):
                        sc.add("sp", lambda e, h=h, cp=cp: e.dma_start(
                            out=bastk[h * 64:(h + 1) * 64, cp:NCH:2, :], in_=batok[cp * 64:(cp + 1) * 64, :, 2 * h:2 * h + 2],
                            allow_slow_non_contiguous=True), (batok.b,), (bastk.b,), dkey="ba")
                sc.add("act", lambda e: e.activation(beta[:], bastk[:, :, 0], AF.Exp, scale=-1.0), (bastk.b,), (beta.b,))
                sc.add("act", lambda e: e.activation(beta[:], beta[:], AF.Ln, bias=1.0), (beta.b,), (beta.b,))
                sc.add("act", lambda e: e.activation(beta[:], beta[:], AF.Exp, scale=-1.0), (beta.b,), (beta.b,))
                sc.add("act", lambda e: e.activation(gg[:], bastk[:, :, 1], AF.Exp, bias=dtb[:]), (bastk.b, dtb.b), (gg.b,))
                sc.add("act", lambda e: e.activation(gg[:], gg[:], AF.Ln, bias=1.0), (gg.b,), (gg.b,))
                sc.add("dve", lambda e: e.tensor_scalar(gg[:], gg[:], nalog[:, 0:1], None, ALU.mult), (gg.b, nalog.b), (gg.b,))

            dsa_qk(0, 0); z_a(); yield
            dsa_qk(0, 1); z_b(); yield
            dsa_v(0); gdn_in(0); yield
            if ti + 1 < NTT:
                prenorm(ti + 1)
            yield
            dsa_qk(1, 0); yield
            dsa_qk(1, 1); gdn_in(1); yield
            dsa_v(1); yield
            dsa_qk(2, 0); gdn_in(2); yield
            dsa_qk(2, 1); beta_part(); yield
            dsa_v(2); yield

        def tileB(st, slot, tt):
            ti = st * TPS + tt
            par = ti % 2
            Qbd, Kbd, Vbd, zbS, bastk, beta, gg = Qbd2[par], Kbd2[par], Vbd2[par], zbS2[par], bastk2[par], beta2[par], gg2[par]
            pgl = ps()
            sc.add("pe", lambda e, pgl=pgl: e.matmul(pgl[:, 0:NCH], onesbdf[:], gg[:], start=True, stop=True),
                   (onesbdf.b, gg.b), (pgl.b,))
            sc.add("pe", lambda e, pgl=pgl: e.matmul(pgl[:, NCH:2 * NCH], lbd[:], gg[:], start=True, stop=True),
                   (lbd.b, gg.b), (pgl.b,))
            sc.add("act", lambda e, pgl=pgl: e.copy(glast[:], pgl[:, 0:NCH]), (pgl.b,), (glast.b,))
            sc.add("act", lambda e, pgl=pgl: e.activation(decf[:], pgl[:, 0:NCH], AF.Exp), (pgl.b,), (decf.b,))
            sc.add("act", lambda e, pgl=pgl: e.activation(expG[:], pgl[:, NCH:2 * NCH], AF.Exp), (pgl.b,), (expG.b,))
            sc.add("dve", lambda e, pgl=pgl: e.tensor_tensor(kdsc[:], glast[:], pgl[:, NCH:2 * NCH], ALU.subtract),
                   (glast.b, pgl.b), (kdsc.b,))
            sc.add("act", lambda e: e.activation(kdsc[:], kdsc[:], AF.Exp), (kdsc.b,), (kdsc.b,))
            sc.add("dve", lambda e: e.tensor_tensor(bsc[:], beta[:], expG[:], ALU.mult), (beta.b, expG.b), (bsc.b,))
            sc.add("dve", lambda e: e.tensor_tensor(
                Rbd[:], uaug[:, 0:128].unsqueeze(1).broadcast_to([128, NCH, 128]),
                gg[:, :].unsqueeze(2).broadcast_to([128, NCH, 128]), ALU.mult), (uaug.b, gg.b), (Rbd.b,))
            pD = ps()
            for c in range(NCH):
                sc.add("pe", lambda e, c=c, pD=pD: e.matmul(pD[:, c * 128:(c + 1) * 128], lbd[:], Rbd[:, c, :], start=True, stop=True),
                       (lbd.b, Rbd.b), (pD.b,))
            sc.add("act", lambda e, pD=pD: e.activation(E1[:].rearrange("p c n -> p (c n)"), pD[:, 0:NCH * 128], AF.Exp), (pD.b,), (E1.b,))
            yield
            pKK = ps()
            pQK = ps()
            for c in range(NCH):
                sc.add("pe", lambda e, c=c, pKK=pKK: e.matmul(pKK[:, c * 128:(c + 1) * 128], Kbd[:, c, :], Kbd[:, c, :], start=True, stop=True),
                       (Kbd.b,), (pKK.b,))
            for c in range(NCH):
                sc.add("pe", lambda e, c=c, pQK=pQK: e.matmul(pQK[:, c * 128:(c + 1) * 128], Qbd[:, c, :], Kbd[:, c, :], start=True, stop=True),
                       (Qbd.b, Kbd.b), (pQK.b,))
            sc.add("dve", lambda e: e.tensor_tensor(MM2[:], E1[:], beta[:, :].unsqueeze(2).broadcast_to([128, NCH, 128]), ALU.mult),
                   (E1.b, beta.b), (MM2.b,))
            sc.add("pool", lambda e: e.tensor_tensor(MM2[:], MM2[:], slneg[:, :].unsqueeze(1).broadcast_to([128, NCH, 128]), ALU.mult),
                   (MM2.b, slneg.b), (MM2.b,))
            sc.add("dve", lambda e, pKK=pKK: e.tensor_tensor(Cb[0][:].rearrange("p c n -> p (c n)"), pKK[:, 0:NCH * 128],
                                                           MM2[:].rearrange("p c n -> p (c n)"), ALU.mult), (pKK.b, MM2.b), (Cb[0].b,))
            sc.add("pool", lambda e: e.tensor_tensor(E1[:], E1[:], li[:, :].unsqueeze(1).broadcast_to([128, NCH, 128]), ALU.mult),
                   (E1.b, li.b), (E1.b,))
            sc.add("dve", lambda e, pQK=pQK: e.tensor_tensor(attn[:].rearrange("p c n -> p (c n)"), pQK[:, 0:NCH * 128],
                                                           E1[:].rearrange("p c n -> p (c n)"), ALU.mult), (pQK.b, E1.b), (attn.b,))
            yield
            pT1 = ps(); pT2 = ps(); pT3 = ps(); pT4 = ps()
            for c in range(NCH):
                for (pt, src_) in ((pT1, Cb[0]), (pT2, attn), (pT3, Kbd), (pT4, Vbd)):
                    sc.add("pe", lambda e, c=c, pt=pt, src_=src_: e.transpose(
                        pt[:].bitcast(BF16)[:, c * 128:(c + 1) * 128], src_[:, c, :], ident[:]), (src_.b, ident.b), (pt.b,))
            sc.add("act", lambda e: e.copy(Bb[0][:].rearrange("p c n -> p (c n)"), pT1[:].bitcast(BF16)[:, 0:NCH * 128]), (pT1.b,), (Bb[0].b,))
            sc.add("act", lambda e: e.copy(attnT[:].rearrange("p c n -> p (c n)"), pT2[:].bitcast(BF16)[:, 0:NCH * 128]), (pT2.b,), (attnT.b,))
            pT3v = pT3[:].bitcast(BF16)[:, 0:NCH * 128].rearrange("p (c n) -> p c n", n=128)
            pT4v = pT4[:].bitcast(BF16)[:, 0:NCH * 128].rearrange("p (c n) -> p c n", n=128)
            sc.add("dve", lambda e, pT3v=pT3v: e.tensor_tensor(Kdec[:], pT3v, kdsc[:, :].unsqueeze(2).broadcast_to([128, NCH, 128]), ALU.mult),
                   (pT3.b, kdsc.b), (Kdec.b,))
            for h in range(2):
                hs = slice(h * 64, (h + 1) * 64)
                sc.add("dve", lambda e, h=h, hs=hs, pT3v=pT3v: e.tensor_tensor(
                    Xb[0][hs, :, 64:128], pT3v[hs, :, h * 64:(h + 1) * 64], bsc[hs, :].unsqueeze(2).broadcast_to([64, NCH, 64]), ALU.mult),
                    (pT3.b, bsc.b), (Xb[0].b,))
                sc.add("dve", lambda e, h=h, hs=hs, pT4v=pT4v: e.tensor_tensor(
                    Xb[0][hs, :, 0:64], pT4v[hs, :, h * 64:(h + 1) * 64], beta[hs, :].unsqueeze(2).broadcast_to([64, NCH, 64]), ALU.mult),
                    (pT4.b, beta.b), (Xb[0].b,))
            yield
            cur = 0
            for lvl in range(6):
                yield
                Bc, Cc, Xc = Bb[cur], Cb[cur], Xb[cur]
                Bn, Cn, Xn = Bb[1 - cur], Cb[1 - cur], Xb[1 - cur]
                pX = ps()
                for c in range(NCH):
                    sc.add("pe", lambda e, c=c, pX=pX, Bc=Bc, Xc=Xc: e.matmul(pX[:, c * 128:(c + 1) * 128], Bc[:, c, :], Xc[:, c, :], start=True, stop=True),
                           (Bc.b, Xc.b), (pX.b,))
                if lvl < 5:
                    sc.add("dve", lambda e, pX=pX, Xc=Xc, Xn=Xn: e.tensor_tensor(
                        Xn[:].rearrange("p c n -> p (c n)"), pX[:, 0:NCH * 128], Xc[:].rearrange("p c n -> p (c n)"), ALU.add),
                        (pX.b, Xc.b), (Xn.b,))
                    pB = ps()
                    for c in range(NCH):
                        sc.add("pe", lambda e, c=c, pB=pB, Bc=Bc, Cc=Cc: e.matmul(pB[:, c * 128:(c + 1) * 128], Cc[:, c, :], Bc[:, c, :], start=True, stop=True),
                               (Bc.b, Cc.b), (pB.b,))
                    sc.add("act", lambda e, pB=pB, Bn=Bn: e.copy(Bn[:].rearrange("p c n -> p (c n)"), pB[:, 0:NCH * 128]), (pB.b,), (Bn.b,))
                    if lvl < 4:
                        pC = ps()
                        for c in range(NCH):
                            sc.add("pe", lambda e, c=c, pC=pC, Bc=Bc, Cc=Cc: e.matmul(pC[:, c * 128:(c + 1) * 128], Bc[:, c, :], Cc[:, c, :], start=True, stop=True),
                                   (Bc.b, Cc.b), (pC.b,))
                        sc.add("act", lambda e, pC=pC, Cn=Cn: e.copy(Cn[:].rearrange("p c n -> p (c n)"), pC[:, 0:NCH * 128]), (pC.b,), (Cn.b,))
                    cur = 1 - cur
                else:
                    pXv = pX[:, 0:NCH * 128].rearrange("p (c n) -> p c n", n=128)
                    sc.add("dve", lambda e, pXv=pXv, Xc=Xc: e.tensor_tensor(ufin[:], pXv[:, :, 0:64], Xc[:, :, 0:64], ALU.add),
                           (pX.b, Xc.b), (ufin.b,))
                    for h in range(2):
                        hs = slice(h * 64, (h + 1) * 64)
                        sc.add("dve", lambda e, pXv=pXv, Xc=Xc, hs=hs, h=h: e.tensor_tensor(
                            Wbd2[hs, :, h * 64:(h + 1) * 64], pXv[hs, :, 64:128], Xc[hs, :, 64:128], ALU.add),
                            (pX.b, Xc.b), (Wbd2.b,))
            pT5 = ps()
            for c in range(NCH):
                sc.add("pe", lambda e, c=c, pT5=pT5: e.transpose(pT5[:].bitcast(BF16)[:, c * 128:(c + 1) * 128], Wbd2[:, c, :], ident[:]),
                       (Wbd2.b, ident.b), (pT5.b,))
            sc.add("act", lambda e, pT5=pT5: e.copy(Wt[:].rearrange("p c n -> p (c n)"), pT5[:].bitcast(BF16)[:, 0:NCH * 128]), (pT5.b,), (Wt.b,))
            yield
            for c in range(NCH):
                yield
                pw = ps()
                sc.add("pe", lambda e, c=c, pw=pw: e.matmul(pw[:, 0:64], Wt[:, c, :], Sbf[:], start=True, stop=True), (Wt.b, Sbf.b), (pw.b,))
                sc.add("dve", lambda e, c=c, pw=pw: e.tensor_tensor(vnew[:], ufin[:, c, :], pw[:, 0:64], ALU.subtract), (ufin.b, pw.b), (vnew.b,))
                po = ps()
                sc.add("pe", lambda e, c=c, po=po: e.matmul(po[:, 0:64], Qbd[:, c, :], Sbf[:], start=True, stop=True), (Qbd.b, Sbf.b), (po.b,))
                sc.add("pe", lambda e, c=c, po=po: e.matmul(po[:, 64:128], attnT[:, c, :], vnew[:], start=True, stop=True), (attnT.b, vnew.b), (po.b,))
                sc.add("pe", lambda e, c=c, po=po: e.matmul(po[:, 128:192], Kdec[:, c, :], vnew[:], start=True, stop=True), (Kdec.b, vnew.b), (po.b,))
                sc.add("dve", lambda e, c=c, po=po: e.scalar_tensor_tensor(Sst[:], Sst[:], decf[:, c:c + 1], po[:, 128:192], ALU.mult, ALU.add),
                       (Sst.b, decf.b, po.b), (Sst.b,))
                sc.add("act", lambda e: e.copy(Sbf[:], Sst[:]), (Sst.b,), (Sbf.b,))
                sc.add("act", lambda e, po=po: e.copy(oB[:], po[:, 64:128]), (po.b,), (oB.b,))
                sc.add("dve", lambda e, c=c, po=po: e.scalar_tensor_tensor(otile[:, c, :], po[:, 0:64], expG[:, c:c + 1], oB[:], ALU.mult, ALU.add),
                       (po.b, expG.b, oB.b), (otile.b,))
            yield
            sc.add("pool", lambda e: e.tensor_tensor(Rbd[:, :, 0:64], otile[:], otile[:], ALU.mult), (otile.b,), (Rbd.b,))
            sc.add("dve", lambda e: e.tensor_reduce(oss[:], Rbd[:, :, 0:64], AX.X, ALU.add), (Rbd.b,), (oss.b,))
            sc.add("act", lambda e: e.activation(oss[:], oss[:], AF.Ln, bias=EPS, scale=1.0 / 64.0), (oss.b,), (oss.b,))
            sc.add("act", lambda e: e.activation(oss[:], oss[:], AF.Exp, scale=-0.5), (oss.b,), (oss.b,))
            sc.add("dve", lambda e: e.tensor_tensor(otile[:], otile[:], oss[:, :].unsqueeze(2).broadcast_to([128, NCH, 64]), ALU.mult),
                   (otile.b, oss.b), (otile.b,))
            sc.add("pool", lambda e: e.tensor_tensor(otile[:], otile[:], gnw[:, :].unsqueeze(1).broadcast_to([128, NCH, 64]), ALU.mult),
                   (otile.b, gnw.b), (otile.b,))
            for h in range(2):
                hs = slice(h * 64, (h + 1) * 64)
                sc.add("act", lambda e, hs=hs, h=h: e.copy(onbd[hs, :, h * 64:(h + 1) * 64], otile[hs, :, :]), (otile.b,), (onbd.b,))
            pT6 = ps()
            for c in range(NCH):
                sc.add("pe", lambda e, c=c, pT6=pT6: e.transpose(pT6[:].bitcast(BF16)[:, c * 128:(c + 1) * 128], onbd[:, c, :], ident[:]),
                       (onbd.b, ident.b), (pT6.b,))
            for h in range(2):
                hs = slice(h * 64, (h + 1) * 64)
                sc.add("dve", lambda e, hs=hs, h=h, tt=tt, pT6=pT6: e.tensor_tensor(
                    gbT[hs, tt * TT:(tt + 1) * TT].rearrange("p (c i) -> p c i", i=64),
                    pT6[:].bitcast(BF16)[hs, 0:NCH * 128].rearrange("p (c n) -> p c n", n=128)[:, :, h * 64:(h + 1) * 64],
                    zbS[hs, :].rearrange("p (c i) -> p c i", i=64), ALU.mult), (pT6.b, zbS.b), (gbT.b,))
        prevB = None
        for tt in range(TPS):
            interleave(tileA(st, slot, tt), prevB)
            prevB = tileB(st, slot, tt)

        def w2_stage_pieces(stage):
            def flat(t_):
                return t_[:].rearrange("p b n -> p (b n)") if len(t_[:].shape) == 3 else t_[:]
            def gate(t_, c):
                return [(t_, flat(t_)[:, j * TT:(j + 1) * TT], wg[:, c, j * TT:(j + 1) * TT]) for j in range(2048 // TT)]
            def two(t_, src, i):
                return [(t_, flat(t_)[:, (c % 2) * 1024 + j * TT:(c % 2) * 1024 + (j + 1) * TT], src[:, c, j * TT:(j + 1) * TT])
                        for c in (2 * i, 2 * i + 1) for j in range(1024 // TT)]
            if stage == 0:
                return two(Vt[0], wua, 1) + two(Vt[1], wub, 0) + two(Vt[2], wub, 1)
            if stage == 1:
                return gate(Kt[0][0], 0) + gate(Kt[0][1], 1) + gate(Qt[0], 6) + two(Vs[0][0], wo, 0) + two(Vs[0][1], wo, 1)
            if stage == 2:
                return gate(Kt[1][0], 2) + gate(Kt[1][1], 3) + gate(Qt[1], 7) + two(Vs[1][0], wo, 2) + two(Vs[1][1], wo, 3)
            return gate(Kt[2][0], 4) + gate(Kt[2][1], 5) + two(Qt[2], wua, 0)

        def emit_pieces(q, n):
            for _ in range(min(n, len(q))):
                t_, dst_ap, src_ap = q.pop(0)
                stb, k = ring()
                sc.add("sp", lambda e, stb=stb, src_ap=src_ap: e.dma_start(out=stb[:, :], in_=src_ap), (), (stb.b,), dkey=k)
                cast_to(dst_ap, t_.b, stb[:, :], stb.b)

        def attention(st, slot):
            wq = []
            for g in range(3):
                for q4 in range(4):
                    pv = ps()
                    for j in range(4):
                        blk = q4 * 4 + j
                        sc.add("pe", lambda e, pv=pv, j=j, blk=blk, g=g: e.transpose(
                            pv[:].bitcast(BF16)[:, j * 128:(j + 1) * 128], Vt[g][:, blk * 128:(blk + 1) * 128], ident[:]),
                            (Vt[g].b, ident.b), (pv.b,))
                    sc.add("act", lambda e, pv=pv, q4=q4, g=g, slot=slot: e.copy(
                        Vs[g][slot][:, q4 * 4:(q4 + 1) * 4, :].rearrange("p b n -> p (b n)"), pv[:].bitcast(BF16)[:, 0:512]),
                        (pv.b,), (Vs[g][slot].b,))
                    yield
            units = [(g, blk) for g in range(3) for blk in range(16)]
            if st == 3:
                wq.extend(w2_stage_pieces(0))

            def stage_s(u):
                g, blk = units[u]
                d = DIL[g]
                nps = 16 // d
                r, n = blk // nps, blk % nps
                halves = []
                if n > 0:
                    halves.append((0, slot, blk - 1))
                elif st > 0:
                    halves.append((0, 1 - slot, r * nps + nps - 1))
                halves.append((1, slot, blk))
                nh = len(halves)
                pS = ps()
                Pb = Pt[u % 3]
                qx = Qx[u % 2]
                sc.add("pool", lambda e, qx=qx, g=g, blk=blk: e.tensor_copy(qx[0:64, 0:128], Qt[g][0:64, blk * 128:(blk + 1) * 128]),
                       (Qt[g].b,), (qx.b,))
                sc.add("act", lambda e, qx=qx, g=g, blk=blk: e.copy(qx[64:128, 128:256], Qt[g][64:128, blk * 128:(blk + 1) * 128]),
                       (Qt[g].b,), (qx.b,))
                for (hf, sl, kb) in halves:
                    sc.add("pe", lambda e, hf=hf, sl=sl, kb=kb, g=g, pS=pS, qx=qx: e.matmul(
                        pS[:, hf * 256:(hf + 1) * 256], Kt[g][sl][:, kb * 128:(kb + 1) * 128], qx[:, :],
                        start=True, stop=True), (Kt[g][sl].b, qx.b), (pS.b,))
                lo = halves[0][0] * 256
                sc.add("act", lambda e, lo=lo, Pb=Pb, pS=pS: e.activation(
                    Pb[:, lo:512], pS[:, lo:512], AF.Exp, scale=0.125), (pS.b,), (Pb.b,))
                sc.add("dve", lambda e, Pb=Pb, lo=lo: e.tensor_tensor(Pb[:, lo:512], Pb[:, lo:512], dmask[:, lo:512], ALU.mult),
                       (Pb.b, dmask.b), (Pb.b,))
                return (g, d, r, n, halves, nh, Pb)

            def stage_pv(ctx):
                g, d, r, n, halves, nh, Pb = ctx
                pO = ps()
                for h in range(2):
                    hs = slice(h * 64, (h + 1) * 64)
                    for k, (hf, sl, kb) in enumerate(halves):
                        sc.add("pe", lambda e, h=h, hs=hs, hf=hf, sl=sl, kb=kb, k=k, g=g, Pb=Pb, pO=pO, nh=nh: e.matmul(
                            pO[hs, 0:128], Vs[g][sl][:, kb, h * 64:(h + 1) * 64], Pb[:, hf * 256 + h * 128:hf * 256 + (h + 1) * 128],
                            start=(k == 0), stop=(k == nh - 1)), (Vs[g][sl].b, Pb.b), (pO.b,))
                    for k, (hf, sl, kb) in enumerate(halves):
                        sc.add("pe", lambda e, h=h, hs=hs, hf=hf, k=k, Pb=Pb, pO=pO, nh=nh: e.matmul(
                            pO[hs, 128:256], ones[:, 0:64], Pb[:, hf * 256 + h * 128:hf * 256 + (h + 1) * 128],
                            start=(k == 0), stop=(k == nh - 1)), (ones.b, Pb.b), (pO.b,))
                off = n * 128 * d + r
                dstv = ndacc[:, :, off:off + 127 * d + 1:d] if d > 1 else ndacc[:, :, off:off + 128]
                srcv = pO[:, 0:256].rearrange("p (a q) -> p a q", a=2)
                if g == 0:
                    sc.add("act", lambda e, dstv=dstv, srcv=srcv: e.copy(dstv, srcv), (pO.b,), (ndacc.b,))
                else:
                    sc.add("dve", lambda e, dstv=dstv, srcv=srcv: e.tensor_tensor(dstv, srcv, dstv, ALU.add), (pO.b, ndacc.b), (ndacc.b,))

            ctx = stage_s(0)
            for u in range(len(units)):
                nxt = stage_s(u + 1) if u + 1 < len(units) else None
                stage_pv(ctx)
                ctx = nxt
                if st == 3:
                    if u == 15:
                        wq.extend(w2_stage_pieces(1))
                    if u == 31:
                        wq.extend(w2_stage_pieces(2))
                    emit_pieces(wq, 3)
                yield
            sc.add("act", lambda e: e.activation(ndacc[:, 1, :], ndacc[:, 1, :], AF.Ln), (ndacc.b,), (ndacc.b,))
            sc.add("act", lambda e: e.activation(ndacc[:, 1, :], ndacc[:, 1, :], AF.Exp, scale=-1.0), (ndacc.b,), (ndacc.b,))
            sc.add("dve", lambda e: e.tensor_tensor(ndacc[:, 0, :], ndacc[:, 0, :], ndacc[:, 1, :], ALU.mult), (ndacc.b,), (ndacc.b,))
            sc.add("pool", lambda e: e.tensor_tensor(gaT[:], ndacc[:, 0, :], zaS[:], ALU.mult), (ndacc.b, zaS.b), (gaT.b,))
            if st == 3:
                wq.extend(w2_stage_pieces(3))
                emit_pieces(wq, len(wq))
            yield

        interleave(attention(st, slot), prevB)
        if st > 0:
            copy_out(st - 1)
        sc.add("pool", lambda e, st=st: e.dma_start(out=bin_[st][0:128, :], in_=gaT[:]), (gaT.b,), (bin_b[st],), dkey="bi")
        sc.add("pool", lambda e, st=st: e.dma_start(out=bin_[st][128:256, :], in_=gbT[:]), (gbT.b,), (bin_b[st],), dkey="bi")
        sc.add("pool", lambda e, st=st: e.collective_compute(
            "AllGather", ALU.bypass, replica_groups=[[0, 1, 2, 3], [4, 5, 6, 7]], ins=[bin_[st][:, :]], outs=[bout[st][:, :]]),
            (bin_b[st],), (bout_b[st],), dkey="cc%d" % st)

    copy_out(3)
    es2 = es
    Wg = T.__new__(T); Wg.b = Buf("Wg")
    def alias(src, shape_str, dt=None, **kw):
        ap = src[:]
        if dt is not None:
            ap = ap.bitcast(dt)
        return ap

    class A:
        def __init__(self, ap, name, base=None, share=False):
            self.ap = ap
            if share:
                self.b = base.b
            else:
                self.b = Buf(name)
                if base is not None:
                    sc.link(self.b, base.b)

        def __getitem__(self, k):
            return self.ap[k]

    wg_parts = [Kt[0][0], Kt[0][1], Kt[1][0], Kt[1][1], Kt[2][0], Kt[2][1], Qt[0], Qt[1]]
    Wgc = [A(p_[:], "Wg%d" % i, p_, True) for i, p_ in enumerate(wg_parts)]
    Wua = [A(t_[:], "wua%d" % i, t_, True) for i, t_ in enumerate((Qt[2], Vt[0]))]
    Wub = [A(t_[:], "wub%d" % i, t_, True) for i, t_ in enumerate((Vt[1], Vt[2]))]
    wo_parts = [Vs[0][0], Vs[0][1], Vs[1][0], Vs[1][1]]
    Woc = [A(t_[:].rearrange("p b n -> p (b n)"), "wo%d" % i, t_, True) for i, t_ in enumerate(wo_parts)]
    fnw = A(ndacc[:, 0, 0:1024], "fnw", ndacc)
    xown = [A(ndacc[:, 1, 0:1024], "xown0", ndacc), A(ndacc[:, 1, 1024:2048], "xown1", ndacc)]
    ybuf = A(ndacc[:, 0, 1024:2048], "ybuf", ndacc)
    ga_g = A(Vs[2][0][:].rearrange("p b n -> p (b n)")[:, 0:4 * 512].rearrange("p (r t) -> p r t", r=4), "ga_g", Vs[2][0])
    gb_g = A(Vs[2][1][:].rearrange("p b n -> p (b n)")[:, 0:4 * 512].rearrange("p (r t) -> p r t", r=4), "gb_g", Vs[2][1])
    hTo = hT2[0]
    cur_hT[0] = hTo
    sgA = rt1
    sgB = rt2
    merged = A(gbT[:].rearrange("p (c t) -> p c t", c=8), "merged", gbT)
    ssq2 = A(oss[:, 0:1], "ssq2", oss)
    junk = A(gaT[:].bitcast(F32), "junk", gaT)

    def load_w2(dsts, src, nchunk, ncols, per):
        for c in range(nchunk):
            d_ = dsts[c // per]
            base = (c % per) * ncols
            for j in range(ncols // TT):
                stb, k = ring()
                sc.add("sp", lambda e, stb=stb, c=c, j=j: e.dma_start(out=stb[:, :], in_=src[:, c, j * TT:(j + 1) * TT]), (), (stb.b,), dkey=k)
                cast_to(d_[:, base + j * TT: base + (j + 1) * TT], d_.b, stb[:, :], stb.b)

    sc.add("sp", lambda e: e.dma_start(out=fnw[:, :], in_=fnw_d), (), (fnw.b,), dkey="fnw")

    for st in range(4):
        def p2(st, half):
            tcol = half * TT
            for r in range(4):
                sc.add("sp", lambda e, st=st, r=r, tcol=tcol: e.dma_start(
                    out=ga_g[:, r, 0:TT], in_=gown[st][r * 256:r * 256 + 128, tcol:tcol + TT]),
                    (gown_b[st],), (ga_g.b,), dkey="gag")
                sc.add("sp", lambda e, st=st, r=r, tcol=tcol: e.dma_start(
                    out=gb_g[:, r, 0:TT], in_=gown[st][r * 256 + 128:r * 256 + 256, tcol:tcol + TT]),
                    (gown_b[st],), (gb_g.b,), dkey="gbg")
            rmsnorm_tile(xTo, st * 512 + tcol, hTo)
            for mb in range(8):
                k2 = mb % 2
                pa = ps()
                for c in range(8):
                    sc.add("pe", lambda e, c=c, pa=pa, mb=mb: e.matmul(pa[:, 0:TT], Wgc[c][:, mb * 128:(mb + 1) * 128], hTo[:, c, :],
                                                                     start=(c == 0), stop=(c == 7)), (Wgc[c].b, hTo.b), (pa.b,))
                sc.add("act", lambda e, pa=pa, k2=k2: e.activation(sgA[k2][:], pa[:, 0:TT], AF.Sigmoid), (pa.b,), (sgA[k2].b,))
                pb_ = ps()
                for c in range(8):
                    sc.add("pe", lambda e, c=c, pb_=pb_, mb=mb: e.matmul(pb_[:, 0:TT], Wgc[c][:, 1024 + mb * 128:1024 + (mb + 1) * 128], hTo[:, c, :],
                                                                      start=(c == 0), stop=(c == 7)), (Wgc[c].b, hTo.b), (pb_.b,))
                sc.add("act", lambda e, pb_=pb_, k2=k2: e.activation(sgB[k2][:], pb_[:, 0:TT], AF.Sigmoid), (pb_.b,), (sgB[k2].b,))
                pya = ps()
                for r in range(4):
                    sc.add("pe", lambda e, r=r, pya=pya, mb=mb: e.matmul(
                        pya[:, 0:TT], Wua[r // 2][:, (r % 2) * 1024 + mb * 128:(r % 2) * 1024 + (mb + 1) * 128], ga_g[:, r, 0:TT],
                        start=(r == 0), stop=(r == 3)), (Wua[r // 2].b, ga_g.b), (pya.b,))
                pyb = ps()
                for r in range(4):
                    sc.add("pe", lambda e, r=r, pyb=pyb, mb=mb: e.matmul(
                        pyb[:, 0:TT], Wub[r // 2][:, (r % 2) * 1024 + mb * 128:(r % 2) * 1024 + (mb + 1) * 128], gb_g[:, r, 0:TT],
                        start=(r == 0), stop=(r == 3)), (Wub[r // 2].b, gb_g.b), (pyb.b,))
                sc.add("dve", lambda e, pya=pya, k2=k2: e.tensor_tensor(sgA[k2][:], pya[:, 0:TT], sgA[k2][:], ALU.mult), (pya.b, sgA[k2].b), (sgA[k2].b,))
                sc.add("dve", lambda e, pyb=pyb, k2=k2: e.tensor_tensor(sgB[k2][:], pyb[:, 0:TT], sgB[k2][:], ALU.mult), (pyb.b, sgB[k2].b), (sgB[k2].b,))
                sc.add("pool", lambda e, k2=k2, mb=mb: e.tensor_tensor(merged[:, mb, :], sgA[k2][:], sgB[k2][:], ALU.add), (sgA[k2].b, sgB[k2].b), (merged.b,))
            for tb in range(TT // 128):
                row0 = st * 512 + tcol + tb * 128
                xw = xown[tb % 2]
                sc.add("sp", lambda e, xw=xw, row0=row0: e.dma_start(out=xw[:, :], in_=xo[row0:row0 + 128, :]), (), (xw.b,), dkey="xo%d" % (tb % 2))
                po2 = [ps(), ps()]
                for nh_ in range(2):
                    for c in range(8):
                        sc.add("pe", lambda e, c=c, nh_=nh_, tb=tb, po2=po2: e.matmul(
                            po2[nh_][:, 0:512], merged[:, c, tb * 128:(tb + 1) * 128],
                            Woc[c // 2][:, (c % 2) * 1024 + nh_ * 512:(c % 2) * 1024 + (nh_ + 1) * 512],
                            start=(c == 0), stop=(c == 7)), (merged.b, Woc[c // 2].b), (po2[nh_].b,))
                for nh_ in range(2):
                    sc.add("dve", lambda e, nh_=nh_, xw=xw, po2=po2: e.tensor_tensor(
                        ybuf[:, nh_ * 512:(nh_ + 1) * 512], po2[nh_][:, 0:512], xw[:, nh_ * 512:(nh_ + 1) * 512], ALU.add),
                        (po2[nh_].b, xw.b), (ybuf.b,))
                sc.add("act", lambda e: e.activation(junk[:, :], ybuf[:, :], AF.Square, accum_out=ssq2[:]), (ybuf.b,), (junk.b, ssq2.b))
                sc.add("act", lambda e: e.activation(ssq2[:], ssq2[:], AF.Ln, bias=EPS, scale=1.0 / 1024.0), (ssq2.b,), (ssq2.b,))
                sc.add("act", lambda e: e.activation(ssq2[:], ssq2[:], AF.Exp, scale=-0.5), (ssq2.b,), (ssq2.b,))
                sc.add("dve", lambda e: e.scalar_tensor_tensor(ybuf[:, :], ybuf[:, :], ssq2[:, 0:1], fnw[:, :], ALU.mult, ALU.mult),
                       (ybuf.b, ssq2.b, fnw.b), (ybuf.b,))
                sc.add("sp", lambda e, row0=row0: e.dma_start(out=out_d[row0:row0 + 128, :], in_=ybuf[:, :]), (ybuf.b,), (Buf(),), dkey="out")
        for half in range(2):
            p2(st, half)
    fin = Buf("fin")
    last_out = [o for o in sc.ops["sp"] if o.dkey == "out"][-1]
    op = sc.add("sp", None, (), ())
    op.deps = [last_out]
    sc.reorder()
    build_program.stats = (sc.est_total, {e: len(sc.ops[e]) for e in sc.ENGS})
    sc.emit(nc, es)
    es.close()
    return nc


def _consts():
    bf = ml_dtypes.bfloat16
    idx = np.arange(128)
    h = idx // 64
    i = idx % 64
    same = (h[:, None] == h[None, :])
    c = {}
    sw = h * 64 + (i + 32) % 64
    perm = np.zeros((128, 128), np.float32)
    perm[sw, idx] = 1.0
    c["perm"] = perm.astype(bf)
    c["ident"] = np.eye(128, dtype=np.float32).astype(bf)
    c["onesbd"] = same.astype(np.float32).astype(bf)
    c["onesbdf"] = same.astype(np.float32)
    c["ones"] = np.ones((128, 128), np.float32).astype(bf)
    c["lbd"] = (same & (i[:, None] <= i[None, :])).astype(np.float32)
    ua = np.zeros((128, 130), np.float32)
    ua[:, :128] = (same & (i[:, None] > i[None, :])).astype(np.float32)
    ua[:, 128] = 1.0
    c["uaug"] = ua
    c["slneg"] = -(same & (i[:, None] > i[None, :])).astype(np.float32)
    c["li"] = (same & (i[:, None] >= i[None, :])).astype(np.float32)
    j_ = np.arange(128)[:, None]
    q_ = np.arange(128)[None, :]
    prev = (j_ >= q_).astype(np.float32)
    cur = (j_ <= q_).astype(np.float32)
    dm = np.concatenate([prev, prev, cur, cur], axis=1)
    c["dmask"] = dm.astype(bf)
    inv = (10000.0 ** (-np.arange(0, 64, 2, dtype=np.float32) / 64.0)).astype(np.float32)
    ang = (np.arange(S, dtype=np.float32)[:, None] * inv[None, :]).astype(np.float32)
    ang = np.concatenate([ang, ang], axis=-1)
    cos = np.cos(ang).astype(np.float32).T
    sin = np.sin(ang).astype(np.float32).T
    sgn = np.where(np.arange(64) < 32, -1.0, 1.0).astype(np.float32)[:, None]
    c["cosT"] = np.ascontiguousarray(np.concatenate([cos, cos], 0))
    c["sinT"] = np.ascontiguousarray(np.concatenate([sin * sgn, sin * sgn], 0))
    return c


def _chunk(w, nchunk):
    return np.ascontiguousarray(w.reshape(nchunk, 128, -1).transpose(1, 0, 2))


_NC = [None]


def kernel(x, norm_w, w_in, conv_w, a_log, dt_bias, gdn_norm_w, w_up_a, w_up_b, w_out, final_norm_w):
    x = np.asarray(x, np.float32)
    w_in0 = np.asarray(w_in, np.float32)[0]
    conv0 = np.asarray(conv_w, np.float32)[0]
    if _NC[0] is None:
        _NC[0] = build_program()
    nc = _NC[0]
    cst = _consts()
    shared = dict(cst)
    shared["wg"] = _chunk(w_in0[:, 7184:9232], 8)
    shared["wua"] = _chunk(np.asarray(w_up_a, np.float32)[0], 4)
    shared["wub"] = _chunk(np.asarray(w_up_b, np.float32)[0], 4)
    shared["wo"] = _chunk(np.asarray(w_out, np.float32)[0], 8)
    shared["normw"] = np.ascontiguousarray(np.asarray(norm_w, np.float32)[0].reshape(8, 128).T)
    shared["fnw"] = np.ascontiguousarray(np.broadcast_to(np.asarray(final_norm_w, np.float32)[None, :], (128, 1024)))
    shared["gnw"] = np.ascontiguousarray(np.broadcast_to(np.asarray(gdn_norm_w, np.float32)[0][None, :], (128, 64)))
    in_maps = []
    owns = []
    for core in range(8):
        b, c4 = core // 4, core % 4
        h0 = 2 * c4
        xTb = _chunk(np.ascontiguousarray(x[b].T), 8)
        own = np.concatenate([st * ST + c4 * 512 + np.arange(512) for st in range(4)])
        owns.append(own)
        cols = []
        for g in range(3):
            for t in range(3):
                s0 = g * 1536 + t * 512 + h0 * 64
                cols.append(np.arange(s0, s0 + 128))
        cols.append(np.arange(4608 + h0 * 64, 4608 + h0 * 64 + 128))
        for t in range(3):
            s0 = 5120 + t * 512 + h0 * 64
            cols.append(np.arange(s0, s0 + 128))
        cols.append(np.arange(6656 + h0 * 64, 6656 + h0 * 64 + 128))
        w1 = np.zeros((1024, NB1 * 128), np.float32)
        cc = np.concatenate(cols)
        w1[:, :14 * 128] = w_in0[:, cc]
        w1[:, 14 * 128 + 0] = w_in0[:, 7168 + h0]
        w1[:, 14 * 128 + 1] = w_in0[:, 7176 + h0]
        w1[:, 14 * 128 + 2] = w_in0[:, 7168 + h0 + 1]
        w1[:, 14 * 128 + 3] = w_in0[:, 7176 + h0 + 1]
        convw = np.zeros((128, 12), np.float32)
        for t in range(3):
            s0 = t * 512 + h0 * 64
            convw[:, t * 4:(t + 1) * 4] = conv0[:, s0:s0 + 128].T
        hh = np.arange(128) // 64
        m = dict(shared)
        m["xT"] = xTb
        m["xTo"] = np.ascontiguousarray(xTb[:, :, own])
        m["xo"] = np.ascontiguousarray(x[b][own])
        m["w1"] = _chunk(w1, 8)
        m["convw"] = convw
        m["alog"] = np.asarray(a_log, np.float32)[0][h0 + hh][:, None].copy()
        m["dtb"] = np.asarray(dt_bias, np.float32)[0][h0 + hh][:, None].copy()
        in_maps.append(m)
    res = run_bass_kernel_spmd(nc, in_maps, core_ids=list(range(8)))
    out = np.zeros((2, S, 1024), np.float32)
    for core in range(8):
        out[core // 4, owns[core]] = np.asarray(res.results[core]["out"], np.float32)
    return out
```

```python
from contextlib import ExitStack
import numpy as np
import ml_dtypes
import concourse.bass as bass
import concourse.mybir as mybir
from concourse.bass_utils import run_bass_kernel_spmd

F32 = mybir.dt.float32
BF16 = mybir.dt.bfloat16
ALU = mybir.AluOpType
AF = mybir.ActivationFunctionType
AX = mybir.AxisListType

S = 8192
TT = 256
NTT = S // TT
ST = 2048
TPS = ST // TT
NCH = TT // 64
DIL = (1, 4, 16)
EPS = 1e-6
NB1 = 15
(ZA, GQ, GK, GV, ZB, BA) = (9, 10, 11, 12, 13, 14)


class Buf:
    __slots__ = ("n", "psum")

    def __init__(self, n="", psum=False):
        self.n = n
        self.psum = psum


class Op:
    __slots__ = ("eng", "fn", "deps", "sig", "cnt", "dkey", "dcnt", "idx", "alldeps", "dur", "st", "fin", "nun", "users", "bl")


class _FakeEng:
    def __init__(self):
        self.rec = None

    def __getattr__(self, name):
        def f(*a, **k):
            self.rec = (name, a, k)
            return self
        return f


def _free_elems(ap):
    try:
        sh = list(ap.shape)
        n = 1
        for v in sh[1:]:
            n *= int(v)
        return n
    except Exception:
        return 0


def _estimate(op):
    if op.fn is None:
        return 0.0
    if op.dkey is not None:
        return 0.1
    fe = _FakeEng()
    try:
        op.fn(fe)
    except Exception:
        return 0.5
    name, a, k = fe.rec if fe.rec else ("", (), {})
    args = list(a) + list(k.values())
    n = max([_free_elems(x) for x in args if hasattr(x, "shape")] + [1])
    if op.eng == "pe":
        if name == "matmul":
            rhs = a[2] if len(a) > 2 else k.get("rhs")
            n = _free_elems(rhs)
            f32 = str(getattr(rhs, "dtype", "")).endswith("float32")
            small = 0.0
            try:
                if int(a[1].shape[0]) < 128 or _free_elems(a[1]) < 128:
                    small = 0.08
            except Exception:
                pass
            return 0.05 + small + n * (0.0016 if f32 else 0.0004)
        return 0.11
    if op.eng == "act":
        return 0.20 + n * 0.0008
    if op.eng == "dve":
        return 0.20 + n * 0.0010
    if op.eng == "pool":
        return 0.35 + n * 0.0015
    return 0.1


class Sched:
    ENGS = ("pe", "act", "dve", "pool", "sp")

    def __init__(self):
        self.ops = {e: [] for e in self.ENGS}
        self.last_w = {}
        self.readers = {}
        self.dma_n = {}
        self.nops = 0
        self.last_key = {}

    def add(self, eng, fn, r=(), w=(), dkey=None):
        op = Op()
        op.eng, op.fn, op.sig, op.cnt, op.dkey, op.dcnt = eng, fn, False, 0, dkey, 0
        op.idx = self.nops
        self.nops += 1
        if dkey is not None:
            self.dma_n[dkey] = self.dma_n.get(dkey, 0) + 1
            op.dcnt = self.dma_n[dkey]
        w = tuple(w) + tuple(b for b in r if b.psum and b not in w)
        deps = {}
        for b in r:
            lw = self.last_w.get(b)
            if lw is not None:
                deps[id(lw)] = lw
        for b in w:
            lw = self.last_w.get(b)
            if lw is not None:
                deps[id(lw)] = lw
            for o in self.readers.get(b, {}).values():
                deps[id(o)] = o
        op.alldeps = list(deps.values())
        if dkey is not None:
            lk = self.last_key.get(dkey)
            if lk is not None:
                op.alldeps.append(lk)
            self.last_key[dkey] = op
        op.deps = [d for d in deps.values()
                   if not (d.eng == "pe" and eng == "pe" and d.dkey is None and dkey is None)]
        for d in op.deps:
            d.sig = True
        rk = eng if dkey is None else ("dma", dkey)
        for b in r:
            self.readers.setdefault(b, {})[rk] = op
        for b in w:
            self.last_w[b] = op
            self.readers[b] = {}
        self.ops[eng].append(op)
        return op

    def link(self, new, old):
        lw = self.last_w.get(old)
        if lw is not None:
            self.last_w[new] = lw
        self.readers[new] = dict(self.readers.get(old, {}))

    def barrier(self):
        lasts = [self.ops[e][-1] for e in self.ENGS if self.ops[e]]
        b = Buf("barrier")
        for o in lasts:
            self.last_w.pop(b, None)
        for e in self.ENGS:
            op = self.add(e, None, (), ())
            op.deps = [o for o in lasts]
            op.alldeps = [o for o in lasts]
            for o in lasts:
                o.sig = True

    def reorder(self, window=40, lat=0.12, dma_lat=2.5, use_bl=True, bl_engs=("pe",)):
        allops = []
        for e in self.ENGS:
            allops.extend(self.ops[e])
        for op in allops:
            op.dur = _estimate(op)
            op.st = op.fin = None
            op.users = []
        for op in allops:
            op.nun = 0
        for e in self.ENGS:
            prev = None
            fence = None
            for op in self.ops[e]:
                if prev is not None and op.fn is None:
                    op.alldeps = list(op.alldeps) + [prev]
                if fence is not None:
                    op.alldeps = list(op.alldeps) + [fence]
                if op.fn is None:
                    fence = op
                prev = op
        for op in allops:
            seen = set()
            dd = []
            for d in op.alldeps:
                if id(d) not in seen:
                    seen.add(id(d))
                    dd.append(d)
            op.alldeps = dd
            op.nun = len(dd)
            for d in dd:
                d.users.append(op)
        eff = lambda o: ((100.0 if o.dkey.startswith('cc') else (8.0 if o.dkey.startswith('go') else dma_lat)) if o.dkey is not None else o.dur)
        for op in sorted(allops, key=lambda o: -o.idx):
            b = 0.0
            for u in op.users:
                if u.bl + lat > b:
                    b = u.bl + lat
            op.bl = b + eff(op)
        pend = {e: list(self.ops[e]) for e in self.ENGS}
        head = {e: 0 for e in self.ENGS}
        tfree = {e: 0.0 for e in self.ENGS}
        order = {e: [] for e in self.ENGS}
        remaining = len(allops)
        while remaining:
            best = None
            for e in self.ENGS:
                lst = pend[e]
                i = head[e]
                cnt = 0
                tf = tfree[e]
                cand = None
                while i < len(lst) and cnt < window:
                    op = lst[i]
                    i += 1
                    if op is None:
                        continue
                    cnt += 1
                    if op.nun:
                        continue
                    rt = tf
                    for d in op.alldeps:
                        v = d.fin + lat
                        if v > rt:
                            rt = v
                    if use_bl and e in bl_engs:
                        key = (rt if rt > tf + 0.02 else tf, -op.bl, op.idx)
                    else:
                        key = (rt, op.idx, 0)
                    if cand is None or key < cand[0]:
                        cand = (key, e, i - 1, op, rt)
                if cand is not None and (best is None or cand[0] < best[0]):
                    best = cand
            _, e, pos, op, rt = best
            rt = max(rt, tfree[e])
            op.st = rt
            busy = op.dur
            op.fin = rt + ((100.0 if op.dkey.startswith('cc') else (8.0 if op.dkey.startswith('go') else dma_lat)) if op.dkey is not None else busy)
            tfree[e] = rt + busy
            pend[e][pos] = None
            while head[e] < len(pend[e]) and pend[e][head[e]] is None:
                head[e] += 1
            order[e].append(op)
            for u in op.users:
                u.nun -= 1
            remaining -= 1
        for e in self.ENGS:
            self.ops[e] = order[e]
        self.est_total = max(tfree.values())

    def emit(self, nc, es):
        engobj = {"pe": nc.tensor, "act": nc.scalar, "dve": nc.vector, "pool": nc.gpsimd, "sp": nc.sync}
        sems = {e: es.enter_context(nc.semaphore("s_" + e)) for e in self.ENGS}
        dsems = {k: es.enter_context(nc.semaphore("d_%s" % (k,))) for k in self.dma_n}
        for e in self.ENGS:
            c = 0
            for op in self.ops[e]:
                if op.dkey is None and op.sig:
                    c += 1
                op.cnt = c
        block = es.enter_context(nc.Block())

        def run(e, eng):
            waited = {}
            for op in self.ops[e]:
                for d in op.deps:
                    if d.dkey is not None and d.dkey.startswith("cc"):
                        sem, val, key = dsems[d.dkey], 1, ("d", d.dkey)
                    elif d.dkey is not None:
                        sem, val, key = dsems[d.dkey], 16 * d.dcnt, ("d", d.dkey)
                    else:
                        sem, val, key = sems[d.eng], d.cnt, ("c", d.eng)
                    if waited.get(key, 0) >= val:
                        continue
                    waited[key] = val
                    eng.wait_ge(sem, val)
                if op.fn is None:
                    continue
                ins = op.fn(eng)
                if op.dkey is not None and op.dkey.startswith("cc"):
                    ins.then_inc(dsems[op.dkey])
                elif op.dkey is not None:
                    ins.then_inc(dsems[op.dkey], 16)
                elif op.sig:
                    ins.then_inc(sems[e], 1)

        @block.tensor
        def _(eng):
            run("pe", eng)

        @block.scalar
        def _(eng):
            run("act", eng)

        @block.vector
        def _(eng):
            run("dve", eng)

        @block.gpsimd
        def _(eng):
            run("pool", eng)

        @block.sync
        def _(eng):
            run("sp", eng)


class T:
    def __init__(self, nc, es, name, shape, dt, psum=False):
        self.t = es.enter_context((nc.psum_tensor if psum else nc.sbuf_tensor)("t_" + name, list(shape), dt))
        self.b = Buf(name, psum)

    def __getitem__(self, k):
        return self.t[k]


def build_program():
    nc = bass.Bass("TRN2", target_bir_lowering=False)
    es = ExitStack()
    sc = Sched()

    def dram_in(name, shape, dt=F32):
        return nc.dram_tensor(name, list(shape), dt, kind="ExternalInput").ap()

    xT = dram_in("xT", [128, 8, S])
    xTo = dram_in("xTo", [128, 8, ST])
    xo = dram_in("xo", [ST, 1024])
    w1 = dram_in("w1", [128, 8, NB1 * 128])
    wg = dram_in("wg", [128, 8, 2048])
    wua = dram_in("wua", [128, 4, 1024])
    wub = dram_in("wub", [128, 4, 1024])
    wo = dram_in("wo", [128, 8, 1024])
    normw_d = dram_in("normw", [128, 8])
    fnw_d = dram_in("fnw", [128, 1024])
    convw_d = dram_in("convw", [128, 12])
    alog_d = dram_in("alog", [128, 1])
    dtb_d = dram_in("dtb", [128, 1])
    gnw_d = dram_in("gnw", [128, 64])
    cos_d = dram_in("cosT", [128, S])
    sin_d = dram_in("sinT", [128, S])
    perm_d = dram_in("perm", [128, 128], BF16)
    ident_d = dram_in("ident", [128, 128], BF16)
    onesbd_d = dram_in("onesbd", [128, 128], BF16)
    onesbdf_d = dram_in("onesbdf", [128, 128])
    ones_d = dram_in("ones", [128, 128], BF16)
    lbd_d = dram_in("lbd", [128, 128])
    uaug_d = dram_in("uaug", [128, 130])
    slneg_d = dram_in("slneg", [128, 128])
    li_d = dram_in("li", [128, 128])
    dmask_d = dram_in("dmask", [128, 512], BF16)
    out_d = nc.dram_tensor("out", [ST, 1024], F32, kind="ExternalOutput").ap()
    bin0 = nc.dram_tensor("bin0", [256, ST], BF16)
    bout0 = nc.dram_tensor("bout0", [1024, ST], BF16)
    bin_ = [bin0] * 4
    bout = [bout0] * 4
    bb_, bo_ = Buf("bin"), Buf("bout")
    bin_b = [bb_] * 4
    bout_b = [bo_] * 4
    gown = [nc.dram_tensor("gown%d" % i, [1024, 512], BF16) for i in range(4)]
    gown_b = [Buf("gown%d" % i) for i in range(4)]
    cid_cache = {}

    def cidx(eng):
        if "v" not in cid_cache:
            cid_cache["v"] = eng.snap((eng.partition_id() % 4) * 512, min_val=0, max_val=1536)
        return cid_cache["v"]

    def copy_out(st):
        sc.add("sp", lambda e, st=st: e.dma_start(out=gown[st][:, :], in_=bout0[:, bass.ds(cidx(e), 512)]),
               (bo_,), (gown_b[st],), dkey="go%d" % st)

    def sb(name, shape, dt=F32):
        return T(nc, es, name, shape, dt)

    banks = [T(nc, es, "ps%d" % i, [128, 512], F32, psum=True) for i in range(8)]
    POOLS = {"ALL": [0, 1, 2, 3, 4, 5, 6, 7], "A": [0, 1, 2, 7], "B": [3, 4, 5, 6]}
    pool_i = {"ALL": 0, "A": 0, "B": 0}
    cur_pool = ["ALL"]

    def ps():
        k = cur_pool[0]
        lst = POOLS[k]
        b = banks[lst[pool_i[k] % len(lst)]]
        pool_i[k] += 1
        return b

    def step(gen, pool):
        cur_pool[0] = pool
        try:
            next(gen)
            return True
        except StopIteration:
            return False
        finally:
            cur_pool[0] = "ALL"

    def interleave(g1, g2):
        a1, a2 = g1 is not None, g2 is not None
        while a1 or a2:
            if a1:
                a1 = step(g1, "A")
            if a2:
                a2 = step(g2, "B")

    kid = [0]

    def load_const(dst, src, eng="sp"):
        kid[0] += 1
        sc.add(eng, lambda e, d=dst, s=src: e.dma_start(out=d[:], in_=s), (), (dst.b,), dkey="c%d" % kid[0])

    normw = sb("normw", [128, 8]); load_const(normw, normw_d)
    convw = sb("convw", [128, 12]); load_const(convw, convw_d)
    alog = sb("alog", [128, 1]); load_const(alog, alog_d)
    dtb = sb("dtb", [128, 1]); load_const(dtb, dtb_d)
    gnw = sb("gnw", [128, 64]); load_const(gnw, gnw_d)
    perm = sb("perm", [128, 128], BF16); load_const(perm, perm_d)
    ident = sb("ident", [128, 128], BF16); load_const(ident, ident_d)
    onesbd = sb("onesbd", [128, 128], BF16); load_const(onesbd, onesbd_d)
    onesbdf = sb("onesbdf", [128, 128]); load_const(onesbdf, onesbdf_d)
    ones = sb("ones", [128, 128], BF16); load_const(ones, ones_d)
    lbd = sb("lbd", [128, 128]); load_const(lbd, lbd_d)
    uaug = sb("uaug", [128, 130]); load_const(uaug, uaug_d)
    slneg = sb("slneg", [128, 128]); load_const(slneg, slneg_d)
    li = sb("li", [128, 128]); load_const(li, li_d)
    dmask = sb("dmask", [128, 512], BF16); load_const(dmask, dmask_d)
    nalog = sb("nalog", [128, 1])
    sc.add("act", lambda e: e.activation(nalog[:], alog[:], AF.Exp), (alog.b,), (nalog.b,))
    sc.add("dve", lambda e: e.tensor_scalar(nalog[:], nalog[:], -1.0, None, ALU.mult), (nalog.b,), (nalog.b,))

    XR = 12
    xs = [sb("xs%d" % i, [128, TT]) for i in range(XR)]
    xs_i = [0]

    def ring():
        b = xs[xs_i[0] % XR]
        k = "x%d" % (xs_i[0] % XR)
        xs_i[0] += 1
        return b, k

    cast_i = [0]

    def cast_to(dst_ap, dst_b, src_ap, src_b):
        cast_i[0] += 1
        if cast_i[0] % 2 == 0:
            sc.add("dve", lambda e: e.tensor_copy(dst_ap, src_ap), (src_b,), (dst_b,))
        else:
            sc.add("act", lambda e: e.copy(dst_ap, src_ap), (src_b,), (dst_b,))

    W1 = sb("W1", [128, 8, (NB1 - 1) * 128], BF16)
    W1b = [Buf("W1b%d" % i) for i in range(NB1 - 1)]
    for j in range((NB1 - 1) * 128 // TT):
        for c in range(8):
            stb, k = ring()
            sc.add("sp", lambda e, stb=stb, j=j, c=c: e.dma_start(out=stb[:, :], in_=w1[:, c, j * TT:(j + 1) * TT]),
                   (), (stb.b,), dkey=k)
            for bi in (2 * j, 2 * j + 1):
                cast_i[0] += (bi == 2 * j)
                dst_ap = W1[:, c, bi * 128:(bi + 1) * 128]
                src_ap = stb[:, (bi - 2 * j) * 128:(bi - 2 * j + 1) * 128]
                if cast_i[0] % 2 == 0:
                    sc.add("dve", lambda e, dst_ap=dst_ap, src_ap=src_ap: e.tensor_copy(dst_ap, src_ap), (stb.b,), (W1b[bi],))
                else:
                    sc.add("act", lambda e, dst_ap=dst_ap, src_ap=src_ap: e.copy(dst_ap, src_ap), (stb.b,), (W1b[bi],))
    Wba = sb("Wba", [128, 8, 4], BF16)
    stb, k = ring()
    stv = stb[:, 0:32].rearrange("p (a b) -> p a b", a=8)
    sc.add("sp", lambda e, stv=stv: e.dma_start(out=stv, in_=w1[:, :, BA * 128:BA * 128 + 4]), (), (stb.b,), dkey=k)
    cast_to(Wba[:], Wba.b, stv, stb.b)

    sqb = [sb("sq%d" % i, [128, TT], BF16) for i in range(3)]
    rstd = sb("rstd", [128, TT])
    hT2 = [sb("hT%d" % i, [128, 8, TT], BF16) for i in range(2)]
    cur_hT = [hT2[0]]
    cosb = [sb("cos%d" % i, [128, TT]) for i in range(2)]
    sinb = [sb("sin%d" % i, [128, TT]) for i in range(2)]
    Kt = [[sb("Kt%d_%d" % (g, s), [128, ST], BF16) for s in range(2)] for g in range(3)]
    Qt = [sb("Qt%d" % g, [128, ST], BF16) for g in range(3)]
    Vt = [sb("Vt%d" % g, [128, ST], BF16) for g in range(3)]
    Vs = [[sb("Vs%d_%d" % (g, s), [128, 16, 128], BF16) for s in range(2)] for g in range(3)]
    ndacc = sb("ndacc", [128, 2, ST])
    zaS = sb("zaS", [128, ST], BF16)
    gaT = sb("gaT", [128, ST], BF16)
    gbT = sb("gbT", [128, ST], BF16)
    qraw = [sb("qraw%d" % i, [128, TT], BF16) for i in range(2)]
    rt1 = [sb("rt1_%d" % i, [128, TT]) for i in range(2)]
    rt2 = [sb("rt2_%d" % i, [128, TT]) for i in range(2)]
    Pt = [sb("Pt%d" % i, [128, 512], BF16) for i in range(3)]
    Qx = [sb("Qx%d" % i, [128, 256], BF16) for i in range(2)]
    xpad = [sb("xpad%d" % i, [128, TT + 3], BF16) for i in range(3)]
    cacc = [sb("cacc%d" % i, [128, TT]) for i in range(2)]
    diagw = sb("diagw", [128, 12, 128], BF16)
    gsq = sb("gsq", [128, TT], BF16)
    grn = sb("grn", [128, TT])
    Qbd2 = [sb("Qbd%d" % i, [128, NCH, 128], BF16) for i in range(2)]
    Kbd2 = [sb("Kbd%d" % i, [128, NCH, 128], BF16) for i in range(2)]
    Vbd2 = [sb("Vbd%d" % i, [128, NCH, 128], BF16) for i in range(2)]
    zbS2 = [sb("zbS%d" % i, [128, TT], BF16) for i in range(2)]
    batok = sb("batok", [128, TT // 128, 4])
    bastk2 = [sb("bastk%d" % i, [128, NCH, 2]) for i in range(2)]
    beta2 = [sb("beta%d" % i, [128, NCH]) for i in range(2)]
    gg2 = [sb("gg%d" % i, [128, NCH]) for i in range(2)]
    Rbd = sb("Rbd", [128, NCH, 128])
    E1 = sb("E1", [128, NCH, 128])
    MM2 = sb("MM2", [128, NCH, 128])
    expG = sb("expG", [128, NCH])
    glast = sb("glast", [128, NCH])
    decf = sb("decf", [128, NCH])
    kdsc = sb("kdsc", [128, NCH])
    bsc = sb("bsc", [128, NCH])
    Cb = [sb("Cb%d" % i, [128, NCH, 128], BF16) for i in range(2)]
    Bb = [sb("Bb%d" % i, [128, NCH, 128], BF16) for i in range(2)]
    Xb = [sb("Xb%d" % i, [128, NCH, 128], BF16) for i in range(2)]
    ufin = sb("ufin", [128, NCH, 64])
    Wbd2 = sb("Wbd2", [128, NCH, 128], BF16)
    Wt = sb("Wt", [128, NCH, 128], BF16)
    attn = sb("attn", [128, NCH, 128], BF16)
    attnT = sb("attnT", [128, NCH, 128], BF16)
    Kdec = sb("Kdec", [128, NCH, 128], BF16)
    Sst = sb("Sst", [128, 64])
    Sbf = sb("Sbf", [128, 64], BF16)
    vnew = sb("vnew", [128, 64], BF16)
    oB = sb("oB", [128, 64])
    otile = sb("otile", [128, NCH, 64])
    oss = sb("oss", [128, NCH])
    onbd = sb("onbd", [128, NCH, 128], BF16)

    for k_ in range(12):
        sc.add("dve", lambda e, k_=k_: e.tensor_scalar(diagw[:, k_, :], ident[:], convw[:, k_:k_ + 1], None, ALU.mult),
               (ident.b, convw.b), (diagw.b,))
    for t_ in Qx:
        sc.add("pool", lambda e, t_=t_: e.memset(t_[:], 0.0), (), (t_.b,))
    for t_, eng in ((Sst, "dve"), (Sbf, "dve"), (Wbd2, "pool"), (onbd, "pool"), (Qbd2[0], "pool"), (Kbd2[0], "pool"),
                    (Vbd2[0], "pool"), (Qbd2[1], "pool"), (Kbd2[1], "pool"), (Vbd2[1], "pool"), (xpad[0], "dve"), (xpad[1], "dve"), (xpad[2], "dve")):
        sc.add(eng, lambda e, t_=t_: e.memset(t_[:], 0.0), (), (t_.b,))

    def rmsnorm_tile(src_dram, col0, hdst):
        bufs = []
        for c in range(8):
            xb = xs[xs_i[0] % XR]
            k = "x%d" % (xs_i[0] % XR)
            xs_i[0] += 1
            sc.add("sp", lambda e, xb=xb, c=c: e.dma_start(out=xb[:], in_=src_dram[:, c, col0:col0 + TT]),
                   (), (xb.b,), dkey=k)
            bufs.append(xb)
        pss = ps()
        for c in range(8):
            sq = sqb[c % 3]
            sc.add("act", lambda e, sq=sq, xb=bufs[c]: e.activation(sq[:], xb[:], AF.Square), (bufs[c].b,), (sq.b,))
            sc.add("pe", lambda e, sq=sq, c=c: e.matmul(pss[:, 0:TT], ones[:], sq[:], start=(c == 0), stop=(c == 7)),
                   (sq.b, ones.b), (pss.b,))
        sc.add("act", lambda e: e.activation(rstd[:], pss[:, 0:TT], AF.Ln, bias=EPS, scale=1.0 / 1024.0),
               (pss.b,), (rstd.b,))
        sc.add("act", lambda e: e.activation(rstd[:], rstd[:], AF.Exp, scale=-0.5), (rstd.b,), (rstd.b,))
        for c in range(8):
            sc.add("dve", lambda e, c=c, xb=bufs[c]: e.scalar_tensor_tensor(
                hdst[:, c, :], xb[:], normw[:, c:c + 1], rstd[:], ALU.mult, ALU.mult),
                (bufs[c].b, rstd.b, normw.b), (hdst.b,))

    def proj_fm(blk):
        p = ps()
        hT = cur_hT[0]
        for c in range(8):
            sc.add("pe", lambda e, c=c, p=p, hT=hT: e.matmul(p[:, 0:TT], W1[:, c, blk * 128:(blk + 1) * 128], hT[:, c, :],
                                                            start=(c == 0), stop=(c == 7)), (W1b[blk], hT.b), (p.b,))
        return p

    stmp = [sb("stmp%d" % i, [128, TT]) for i in range(2)]
    stmp_i = [0]

    def sigmoid_to(src_ap, src_b, shape_cols=TT):
        t = stmp[stmp_i[0] % 2]
        stmp_i[0] += 1
        tv = t[:, 0:shape_cols]
        sc.add("act", lambda e: e.activation(tv, src_ap, AF.Exp, scale=-1.0), (src_b,), (t.b,))
        sc.add("act", lambda e: e.activation(tv, tv, AF.Ln, bias=1.0), (t.b,), (t.b,))
        sc.add("act", lambda e: e.activation(tv, tv, AF.Exp, scale=-1.0), (t.b,), (t.b,))
        return t

    def perm_view(t_ap_tile, g, tt):
        d = DIL[g]
        n = TT // d
        v = t_ap_tile[:, :].rearrange("p (r m) -> p r m", r=d)
        return v[:, :, tt * n:(tt + 1) * n]

    def src_view(ap, g):
        d = DIL[g]
        return ap.rearrange("p (m r) -> p r m", r=d)

    rope_i = [0]

    def prenorm(ti):
        col0 = ti * TT
        cb, sb_ = cosb[ti % 2], sinb[ti % 2]
        sc.add("sp", lambda e: e.dma_start(out=cb[:], in_=cos_d[:, col0:col0 + TT]), (), (cb.b,), dkey="cos%d" % (ti % 2))
        sc.add("sp", lambda e: e.dma_start(out=sb_[:], in_=sin_d[:, col0:col0 + TT]), (), (sb_.b,), dkey="sin%d" % (ti % 2))
        rmsnorm_tile(xT, col0, hT2[ti % 2])

    prenorm(0)
    for st in range(4):
        slot = st % 2
        def tileA(st, slot, tt):
            ti = st * TPS + tt
            col0 = ti * TT
            par = ti % 2
            Qbd, Kbd, Vbd, zbS, bastk, beta, gg = Qbd2[par], Kbd2[par], Vbd2[par], zbS2[par], bastk2[par], beta2[par], gg2[par]
            cb, sb_ = cosb[ti % 2], sinb[ti % 2]
            hT = hT2[ti % 2]
            cur_hT[0] = hT

            def dsa_qk(g, qk):
                p = proj_fm(3 * g + qk)
                qr = qraw[rope_i[0] % 2]
                t1 = rt1[rope_i[0] % 2]
                t2 = rt2[rope_i[0] % 2]
                rope_i[0] += 1
                sc.add("act", lambda e, p=p, qr=qr: e.copy(qr[:], p[:, 0:TT]), (p.b,), (qr.b,))
                p2 = ps()
                sc.add("pe", lambda e, p2=p2, qr=qr: e.matmul(p2[:, 0:TT], perm[:], qr[:], start=True, stop=True),
                       (perm.b, qr.b), (p2.b,))
                sc.add("dve", lambda e, p=p, t1=t1, cb=cb: e.tensor_tensor(t1[:], p[:, 0:TT], cb[:], ALU.mult),
                       (p.b, cb.b), (t1.b,))
                sc.add("dve", lambda e, p2=p2, t2=t2, sb_=sb_: e.tensor_tensor(t2[:], p2[:, 0:TT], sb_[:], ALU.mult),
                       (p2.b, sb_.b), (t2.b,))
                dst = Qt[g] if qk == 0 else Kt[g][slot]
                sc.add("pool", lambda e, dst=dst, t1=t1, t2=t2, g=g, tt=tt: e.tensor_tensor(
                    perm_view(dst, g, tt), src_view(t1[:, :], g), src_view(t2[:, :], g), ALU.add),
                    (t1.b, t2.b), (dst.b,))

            def dsa_v(g):
                p = proj_fm(3 * g + 2)
                sc.add("act", lambda e, p=p, g=g, tt=tt: e.copy(perm_view(Vt[g], g, tt), src_view(p[:, 0:TT], g)),
                       (p.b,), (Vt[g].b,))

            def z_a():
                p = proj_fm(ZA)
                sg = sigmoid_to(p[:, 0:TT], p.b)
                sc.add("dve", lambda e, p=p, tt=tt, sg=sg: e.tensor_tensor(zaS[:, tt * TT:(tt + 1) * TT], p[:, 0:TT], sg[:], ALU.mult),
                       (p.b, sg.b), (zaS.b,))

            def z_b():
                p = proj_fm(ZB)
                sg = sigmoid_to(p[:, 0:TT], p.b)
                sc.add("dve", lambda e, p=p, sg=sg: e.tensor_tensor(zbS[:], p[:, 0:TT], sg[:], ALU.mult), (p.b, sg.b), (zbS.b,))

            def gdn_in(j):
                blk = (GQ, GK, GV)[j]
                p = proj_fm(blk)
                xp = xpad[j]
                ca = cacc[j % 2]
                sc.add("act", lambda e, xp=xp: e.copy(xp[:, 0:3], xp[:, TT:TT + 3]), (xp.b,), (xp.b,))
                sc.add("act", lambda e, xp=xp, p=p: e.copy(xp[:, 3:TT + 3], p[:, 0:TT]), (p.b, xp.b), (xp.b,))
                pc = ps()
                for tap in range(4):
                    sc.add("pe", lambda e, xp=xp, pc=pc, j=j, tap=tap: e.matmul(
                        pc[:, 0:TT], diagw[:, j * 4 + tap, :], xp[:, tap:tap + TT], start=(tap == 0), stop=(tap == 3)),
                        (diagw.b, xp.b), (pc.b,))
                if j < 2:
                    sg = sigmoid_to(pc[:, 0:TT], pc.b)
                    sc.add("dve", lambda e, ca=ca, pc=pc, sg=sg: e.tensor_tensor(ca[:], pc[:, 0:TT], sg[:], ALU.mult), (pc.b, sg.b), (ca.b,))
                    sc.add("act", lambda e, ca=ca: e.activation(gsq[:], ca[:], AF.Square), (ca.b,), (gsq.b,))
                    p3 = ps()
                    sc.add("pe", lambda e, p3=p3: e.matmul(p3[:, 0:TT], onesbd[:], gsq[:], start=True, stop=True),
                           (onesbd.b, gsq.b), (p3.b,))
                    scl = 64.0 if j == 0 else 1.0
                    sc.add("act", lambda e, p3=p3, scl=scl: e.activation(grn[:], p3[:, 0:TT], AF.Ln, bias=EPS * scl, scale=scl),
                           (p3.b,), (grn.b,))
                    sc.add("act", lambda e: e.activation(grn[:], grn[:], AF.Exp, scale=-0.5), (grn.b,), (grn.b,))
                    dstb = Qbd if j == 0 else Kbd
                    for h in range(2):
                        hs = slice(h * 64, (h + 1) * 64)
                        sc.add("dve", lambda e, dstb=dstb, hs=hs, h=h, ca=ca: e.tensor_tensor(
                            dstb[hs, :, h * 64:(h + 1) * 64], ca[hs, :].rearrange("p (c i) -> p c i", i=64),
                            grn[hs, :].rearrange("p (c i) -> p c i", i=64), ALU.mult), (ca.b, grn.b), (dstb.b,))
                else:
                    sg = sigmoid_to(pc[:, 0:TT], pc.b)
                    for h in range(2):
                        hs = slice(h * 64, (h + 1) * 64)
                        sc.add("dve", lambda e, hs=hs, h=h, pc=pc, sg=sg: e.tensor_tensor(
                            Vbd[hs, :, h * 64:(h + 1) * 64], pc[hs, 0:TT].rearrange("p (c i) -> p c i", i=64),
                            sg[hs, :].rearrange("p (c i) -> p c i", i=64), ALU.mult), (pc.b, sg.b), (Vbd.b,))

            def beta_part():
                pb = ps()
                for tb in range(TT // 128):
                    for c in range(8):
                        sc.add("pe", lambda e, tb=tb, c=c, pb=pb: e.matmul(
                            pb[:, tb * 4:(tb + 1) * 4], hT[:, c, tb * 128:(tb + 1) * 128], Wba[:, c, :],
                            start=(c == 0), stop=(c == 7)), (hT.b, Wba.b), (pb.b,))
                sc.add("dve", lambda e, pb=pb: e.tensor_copy(batok[:], pb[:, 0:(TT // 128) * 4].rearrange("p (b k) -> p b k", k=4)),
                       (pb.b,), (batok.b,))
                for h in range(2):
                    for cp in range(2):
                        sc.add("sp", lambda e, h=h, cp=cp: e.dma_start(
                            out=bastk[h * 64:(h + 1) * 64, cp:NCH:2, :], in_=batok[cp * 64:(cp + 1) * 64, :, 2 * h:2 * h + 2],
                            allow_slow_non_contiguous=True), (batok.b,), (bastk.b,), dkey="ba%d" % par)
                sc.add("act", lambda e: e.activation(beta[:], bastk[:, :, 0], AF.Exp, scale=-1.0), (bastk.b,), (beta.b,))
                sc.add("act", lambda e: e.activation(beta[:], beta[:], AF.Ln, bias=1.0), (beta.b,), (beta.b,))
                sc.add("act", lambda e: e.activation(beta[:], beta[:], AF.Exp, scale=-1.0), (beta.b,), (beta.b,))
                sc.add("act", lambda e: e.activation(gg[:], bastk[:, :, 1], AF.Exp, bias=dtb[:]), (bastk.b, dtb.b), (gg.b,))
                sc.add("act", lambda e: e.activation(gg[:], gg[:], AF.Ln, bias=1.0), (gg.b,), (gg.b,))
                sc.add("dve", lambda e: e.tensor_scalar(gg[:], gg[:], nalog[:, 0:1], None, ALU.mult), (gg.b, nalog.b), (gg.b,))

            dsa_qk(0, 0); z_a(); yield
            dsa_qk(0, 1); z_b(); yield
            dsa_v(0); gdn_in(0); yield
            if ti + 1 < NTT:
                prenorm(ti + 1)
            yield
            dsa_qk(1, 0); yield
            dsa_qk(1, 1); gdn_in(1); yield
            dsa_v(1); yield
            dsa_qk(2, 0); gdn_in(2); yield
            dsa_qk(2, 1); beta_part(); yield
            dsa_v(2); yield

        def tileB(st, slot, tt):
            ti = st * TPS + tt
            par = ti % 2
            Qbd, Kbd, Vbd, zbS, bastk, beta, gg = Qbd2[par], Kbd2[par], Vbd2[par], zbS2[par], bastk2[par], beta2[par], gg2[par]
            pgl = ps()
            sc.add("pe", lambda e, pgl=pgl: e.matmul(pgl[:, 0:NCH], onesbdf[:], gg[:], start=True, stop=True),
                   (onesbdf.b, gg.b), (pgl.b,))
            sc.add("pe", lambda e, pgl=pgl: e.matmul(pgl[:, NCH:2 * NCH], lbd[:], gg[:], start=True, stop=True),
                   (lbd.b, gg.b), (pgl.b,))
            sc.add("act", lambda e, pgl=pgl: e.copy(glast[:], pgl[:, 0:NCH]), (pgl.b,), (glast.b,))
            sc.add("act", lambda e, pgl=pgl: e.activation(decf[:], pgl[:, 0:NCH], AF.Exp), (pgl.b,), (decf.b,))
            sc.add("act", lambda e, pgl=pgl: e.activation(expG[:], pgl[:, NCH:2 * NCH], AF.Exp), (pgl.b,), (expG.b,))
            sc.add("dve", lambda e, pgl=pgl: e.tensor_tensor(kdsc[:], glast[:], pgl[:, NCH:2 * NCH], ALU.subtract),
                   (glast.b, pgl.b), (kdsc.b,))
            sc.add("act", lambda e: e.activation(kdsc[:], kdsc[:], AF.Exp), (kdsc.b,), (kdsc.b,))
            sc.add("dve", lambda e: e.tensor_tensor(bsc[:], beta[:], expG[:], ALU.mult), (beta.b, expG.b), (bsc.b,))
            sc.add("dve", lambda e: e.tensor_tensor(
                Rbd[:], uaug[:, 0:128].unsqueeze(1).broadcast_to([128, NCH, 128]),
                gg[:, :].unsqueeze(2).broadcast_to([128, NCH, 128]), ALU.mult), (uaug.b, gg.b), (Rbd.b,))
            pD = ps()
            for c in range(NCH):
                sc.add("pe", lambda e, c=c, pD=pD: e.matmul(pD[:, c * 128:(c + 1) * 128], lbd[:], Rbd[:, c, :], start=True, stop=True),
                       (lbd.b, Rbd.b), (pD.b,))
            sc.add("act", lambda e, pD=pD: e.activation(E1[:].rearrange("p c n -> p (c n)"), pD[:, 0:NCH * 128], AF.Exp), (pD.b,), (E1.b,))
            yield
            pKK = ps()
            pQK = ps()
            for c in range(NCH):
                sc.add("pe", lambda e, c=c, pKK=pKK: e.matmul(pKK[:, c * 128:(c + 1) * 128], Kbd[:, c, :], Kbd[:, c, :], start=True, stop=True),
                       (Kbd.b,), (pKK.b,))
            for c in range(NCH):
                sc.add("pe", lambda e, c=c, pQK=pQK: e.matmul(pQK[:, c * 128:(c + 1) * 128], Qbd[:, c, :], Kbd[:, c, :], start=True, stop=True),
                       (Qbd.b, Kbd.b), (pQK.b,))
            sc.add("dve", lambda e: e.tensor_tensor(MM2[:], E1[:], beta[:, :].unsqueeze(2).broadcast_to([128, NCH, 128]), ALU.mult),
                   (E1.b, beta.b), (MM2.b,))
            sc.add("pool", lambda e: e.tensor_tensor(MM2[:], MM2[:], slneg[:, :].unsqueeze(1).broadcast_to([128, NCH, 128]), ALU.mult),
                   (MM2.b, slneg.b), (MM2.b,))
            sc.add("dve", lambda e, pKK=pKK: e.tensor_tensor(Cb[0][:].rearrange("p c n -> p (c n)"), pKK[:, 0:NCH * 128],
                                                           MM2[:].rearrange("p c n -> p (c n)"), ALU.mult), (pKK.b, MM2.b), (Cb[0].b,))
            sc.add("pool", lambda e: e.tensor_tensor(E1[:], E1[:], li[:, :].unsqueeze(1).broadcast_to([128, NCH, 128]), ALU.mult),
                   (E1.b, li.b), (E1.b,))
            sc.add("dve", lambda e, pQK=pQK: e.tensor_tensor(attn[:].rearrange("p c n -> p (c n)"), pQK[:, 0:NCH * 128],
                                                           E1[:].rearrange("p c n -> p (c n)"), ALU.mult), (pQK.b, E1.b), (attn.b,))
            yield
            pT1 = ps(); pT2 = ps(); pT3 = ps(); pT4 = ps()
            for c in range(NCH):
                for (pt, src_) in ((pT1, Cb[0]), (pT2, attn), (pT3, Kbd), (pT4, Vbd)):
                    sc.add("pe", lambda e, c=c, pt=pt, src_=src_: e.transpose(
                        pt[:].bitcast(BF16)[:, c * 128:(c + 1) * 128], src_[:, c, :], ident[:]), (src_.b, ident.b), (pt.b,))
            sc.add("act", lambda e: e.copy(Bb[0][:].rearrange("p c n -> p (c n)"), pT1[:].bitcast(BF16)[:, 0:NCH * 128]), (pT1.b,), (Bb[0].b,))
            sc.add("act", lambda e: e.copy(attnT[:].rearrange("p c n -> p (c n)"), pT2[:].bitcast(BF16)[:, 0:NCH * 128]), (pT2.b,), (attnT.b,))
            pT3v = pT3[:].bitcast(BF16)[:, 0:NCH * 128].rearrange("p (c n) -> p c n", n=128)
            pT4v = pT4[:].bitcast(BF16)[:, 0:NCH * 128].rearrange("p (c n) -> p c n", n=128)
            sc.add("dve", lambda e, pT3v=pT3v: e.tensor_tensor(Kdec[:], pT3v, kdsc[:, :].unsqueeze(2).broadcast_to([128, NCH, 128]), ALU.mult),
                   (pT3.b, kdsc.b), (Kdec.b,))
            for h in range(2):
                hs = slice(h * 64, (h + 1) * 64)
                sc.add("dve", lambda e, h=h, hs=hs, pT3v=pT3v: e.tensor_tensor(
                    Xb[0][hs, :, 64:128], pT3v[hs, :, h * 64:(h + 1) * 64], bsc[hs, :].unsqueeze(2).broadcast_to([64, NCH, 64]), ALU.mult),
                    (pT3.b, bsc.b), (Xb[0].b,))
                sc.add("dve", lambda e, h=h, hs=hs, pT4v=pT4v: e.tensor_tensor(
                    Xb[0][hs, :, 0:64], pT4v[hs, :, h * 64:(h + 1) * 64], beta[hs, :].unsqueeze(2).broadcast_to([64, NCH, 64]), ALU.mult),
                    (pT4.b, beta.b), (Xb[0].b,))
            yield
            cur = 0
            for lvl in range(6):
                yield
                Bc, Cc, Xc = Bb[cur], Cb[cur], Xb[cur]
                Bn, Cn, Xn = Bb[1 - cur], Cb[1 - cur], Xb[1 - cur]
                pX = ps()
                for c in range(NCH):
                    sc.add("pe", lambda e, c=c, pX=pX, Bc=Bc, Xc=Xc: e.matmul(pX[:, c * 128:(c + 1) * 128], Bc[:, c, :], Xc[:, c, :], start=True, stop=True),
                           (Bc.b, Xc.b), (pX.b,))
                if lvl < 5:
                    sc.add("dve", lambda e, pX=pX, Xc=Xc, Xn=Xn: e.tensor_tensor(
                        Xn[:].rearrange("p c n -> p (c n)"), pX[:, 0:NCH * 128], Xc[:].rearrange("p c n -> p (c n)"), ALU.add),
                        (pX.b, Xc.b), (Xn.b,))
                    pB = ps()
                    for c in range(NCH):
                        sc.add("pe", lambda e, c=c, pB=pB, Bc=Bc, Cc=Cc: e.matmul(pB[:, c * 128:(c + 1) * 128], Cc[:, c, :], Bc[:, c, :], start=True, stop=True),
                               (Bc.b, Cc.b), (pB.b,))
                    sc.add("act", lambda e, pB=pB, Bn=Bn: e.copy(Bn[:].rearrange("p c n -> p (c n)"), pB[:, 0:NCH * 128]), (pB.b,), (Bn.b,))
                    if lvl < 4:
                        pC = ps()
                        for c in range(NCH):
                            sc.add("pe", lambda e, c=c, pC=pC, Bc=Bc, Cc=Cc: e.matmul(pC[:, c * 128:(c + 1) * 128], Bc[:, c, :], Cc[:, c, :], start=True, stop=True),
                                   (Bc.b, Cc.b), (pC.b,))
                        sc.add("act", lambda e, pC=pC, Cn=Cn: e.copy(Cn[:].rearrange("p c n -> p (c n)"), pC[:, 0:NCH * 128]), (pC.b,), (Cn.b,))
                    cur = 1 - cur
                else:
                    pXv = pX[:, 0:NCH * 128].rearrange("p (c n) -> p c n", n=128)
                    sc.add("dve", lambda e, pXv=pXv, Xc=Xc: e.tensor_tensor(ufin[:], pXv[:, :, 0:64], Xc[:, :, 0:64], ALU.add),
                           (pX.b, Xc.b), (ufin.b,))
                    for h in range(2):
                        hs = slice(h * 64, (h + 1) * 64)
                        sc.add("dve", lambda e, pXv=pXv, Xc=Xc, hs=hs, h=h: e.tensor_tensor(
                            Wbd2[hs, :, h * 64:(h + 1) * 64], pXv[hs, :, 64:128], Xc[hs, :, 64:128], ALU.add),
                            (pX.b, Xc.b), (Wbd2.b,))
            pT5 = ps()
            for c in range(NCH):
                sc.add("pe", lambda e, c=c, pT5=pT5: e.transpose(pT5[:].bitcast(BF16)[:, c * 128:(c + 1) * 128], Wbd2[:, c, :], ident[:]),
                       (Wbd2.b, ident.b), (pT5.b,))
            sc.add("act", lambda e, pT5=pT5: e.copy(Wt[:].rearrange("p c n -> p (c n)"), pT5[:].bitcast(BF16)[:, 0:NCH * 128]), (pT5.b,), (Wt.b,))
            yield
            for c in range(NCH):
                yield
                pw = ps()
                sc.add("pe", lambda e, c=c, pw=pw: e.matmul(pw[:, 0:64], Wt[:, c, :], Sbf[:], start=True, stop=True), (Wt.b, Sbf.b), (pw.b,))
                sc.add("dve", lambda e, c=c, pw=pw: e.tensor_tensor(vnew[:], ufin[:, c, :], pw[:, 0:64], ALU.subtract), (ufin.b, pw.b), (vnew.b,))
                po = ps()
                sc.add("pe", lambda e, c=c, po=po: e.matmul(po[:, 0:64], Qbd[:, c, :], Sbf[:], start=True, stop=True), (Qbd.b, Sbf.b), (po.b,))
                sc.add("pe", lambda e, c=c, po=po: e.matmul(po[:, 64:128], attnT[:, c, :], vnew[:], start=True, stop=True), (attnT.b, vnew.b), (po.b,))
                sc.add("pe", lambda e, c=c, po=po: e.matmul(po[:, 128:192], Kdec[:, c, :], vnew[:], start=True, stop=True), (Kdec.b, vnew.b), (po.b,))
                sc.add("dve", lambda e, c=c, po=po: e.scalar_tensor_tensor(Sst[:], Sst[:], decf[:, c:c + 1], po[:, 128:192], ALU.mult, ALU.add),
                       (Sst.b, decf.b, po.b), (Sst.b,))
                sc.add("act", lambda e: e.copy(Sbf[:], Sst[:]), (Sst.b,), (Sbf.b,))
                sc.add("act", lambda e, po=po: e.copy(oB[:], po[:, 64:128]), (po.b,), (oB.b,))
                sc.add("dve", lambda e, c=c, po=po: e.scalar_tensor_tensor(otile[:, c, :], po[:, 0:64], expG[:, c:c + 1], oB[:], ALU.mult, ALU.add),
                       (po.b, expG.b, oB.b), (otile.b,))
            yield
            sc.add("pool", lambda e: e.tensor_tensor(Rbd[:, :, 0:64], otile[:], otile[:], ALU.mult), (otile.b,), (Rbd.b,))
            sc.add("dve", lambda e: e.tensor_reduce(oss[:], Rbd[:, :, 0:64], AX.X, ALU.add), (Rbd.b,), (oss.b,))
            sc.add("act", lambda e: e.activation(oss[:], oss[:], AF.Ln, bias=EPS, scale=1.0 / 64.0), (oss.b,), (oss.b,))
            sc.add("act", lambda e: e.activation(oss[:], oss[:], AF.Exp, scale=-0.5), (oss.b,), (oss.b,))
            sc.add("dve", lambda e: e.tensor_tensor(otile[:], otile[:], oss[:, :].unsqueeze(2).broadcast_to([128, NCH, 64]), ALU.mult),
                   (otile.b, oss.b), (otile.b,))
            sc.add("pool", lambda e: e.tensor_tensor(otile[:], otile[:], gnw[:, :].unsqueeze(1).broadcast_to([128, NCH, 64]), ALU.mult),
                   (otile.b, gnw.b), (otile.b,))
            for h in range(2):
                hs = slice(h * 64, (h + 1) * 64)
                sc.add("act", lambda e, hs=hs, h=h: e.copy(onbd[hs, :, h * 64:(h + 1) * 64], otile[hs, :, :]), (otile.b,), (onbd.b,))
            pT6 = ps()
            for c in range(NCH):
                sc.add("pe", lambda e, c=c, pT6=pT6: e.transpose(pT6[:].bitcast(BF16)[:, c * 128:(c + 1) * 128], onbd[:, c, :], ident[:]),
                       (onbd.b, ident.b), (pT6.b,))
            for h in range(2):
                hs = slice(h * 64, (h + 1) * 64)
                sc.add("dve", lambda e, hs=hs, h=h, tt=tt, pT6=pT6: e.tensor_tensor(
                    gbT[hs, tt * TT:(tt + 1) * TT].rearrange("p (c i) -> p c i", i=64),
                    pT6[:].bitcast(BF16)[hs, 0:NCH * 128].rearrange("p (c n) -> p c n", n=128)[:, :, h * 64:(h + 1) * 64],
                    zbS[hs, :].rearrange("p (c i) -> p c i", i=64), ALU.mult), (pT6.b, zbS.b), (gbT.b,))
        prevB = None
        for tt in range(TPS):
            interleave(tileA(st, slot, tt), prevB)
            prevB = tileB(st, slot, tt)

        def w2_stage_pieces(stage):
            def flat(t_):
                return t_[:].rearrange("p b n -> p (b n)") if len(t_[:].shape) == 3 else t_[:]
            def gate(t_, c):
                return [(t_, flat(t_)[:, j * TT:(j + 1) * TT], wg[:, c, j * TT:(j + 1) * TT]) for j in range(2048 // TT)]
            def two(t_, src, i):
                return [(t_, flat(t_)[:, (c % 2) * 1024 + j * TT:(c % 2) * 1024 + (j + 1) * TT], src[:, c, j * TT:(j + 1) * TT])
                        for c in (2 * i, 2 * i + 1) for j in range(1024 // TT)]
            if stage == 0:
                return two(Vt[0], wua, 1) + two(Vt[1], wub, 0) + two(Vt[2], wub, 1)
            if stage == 1:
                return gate(Kt[0][0], 0) + gate(Kt[0][1], 1) + gate(Qt[0], 6) + two(Vs[0][0], wo, 0) + two(Vs[0][1], wo, 1)
            if stage == 2:
                return gate(Kt[1][0], 2) + gate(Kt[1][1], 3) + gate(Qt[1], 7) + two(Vs[1][0], wo, 2) + two(Vs[1][1], wo, 3)
            return gate(Kt[2][0], 4) + gate(Kt[2][1], 5) + two(Qt[2], wua, 0)

        def emit_pieces(q, n):
            for _ in range(min(n, len(q))):
                t_, dst_ap, src_ap = q.pop(0)
                stb, k = ring()
                sc.add("sp", lambda e, stb=stb, src_ap=src_ap: e.dma_start(out=stb[:, :], in_=src_ap), (), (stb.b,), dkey=k)
                cast_to(dst_ap, t_.b, stb[:, :], stb.b)

        def attention(st, slot):
            wq = []
            for g in range(3):
                for q4 in range(4):
                    pv = ps()
                    for j in range(4):
                        blk = q4 * 4 + j
                        sc.add("pe", lambda e, pv=pv, j=j, blk=blk, g=g: e.transpose(
                            pv[:].bitcast(BF16)[:, j * 128:(j + 1) * 128], Vt[g][:, blk * 128:(blk + 1) * 128], ident[:]),
                            (Vt[g].b, ident.b), (pv.b,))
                    sc.add("act", lambda e, pv=pv, q4=q4, g=g, slot=slot: e.copy(
                        Vs[g][slot][:, q4 * 4:(q4 + 1) * 4, :].rearrange("p b n -> p (b n)"), pv[:].bitcast(BF16)[:, 0:512]),
                        (pv.b,), (Vs[g][slot].b,))
                    yield
            units = [(g, blk) for g in range(3) for blk in range(16)]
            if st == 3:
                wq.extend(w2_stage_pieces(0))

            def stage_s(u):
                g, blk = units[u]
                d = DIL[g]
                nps = 16 // d
                r, n = blk // nps, blk % nps
                halves = []
                if n > 0:
                    halves.append((0, slot, blk - 1))
                elif st > 0:
                    halves.append((0, 1 - slot, r * nps + nps - 1))
                halves.append((1, slot, blk))
                nh = len(halves)
                pS = ps()
                Pb = Pt[u % 3]
                qx = Qx[u % 2]
                sc.add("pool", lambda e, qx=qx, g=g, blk=blk: e.tensor_copy(qx[0:64, 0:128], Qt[g][0:64, blk * 128:(blk + 1) * 128]),
                       (Qt[g].b,), (qx.b,))
                sc.add("act", lambda e, qx=qx, g=g, blk=blk: e.copy(qx[64:128, 128:256], Qt[g][64:128, blk * 128:(blk + 1) * 128]),
                       (Qt[g].b,), (qx.b,))
                for (hf, sl, kb) in halves:
                    sc.add("pe", lambda e, hf=hf, sl=sl, kb=kb, g=g, pS=pS, qx=qx: e.matmul(
                        pS[:, hf * 256:(hf + 1) * 256], Kt[g][sl][:, kb * 128:(kb + 1) * 128], qx[:, :],
                        start=True, stop=True), (Kt[g][sl].b, qx.b), (pS.b,))
                lo = halves[0][0] * 256
                sc.add("act", lambda e, lo=lo, Pb=Pb, pS=pS: e.activation(
                    Pb[:, lo:512], pS[:, lo:512], AF.Exp, scale=0.125), (pS.b,), (Pb.b,))
                sc.add("dve", lambda e, Pb=Pb, lo=lo: e.tensor_tensor(Pb[:, lo:512], Pb[:, lo:512], dmask[:, lo:512], ALU.mult),
                       (Pb.b, dmask.b), (Pb.b,))
                return (g, d, r, n, halves, nh, Pb)

            def stage_pv(ctx):
                g, d, r, n, halves, nh, Pb = ctx
                pO = ps()
                for h in range(2):
                    hs = slice(h * 64, (h + 1) * 64)
                    for k, (hf, sl, kb) in enumerate(halves):
                        sc.add("pe", lambda e, h=h, hs=hs, hf=hf, sl=sl, kb=kb, k=k, g=g, Pb=Pb, pO=pO, nh=nh: e.matmul(
                            pO[hs, 0:128], Vs[g][sl][:, kb, h * 64:(h + 1) * 64], Pb[:, hf * 256 + h * 128:hf * 256 + (h + 1) * 128],
                            start=(k == 0), stop=(k == nh - 1)), (Vs[g][sl].b, Pb.b), (pO.b,))
                    for k, (hf, sl, kb) in enumerate(halves):
                        sc.add("pe", lambda e, h=h, hs=hs, hf=hf, k=k, Pb=Pb, pO=pO, nh=nh: e.matmul(
                            pO[hs, 128:256], ones[:, 0:64], Pb[:, hf * 256 + h * 128:hf * 256 + (h + 1) * 128],
                            start=(k == 0), stop=(k == nh - 1)), (ones.b, Pb.b), (pO.b,))
                off = n * 128 * d + r
                dstv = ndacc[:, :, off:off + 127 * d + 1:d] if d > 1 else ndacc[:, :, off:off + 128]
                srcv = pO[:, 0:256].rearrange("p (a q) -> p a q", a=2)
                if g == 0:
                    sc.add("act", lambda e, dstv=dstv, srcv=srcv: e.copy(dstv, srcv), (pO.b,), (ndacc.b,))
                else:
                    sc.add("dve", lambda e, dstv=dstv, srcv=srcv: e.tensor_tensor(dstv, srcv, dstv, ALU.add), (pO.b, ndacc.b), (ndacc.b,))

            ctx = stage_s(0)
            for u in range(len(units)):
                nxt = stage_s(u + 1) if u + 1 < len(units) else None
                stage_pv(ctx)
                ctx = nxt
                if st == 3:
                    if u == 15:
                        wq.extend(w2_stage_pieces(1))
                    if u == 31:
                        wq.extend(w2_stage_pieces(2))
                    emit_pieces(wq, 3)
                yield
            sc.add("act", lambda e: e.activation(ndacc[:, 1, :], ndacc[:, 1, :], AF.Ln), (ndacc.b,), (ndacc.b,))
            sc.add("act", lambda e: e.activation(ndacc[:, 1, :], ndacc[:, 1, :], AF.Exp, scale=-1.0), (ndacc.b,), (ndacc.b,))
            sc.add("dve", lambda e: e.tensor_tensor(ndacc[:, 0, :], ndacc[:, 0, :], ndacc[:, 1, :], ALU.mult), (ndacc.b,), (ndacc.b,))
            sc.add("pool", lambda e: e.tensor_tensor(gaT[:], ndacc[:, 0, :], zaS[:], ALU.mult), (ndacc.b, zaS.b), (gaT.b,))
            if st == 3:
                wq.extend(w2_stage_pieces(3))
                emit_pieces(wq, len(wq))
            yield

        interleave(attention(st, slot), prevB)
        if st > 0:
            copy_out(st - 1)
        sc.add("pool", lambda e, st=st: e.dma_start(out=bin_[st][0:128, :], in_=gaT[:]), (gaT.b,), (bin_b[st],), dkey="bi")
        sc.add("pool", lambda e, st=st: e.dma_start(out=bin_[st][128:256, :], in_=gbT[:]), (gbT.b,), (bin_b[st],), dkey="bi")
        sc.add("pool", lambda e, st=st: e.collective_compute(
            "AllGather", ALU.bypass, replica_groups=[[0, 1, 2, 3], [4, 5, 6, 7]], ins=[bin_[st][:, :]], outs=[bout[st][:, :]]),
            (bin_b[st],), (bout_b[st],), dkey="cc%d" % st)

    copy_out(3)
    es2 = es
    Wg = T.__new__(T); Wg.b = Buf("Wg")
    def alias(src, shape_str, dt=None, **kw):
        ap = src[:]
        if dt is not None:
            ap = ap.bitcast(dt)
        return ap

    class A:
        def __init__(self, ap, name, base=None, share=False):
            self.ap = ap
            if share:
                self.b = base.b
            else:
                self.b = Buf(name)
                if base is not None:
                    sc.link(self.b, base.b)

        def __getitem__(self, k):
            return self.ap[k]

    wg_parts = [Kt[0][0], Kt[0][1], Kt[1][0], Kt[1][1], Kt[2][0], Kt[2][1], Qt[0], Qt[1]]
    Wgc = [A(p_[:], "Wg%d" % i, p_, True) for i, p_ in enumerate(wg_parts)]
    Wua = [A(t_[:], "wua%d" % i, t_, True) for i, t_ in enumerate((Qt[2], Vt[0]))]
    Wub = [A(t_[:], "wub%d" % i, t_, True) for i, t_ in enumerate((Vt[1], Vt[2]))]
    wo_parts = [Vs[0][0], Vs[0][1], Vs[1][0], Vs[1][1]]
    Woc = [A(t_[:].rearrange("p b n -> p (b n)"), "wo%d" % i, t_, True) for i, t_ in enumerate(wo_parts)]
    fnw = A(ndacc[:, 0, 0:1024], "fnw", ndacc)
    xown = [A(ndacc[:, 1, 0:1024], "xown0", ndacc), A(ndacc[:, 1, 1024:2048], "xown1", ndacc)]
    ybuf = A(ndacc[:, 0, 1024:2048], "ybuf", ndacc)
    ga_g = A(Vs[2][0][:].rearrange("p b n -> p (b n)")[:, 0:4 * 512].rearrange("p (r t) -> p r t", r=4), "ga_g", Vs[2][0])
    gb_g = A(Vs[2][1][:].rearrange("p b n -> p (b n)")[:, 0:4 * 512].rearrange("p (r t) -> p r t", r=4), "gb_g", Vs[2][1])
    hTo = hT2[0]
    cur_hT[0] = hTo
    sgA = rt1
    sgB = rt2
    merged = A(gbT[:].rearrange("p (c t) -> p c t", c=8), "merged", gbT)
    ssq2 = A(oss[:, 0:1], "ssq2", oss)
    junk = A(gaT[:].bitcast(F32), "junk", gaT)

    def load_w2(dsts, src, nchunk, ncols, per):
        for c in range(nchunk):
            d_ = dsts[c // per]
            base = (c % per) * ncols
            for j in range(ncols // TT):
                stb, k = ring()
                sc.add("sp", lambda e, stb=stb, c=c, j=j: e.dma_start(out=stb[:, :], in_=src[:, c, j * TT:(j + 1) * TT]), (), (stb.b,), dkey=k)
                cast_to(d_[:, base + j * TT: base + (j + 1) * TT], d_.b, stb[:, :], stb.b)

    sc.add("sp", lambda e: e.dma_start(out=fnw[:, :], in_=fnw_d), (), (fnw.b,), dkey="fnw")

    for st in range(4):
        def p2(st, half):
            tcol = half * TT
            for r in range(4):
                sc.add("sp", lambda e, st=st, r=r, tcol=tcol: e.dma_start(
                    out=ga_g[:, r, 0:TT], in_=gown[st][r * 256:r * 256 + 128, tcol:tcol + TT]),
                    (gown_b[st],), (ga_g.b,), dkey="gag")
                sc.add("sp", lambda e, st=st, r=r, tcol=tcol: e.dma_start(
                    out=gb_g[:, r, 0:TT], in_=gown[st][r * 256 + 128:r * 256 + 256, tcol:tcol + TT]),
                    (gown_b[st],), (gb_g.b,), dkey="gbg")
            rmsnorm_tile(xTo, st * 512 + tcol, hTo)
            for mb in range(8):
                k2 = mb % 2
                pa = ps()
                for c in range(8):
                    sc.add("pe", lambda e, c=c, pa=pa, mb=mb: e.matmul(pa[:, 0:TT], Wgc[c][:, mb * 128:(mb + 1) * 128], hTo[:, c, :],
                                                                     start=(c == 0), stop=(c == 7)), (Wgc[c].b, hTo.b), (pa.b,))
                sc.add("act", lambda e, pa=pa, k2=k2: e.activation(sgA[k2][:], pa[:, 0:TT], AF.Sigmoid), (pa.b,), (sgA[k2].b,))
                pb_ = ps()
                for c in range(8):
                    sc.add("pe", lambda e, c=c, pb_=pb_, mb=mb: e.matmul(pb_[:, 0:TT], Wgc[c][:, 1024 + mb * 128:1024 + (mb + 1) * 128], hTo[:, c, :],
                                                                      start=(c == 0), stop=(c == 7)), (Wgc[c].b, hTo.b), (pb_.b,))
                sc.add("act", lambda e, pb_=pb_, k2=k2: e.activation(sgB[k2][:], pb_[:, 0:TT], AF.Sigmoid), (pb_.b,), (sgB[k2].b,))
                pya = ps()
                for r in range(4):
                    sc.add("pe", lambda e, r=r, pya=pya, mb=mb: e.matmul(
                        pya[:, 0:TT], Wua[r // 2][:, (r % 2) * 1024 + mb * 128:(r % 2) * 1024 + (mb + 1) * 128], ga_g[:, r, 0:TT],
                        start=(r == 0), stop=(r == 3)), (Wua[r // 2].b, ga_g.b), (pya.b,))
                pyb = ps()
                for r in range(4):
                    sc.add("pe", lambda e, r=r, pyb=pyb, mb=mb: e.matmul(
                        pyb[:, 0:TT], Wub[r // 2][:, (r % 2) * 1024 + mb * 128:(r % 2) * 1024 + (mb + 1) * 128], gb_g[:, r, 0:TT],
                        start=(r == 0), stop=(r == 3)), (Wub[r // 2].b, gb_g.b), (pyb.b,))
                sc.add("dve", lambda e, pya=pya, k2=k2: e.tensor_tensor(sgA[k2][:], pya[:, 0:TT], sgA[k2][:], ALU.mult), (pya.b, sgA[k2].b), (sgA[k2].b,))
                sc.add("dve", lambda e, pyb=pyb, k2=k2: e.tensor_tensor(sgB[k2][:], pyb[:, 0:TT], sgB[k2][:], ALU.mult), (pyb.b, sgB[k2].b), (sgB[k2].b,))
                sc.add("pool", lambda e, k2=k2, mb=mb: e.tensor_tensor(merged[:, mb, :], sgA[k2][:], sgB[k2][:], ALU.add), (sgA[k2].b, sgB[k2].b), (merged.b,))
            for tb in range(TT // 128):
                row0 = st * 512 + tcol + tb * 128
                xw = xown[tb % 2]
                sc.add("sp", lambda e, xw=xw, row0=row0: e.dma_start(out=xw[:, :], in_=xo[row0:row0 + 128, :]), (), (xw.b,), dkey="xo%d" % (tb % 2))
                po2 = [ps(), ps()]
                for nh_ in range(2):
                    for c in range(8):
                        sc.add("pe", lambda e, c=c, nh_=nh_, tb=tb, po2=po2: e.matmul(
                            po2[nh_][:, 0:512], merged[:, c, tb * 128:(tb + 1) * 128],
                            Woc[c // 2][:, (c % 2) * 1024 + nh_ * 512:(c % 2) * 1024 + (nh_ + 1) * 512],
                            start=(c == 0), stop=(c == 7)), (merged.b, Woc[c // 2].b), (po2[nh_].b,))
                for nh_ in range(2):
                    sc.add("dve", lambda e, nh_=nh_, xw=xw, po2=po2: e.tensor_tensor(
                        ybuf[:, nh_ * 512:(nh_ + 1) * 512], po2[nh_][:, 0:512], xw[:, nh_ * 512:(nh_ + 1) * 512], ALU.add),
                        (po2[nh_].b, xw.b), (ybuf.b,))
                sc.add("act", lambda e: e.activation(junk[:, :], ybuf[:, :], AF.Square, accum_out=ssq2[:]), (ybuf.b,), (junk.b, ssq2.b))
                sc.add("act", lambda e: e.activation(ssq2[:], ssq2[:], AF.Ln, bias=EPS, scale=1.0 / 1024.0), (ssq2.b,), (ssq2.b,))
                sc.add("act", lambda e: e.activation(ssq2[:], ssq2[:], AF.Exp, scale=-0.5), (ssq2.b,), (ssq2.b,))
                sc.add("dve", lambda e: e.scalar_tensor_tensor(ybuf[:, :], ybuf[:, :], ssq2[:, 0:1], fnw[:, :], ALU.mult, ALU.mult),
                       (ybuf.b, ssq2.b, fnw.b), (ybuf.b,))
                sc.add("sp", lambda e, row0=row0: e.dma_start(out=out_d[row0:row0 + 128, :], in_=ybuf[:, :]), (ybuf.b,), (Buf(),), dkey="out")
        for half in range(2):
            p2(st, half)
    fin = Buf("fin")
    last_out = [o for o in sc.ops["sp"] if o.dkey == "out"][-1]
    op = sc.add("sp", None, (), ())
    op.deps = [last_out]
    sc.reorder(window=40)
    build_program.stats = (sc.est_total, {e: len(sc.ops[e]) for e in sc.ENGS})
    sc.emit(nc, es)
    es.close()
    return nc


def _consts():
    bf = ml_dtypes.bfloat16
    idx = np.arange(128)
    h = idx // 64
    i = idx % 64
    same = (h[:, None] == h[None, :])
    c = {}
    sw = h * 64 + (i + 32) % 64
    perm = np.zeros((128, 128), np.float32)
    perm[sw, idx] = 1.0
    c["perm"] = perm.astype(bf)
    c["ident"] = np.eye(128, dtype=np.float32).astype(bf)
    c["onesbd"] = same.astype(np.float32).astype(bf)
    c["onesbdf"] = same.astype(np.float32)
    c["ones"] = np.ones((128, 128), np.float32).astype(bf)
    c["lbd"] = (same & (i[:, None] <= i[None, :])).astype(np.float32)
    ua = np.zeros((128, 130), np.float32)
    ua[:, :128] = (same & (i[:, None] > i[None, :])).astype(np.float32)
    ua[:, 128] = 1.0
    c["uaug"] = ua
    c["slneg"] = -(same & (i[:, None] > i[None, :])).astype(np.float32)
    c["li"] = (same & (i[:, None] >= i[None, :])).astype(np.float32)
    j_ = np.arange(128)[:, None]
    q_ = np.arange(128)[None, :]
    prev = (j_ >= q_).astype(np.float32)
    cur = (j_ <= q_).astype(np.float32)
    dm = np.concatenate([prev, prev, cur, cur], axis=1)
    c["dmask"] = dm.astype(bf)
    inv = (10000.0 ** (-np.arange(0, 64, 2, dtype=np.float32) / 64.0)).astype(np.float32)
    ang = (np.arange(S, dtype=np.float32)[:, None] * inv[None, :]).astype(np.float32)
    ang = np.concatenate([ang, ang], axis=-1)
    cos = np.cos(ang).astype(np.float32).T
    sin = np.sin(ang).astype(np.float32).T
    sgn = np.where(np.arange(64) < 32, -1.0, 1.0).astype(np.float32)[:, None]
    c["cosT"] = np.ascontiguousarray(np.concatenate([cos, cos], 0))
    c["sinT"] = np.ascontiguousarray(np.concatenate([sin * sgn, sin * sgn], 0))
    return c


def _chunk(w, nchunk):
    return np.ascontiguousarray(w.reshape(nchunk, 128, -1).transpose(1, 0, 2))


_NC = [None]


def kernel(x, norm_w, w_in, conv_w, a_log, dt_bias, gdn_norm_w, w_up_a, w_up_b, w_out, final_norm_w):
    x = np.asarray(x, np.float32)
    w_in0 = np.asarray(w_in, np.float32)[0]
    conv0 = np.asarray(conv_w, np.float32)[0]
    if _NC[0] is None:
        _NC[0] = build_program()
    nc = _NC[0]
    cst = _consts()
    shared = dict(cst)
    shared["wg"] = _chunk(w_in0[:, 7184:9232], 8)
    shared["wua"] = _chunk(np.asarray(w_up_a, np.float32)[0], 4)
    shared["wub"] = _chunk(np.asarray(w_up_b, np.float32)[0], 4)
    shared["wo"] = _chunk(np.asarray(w_out, np.float32)[0], 8)
    shared["normw"] = np.ascontiguousarray(np.asarray(norm_w, np.float32)[0].reshape(8, 128).T)
    shared["fnw"] = np.ascontiguousarray(np.broadcast_to(np.asarray(final_norm_w, np.float32)[None, :], (128, 1024)))
    shared["gnw"] = np.ascontiguousarray(np.broadcast_to(np.asarray(gdn_norm_w, np.float32)[0][None, :], (128, 64)))
    in_maps = []
    owns = []
    for core in range(8):
        b, c4 = core // 4, core % 4
        h0 = 2 * c4
        xTb = _chunk(np.ascontiguousarray(x[b].T), 8)
        own = np.concatenate([st * ST + c4 * 512 + np.arange(512) for st in range(4)])
        owns.append(own)
        cols = []
        for g in range(3):
            for t in range(3):
                s0 = g * 1536 + t * 512 + h0 * 64
                cols.append(np.arange(s0, s0 + 128))
        cols.append(np.arange(4608 + h0 * 64, 4608 + h0 * 64 + 128))
        for t in range(3):
            s0 = 5120 + t * 512 + h0 * 64
            cols.append(np.arange(s0, s0 + 128))
        cols.append(np.arange(6656 + h0 * 64, 6656 + h0 * 64 + 128))
        w1 = np.zeros((1024, NB1 * 128), np.float32)
        cc = np.concatenate(cols)
        w1[:, :14 * 128] = w_in0[:, cc]
        w1[:, 14 * 128 + 0] = w_in0[:, 7168 + h0]
        w1[:, 14 * 128 + 1] = w_in0[:, 7176 + h0]
        w1[:, 14 * 128 + 2] = w_in0[:, 7168 + h0 + 1]
        w1[:, 14 * 128 + 3] = w_in0[:, 7176 + h0 + 1]
        convw = np.zeros((128, 12), np.float32)
        for t in range(3):
            s0 = t * 512 + h0 * 64
            convw[:, t * 4:(t + 1) * 4] = conv0[:, s0:s0 + 128].T
        hh = np.arange(128) // 64
        m = dict(shared)
        m["xT"] = xTb
        m["xTo"] = np.ascontiguousarray(xTb[:, :, own])
        m["xo"] = np.ascontiguousarray(x[b][own])
        m["w1"] = _chunk(w1, 8)
        m["convw"] = convw
        m["alog"] = np.asarray(a_log, np.float32)[0][h0 + hh][:, None].copy()
        m["dtb"] = np.asarray(dt_bias, np.float32)[0][h0 + hh][:, None].copy()
        in_maps.append(m)
    res = run_bass_kernel_spmd(nc, in_maps, core_ids=list(range(8)))
    out = np.zeros((2, S, 1024), np.float32)
    for core in range(8):
        out[core // 4, owns[core]] = np.asarray(res.results[core]["out"], np.float32)
    return out
```

```python
from contextlib import ExitStack
import numpy as np
import ml_dtypes
import concourse.bass as bass
import concourse.mybir as mybir
from concourse.bass_utils import run_bass_kernel_spmd

F32 = mybir.dt.float32
BF16 = mybir.dt.bfloat16
ALU = mybir.AluOpType
AF = mybir.ActivationFunctionType
AX = mybir.AxisListType

S = 8192
TT = 256
NTT = S // TT
ST = 2048
TPS = ST // TT
NCH = TT // 64
DIL = (1, 4, 16)
EPS = 1e-6
NB1 = 15
(ZA, GQ, GK, GV, ZB, BA) = (9, 10, 11, 12, 13, 14)


class Buf:
    __slots__ = ("n", "psum")

    def __init__(self, n="", psum=False):
        self.n = n
        self.psum = psum


class Op:
    __slots__ = ("eng", "fn", "deps", "sig", "cnt", "dkey", "dcnt", "idx", "alldeps", "dur", "st", "fin", "nun", "users", "bl")


class _FakeEng:
    def __init__(self):
        self.rec = None

    def __getattr__(self, name):
        def f(*a, **k):
            self.rec = (name, a, k)
            return self
        return f


def _free_elems(ap):
    try:
        sh = list(ap.shape)
        n = 1
        for v in sh[1:]:
            n *= int(v)
        return n
    except Exception:
        return 0


def _estimate(op):
    if op.fn is None:
        return 0.0
    if op.dkey is not None:
        return 0.1
    fe = _FakeEng()
    try:
        op.fn(fe)
    except Exception:
        return 0.5
    name, a, k = fe.rec if fe.rec else ("", (), {})
    args = list(a) + list(k.values())
    n = max([_free_elems(x) for x in args if hasattr(x, "shape")] + [1])
    if op.eng == "pe":
        if name == "matmul":
            rhs = a[2] if len(a) > 2 else k.get("rhs")
            n = _free_elems(rhs)
            f32 = str(getattr(rhs, "dtype", "")).endswith("float32")
            small = 0.0
            try:
                if int(a[1].shape[0]) < 128 or _free_elems(a[1]) < 128:
                    small = 0.08
            except Exception:
                pass
            return 0.05 + small + n * (0.0016 if f32 else 0.0004)
        return 0.11
    if op.eng == "act":
        return 0.20 + n * 0.0008
    if op.eng == "dve":
        return 0.20 + n * 0.0010
    if op.eng == "pool":
        return 0.35 + n * 0.0015
    return 0.1


class Sched:
    ENGS = ("pe", "act", "dve", "pool", "sp")

    def __init__(self):
        self.ops = {e: [] for e in self.ENGS}
        self.last_w = {}
        self.readers = {}
        self.dma_n = {}
        self.nops = 0
        self.last_key = {}

    def add(self, eng, fn, r=(), w=(), dkey=None):
        op = Op()
        op.eng, op.fn, op.sig, op.cnt, op.dkey, op.dcnt = eng, fn, False, 0, dkey, 0
        op.idx = self.nops
        self.nops += 1
        if dkey is not None:
            self.dma_n[dkey] = self.dma_n.get(dkey, 0) + 1
            op.dcnt = self.dma_n[dkey]
        w = tuple(w) + tuple(b for b in r if b.psum and b not in w)
        deps = {}
        for b in r:
            lw = self.last_w.get(b)
            if lw is not None:
                deps[id(lw)] = lw
        for b in w:
            lw = self.last_w.get(b)
            if lw is not None:
                deps[id(lw)] = lw
            for o in self.readers.get(b, {}).values():
                deps[id(o)] = o
        op.alldeps = list(deps.values())
        if dkey is not None:
            lk = self.last_key.get(dkey)
            if lk is not None:
                op.alldeps.append(lk)
            self.last_key[dkey] = op
        op.deps = [d for d in deps.values()
                   if not (d.eng == "pe" and eng == "pe" and d.dkey is None and dkey is None)]
        for d in op.deps:
            d.sig = True
        rk = eng if dkey is None else ("dma", dkey)
        for b in r:
            self.readers.setdefault(b, {})[rk] = op
        for b in w:
            self.last_w[b] = op
            self.readers[b] = {}
        self.ops[eng].append(op)
        return op

    def link(self, new, old):
        lw = self.last_w.get(old)
        if lw is not None:
            self.last_w[new] = lw
        self.readers[new] = dict(self.readers.get(old, {}))

    def barrier(self):
        lasts = [self.ops[e][-1] for e in self.ENGS if self.ops[e]]
        b = Buf("barrier")
        for o in lasts:
            self.last_w.pop(b, None)
        for e in self.ENGS:
            op = self.add(e, None, (), ())
            op.deps = [o for o in lasts]
            op.alldeps = [o for o in lasts]
            for o in lasts:
                o.sig = True

    def reorder(self, window=40, lat=0.12, dma_lat=2.5, use_bl=True, bl_engs=("pe",)):
        allops = []
        for e in self.ENGS:
            allops.extend(self.ops[e])
        for op in allops:
            op.dur = _estimate(op)
            op.st = op.fin = None
            op.users = []
        for op in allops:
            op.nun = 0
        for e in self.ENGS:
            prev = None
            fence = None
            for op in self.ops[e]:
                if prev is not None and op.fn is None:
                    op.alldeps = list(op.alldeps) + [prev]
                if fence is not None:
                    op.alldeps = list(op.alldeps) + [fence]
                if op.fn is None:
                    fence = op
                prev = op
        for op in allops:
            seen = set()
            dd = []
            for d in op.alldeps:
                if id(d) not in seen:
                    seen.add(id(d))
                    dd.append(d)
            op.alldeps = dd
            op.nun = len(dd)
            for d in dd:
                d.users.append(op)
        eff = lambda o: ((100.0 if o.dkey.startswith('cc') else (8.0 if o.dkey.startswith('go') else dma_lat)) if o.dkey is not None else o.dur)
        for op in sorted(allops, key=lambda o: -o.idx):
            b = 0.0
            for u in op.users:
                if u.bl + lat > b:
                    b = u.bl + lat
            op.bl = b + eff(op)
        pend = {e: list(self.ops[e]) for e in self.ENGS}
        head = {e: 0 for e in self.ENGS}
        tfree = {e: 0.0 for e in self.ENGS}
        order = {e: [] for e in self.ENGS}
        remaining = len(allops)
        while remaining:
            best = None
            for e in self.ENGS:
                lst = pend[e]
                i = head[e]
                cnt = 0
                tf = tfree[e]
                cand = None
                while i < len(lst) and cnt < window:
                    op = lst[i]
                    i += 1
                    if op is None:
                        continue
                    cnt += 1
                    if op.nun:
                        continue
                    rt = tf
                    for d in op.alldeps:
                        v = d.fin + lat
                        if v > rt:
                            rt = v
                    if use_bl and e in bl_engs:
                        key = (rt if rt > tf + 0.02 else tf, -op.bl, op.idx)
                    else:
                        key = (rt, op.idx, 0)
                    if cand is None or key < cand[0]:
                        cand = (key, e, i - 1, op, rt)
                if cand is not None and (best is None or cand[0] < best[0]):
                    best = cand
            _, e, pos, op, rt = best
            rt = max(rt, tfree[e])
            op.st = rt
            busy = op.dur
            op.fin = rt + ((100.0 if op.dkey.startswith('cc') else (8.0 if op.dkey.startswith('go') else dma_lat)) if op.dkey is not None else busy)
            tfree[e] = rt + busy
            pend[e][pos] = None
            while head[e] < len(pend[e]) and pend[e][head[e]] is None:
                head[e] += 1
            order[e].append(op)
            for u in op.users:
                u.nun -= 1
            remaining -= 1
        for e in self.ENGS:
            self.ops[e] = order[e]
        self.est_total = max(tfree.values())

    def emit(self, nc, es):
        engobj = {"pe": nc.tensor, "act": nc.scalar, "dve": nc.vector, "pool": nc.gpsimd, "sp": nc.sync}
        sems = {e: es.enter_context(nc.semaphore("s_" + e)) for e in self.ENGS}
        dsems = {k: es.enter_context(nc.semaphore("d_%s" % (k,))) for k in self.dma_n}
        for e in self.ENGS:
            c = 0
            for op in self.ops[e]:
                if op.dkey is None and op.sig:
                    c += 1
                op.cnt = c
        block = es.enter_context(nc.Block())

        def run(e, eng):
            waited = {}
            for op in self.ops[e]:
                for d in op.deps:
                    if d.dkey is not None and d.dkey.startswith("cc"):
                        sem, val, key = dsems[d.dkey], 1, ("d", d.dkey)
                    elif d.dkey is not None:
                        sem, val, key = dsems[d.dkey], 16 * d.dcnt, ("d", d.dkey)
                    else:
                        sem, val, key = sems[d.eng], d.cnt, ("c", d.eng)
                    if waited.get(key, 0) >= val:
                        continue
                    waited[key] = val
                    eng.wait_ge(sem, val)
                if op.fn is None:
                    continue
                ins = op.fn(eng)
                if op.dkey is not None and op.dkey.startswith("cc"):
                    ins.then_inc(dsems[op.dkey])
                elif op.dkey is not None:
                    ins.then_inc(dsems[op.dkey], 16)
                elif op.sig:
                    ins.then_inc(sems[e], 1)

        @block.tensor
        def _(eng):
            run("pe", eng)

        @block.scalar
        def _(eng):
            run("act", eng)

        @block.vector
        def _(eng):
            run("dve", eng)

        @block.gpsimd
        def _(eng):
            run("pool", eng)

        @block.sync
        def _(eng):
            run("sp", eng)


class T:
    def __init__(self, nc, es, name, shape, dt, psum=False):
        self.t = es.enter_context((nc.psum_tensor if psum else nc.sbuf_tensor)("t_" + name, list(shape), dt))
        self.b = Buf(name, psum)

    def __getitem__(self, k):
        return self.t[k]


def build_program():
    nc = bass.Bass("TRN2", target_bir_lowering=False)
    es = ExitStack()
    sc = Sched()

    def dram_in(name, shape, dt=F32):
        return nc.dram_tensor(name, list(shape), dt, kind="ExternalInput").ap()

    xT = dram_in("xT", [128, 8, S])
    xTo = dram_in("xTo", [128, 8, ST])
    xo = dram_in("xo", [ST, 1024])
    w1 = dram_in("w1", [128, 8, NB1 * 128])
    wg = dram_in("wg", [128, 8, 2048])
    wua = dram_in("wua", [128, 4, 1024])
    wub = dram_in("wub", [128, 4, 1024])
    wo = dram_in("wo", [128, 8, 1024])
    normw_d = dram_in("normw", [128, 8])
    fnw_d = dram_in("fnw", [128, 1024])
    convw_d = dram_in("convw", [128, 12])
    alog_d = dram_in("alog", [128, 1])
    dtb_d = dram_in("dtb", [128, 1])
    gnw_d = dram_in("gnw", [128, 64])
    cos_d = dram_in("cosT", [128, S])
    sin_d = dram_in("sinT", [128, S])
    perm_d = dram_in("perm", [128, 128], BF16)
    ident_d = dram_in("ident", [128, 128], BF16)
    onesbd_d = dram_in("onesbd", [128, 128], BF16)
    onesbdf_d = dram_in("onesbdf", [128, 128])
    ones_d = dram_in("ones", [128, 128], BF16)
    lbd_d = dram_in("lbd", [128, 128])
    uaug_d = dram_in("uaug", [128, 130])
    slneg_d = dram_in("slneg", [128, 128])
    li_d = dram_in("li", [128, 128])
    dmask_d = dram_in("dmask", [128, 512], BF16)
    out_d = nc.dram_tensor("out", [ST, 1024], F32, kind="ExternalOutput").ap()
    bin0 = nc.dram_tensor("bin0", [256, ST], BF16)
    bout0 = nc.dram_tensor("bout0", [1024, ST], BF16)
    bin_ = [bin0] * 4
    bout = [bout0] * 4
    bb_, bo_ = Buf("bin"), Buf("bout")
    bin_b = [bb_] * 4
    bout_b = [bo_] * 4
    gown = [nc.dram_tensor("gown%d" % i, [1024, 512], BF16) for i in range(4)]
    gown_b = [Buf("gown%d" % i) for i in range(4)]
    cid_cache = {}

    def cidx(eng):
        if "v" not in cid_cache:
            cid_cache["v"] = eng.snap((eng.partition_id() % 4) * 512, min_val=0, max_val=1536)
        return cid_cache["v"]

    def copy_out(st):
        sc.add("sp", lambda e, st=st: e.dma_start(out=gown[st][:, :], in_=bout0[:, bass.ds(cidx(e), 512)]),
               (bo_,), (gown_b[st],), dkey="go%d" % st)

    def sb(name, shape, dt=F32):
        return T(nc, es, name, shape, dt)

    banks = [T(nc, es, "ps%d" % i, [128, 512], F32, psum=True) for i in range(8)]
    POOLS = {"ALL": [0, 1, 2, 3, 4, 5, 6, 7], "A": [0, 1, 2, 7], "B": [3, 4, 5, 6]}
    pool_i = {"ALL": 0, "A": 0, "B": 0}
    cur_pool = ["ALL"]

    def ps():
        k = cur_pool[0]
        lst = POOLS[k]
        b = banks[lst[pool_i[k] % len(lst)]]
        pool_i[k] += 1
        return b

    def step(gen, pool):
        cur_pool[0] = pool
        try:
            next(gen)
            return True
        except StopIteration:
            return False
        finally:
            cur_pool[0] = "ALL"

    def interleave(g1, g2):
        a1, a2 = g1 is not None, g2 is not None
        while a1 or a2:
            if a1:
                a1 = step(g1, "A")
            if a2:
                a2 = step(g2, "B")

    kid = [0]

    def load_const(dst, src, eng="sp"):
        kid[0] += 1
        sc.add(eng, lambda e, d=dst, s=src: e.dma_start(out=d[:], in_=s), (), (dst.b,), dkey="c%d" % kid[0])

    normw = sb("normw", [128, 8]); load_const(normw, normw_d)
    convw = sb("convw", [128, 12]); load_const(convw, convw_d)
    alog = sb("alog", [128, 1]); load_const(alog, alog_d)
    dtb = sb("dtb", [128, 1]); load_const(dtb, dtb_d)
    gnw = sb("gnw", [128, 64]); load_const(gnw, gnw_d)
    perm = sb("perm", [128, 128], BF16); load_const(perm, perm_d)
    ident = sb("ident", [128, 128], BF16); load_const(ident, ident_d)
    onesbd = sb("onesbd", [128, 128], BF16); load_const(onesbd, onesbd_d)
    onesbdf = sb("onesbdf", [128, 128]); load_const(onesbdf, onesbdf_d)
    ones = sb("ones", [128, 128], BF16); load_const(ones, ones_d)
    lbd = sb("lbd", [128, 128]); load_const(lbd, lbd_d)
    uaug = sb("uaug", [128, 130]); load_const(uaug, uaug_d)
    slneg = sb("slneg", [128, 128]); load_const(slneg, slneg_d)
    li = sb("li", [128, 128]); load_const(li, li_d)
    dmask = sb("dmask", [128, 512], BF16); load_const(dmask, dmask_d)
    nalog = sb("nalog", [128, 1])
    sc.add("act", lambda e: e.activation(nalog[:], alog[:], AF.Exp), (alog.b,), (nalog.b,))
    sc.add("dve", lambda e: e.tensor_scalar(nalog[:], nalog[:], -1.0, None, ALU.mult), (nalog.b,), (nalog.b,))

    XR = 12
    xs = [sb("xs%d" % i, [128, TT]) for i in range(XR)]
    xs_i = [0]

    def ring():
        b = xs[xs_i[0] % XR]
        k = "x%d" % (xs_i[0] % XR)
        xs_i[0] += 1
        return b, k

    cast_i = [0]

    def cast_to(dst_ap, dst_b, src_ap, src_b):
        cast_i[0] += 1
        if cast_i[0] % 2 == 0:
            sc.add("dve", lambda e: e.tensor_copy(dst_ap, src_ap), (src_b,), (dst_b,))
        else:
            sc.add("act", lambda e: e.copy(dst_ap, src_ap), (src_b,), (dst_b,))

    W1 = sb("W1", [128, 8, (NB1 - 1) * 128], BF16)
    W1b = [Buf("W1b%d" % i) for i in range(NB1 - 1)]
    for j in range((NB1 - 1) * 128 // TT):
        for c in range(8):
            stb, k = ring()
            sc.add("sp", lambda e, stb=stb, j=j, c=c: e.dma_start(out=stb[:, :], in_=w1[:, c, j * TT:(j + 1) * TT]),
                   (), (stb.b,), dkey=k)
            for bi in (2 * j, 2 * j + 1):
                cast_i[0] += (bi == 2 * j)
                dst_ap = W1[:, c, bi * 128:(bi + 1) * 128]
                src_ap = stb[:, (bi - 2 * j) * 128:(bi - 2 * j + 1) * 128]
                if cast_i[0] % 2 == 0:
                    sc.add("dve", lambda e, dst_ap=dst_ap, src_ap=src_ap: e.tensor_copy(dst_ap, src_ap), (stb.b,), (W1b[bi],))
                else:
                    sc.add("act", lambda e, dst_ap=dst_ap, src_ap=src_ap: e.copy(dst_ap, src_ap), (stb.b,), (W1b[bi],))
    Wba = sb("Wba", [128, 8, 4], BF16)
    stb, k = ring()
    stv = stb[:, 0:32].rearrange("p (a b) -> p a b", a=8)
    sc.add("sp", lambda e, stv=stv: e.dma_start(out=stv, in_=w1[:, :, BA * 128:BA * 128 + 4]), (), (stb.b,), dkey=k)
    cast_to(Wba[:], Wba.b, stv, stb.b)

    sqb = [sb("sq%d" % i, [128, TT], BF16) for i in range(3)]
    rstd = sb("rstd", [128, TT])
    hT2 = [sb("hT%d" % i, [128, 8, TT], BF16) for i in range(2)]
    cur_hT = [hT2[0]]
    cosb = [sb("cos%d" % i, [128, TT]) for i in range(2)]
    sinb = [sb("sin%d" % i, [128, TT]) for i in range(2)]
    Kt = [[sb("Kt%d_%d" % (g, s), [128, ST], BF16) for s in range(2)] for g in range(3)]
    Qt = [sb("Qt%d" % g, [128, ST], BF16) for g in range(3)]
    Vt = [sb("Vt%d" % g, [128, ST], BF16) for g in range(3)]
    Vs = [[sb("Vs%d_%d" % (g, s), [128, 16, 128], BF16) for s in range(2)] for g in range(3)]
    ndacc = sb("ndacc", [128, 2, ST])
    zaS = sb("zaS", [128, ST], BF16)
    gaT = sb("gaT", [128, ST], BF16)
    gbT = sb("gbT", [128, ST], BF16)
    qraw = [sb("qraw%d" % i, [128, TT], BF16) for i in range(2)]
    rt1 = [sb("rt1_%d" % i, [128, TT]) for i in range(2)]
    rt2 = [sb("rt2_%d" % i, [128, TT]) for i in range(2)]
    Pt = [sb("Pt%d" % i, [128, 512], BF16) for i in range(3)]
    Qx = [sb("Qx%d" % i, [128, 256], BF16) for i in range(2)]
    xpad = [sb("xpad%d" % i, [128, TT + 3], BF16) for i in range(3)]
    cacc = [sb("cacc%d" % i, [128, TT]) for i in range(2)]
    diagw = sb("diagw", [128, 12, 128], BF16)
    gsq = sb("gsq", [128, TT], BF16)
    grn = sb("grn", [128, TT])
    Qbd2 = [sb("Qbd%d" % i, [128, NCH, 128], BF16) for i in range(2)]
    Kbd2 = [sb("Kbd%d" % i, [128, NCH, 128], BF16) for i in range(2)]
    Vbd2 = [sb("Vbd%d" % i, [128, NCH, 128], BF16) for i in range(2)]
    zbS2 = [sb("zbS%d" % i, [128, TT], BF16) for i in range(2)]
    batok = sb("batok", [128, TT // 128, 4])
    bastk2 = [sb("bastk%d" % i, [128, NCH, 2]) for i in range(2)]
    beta2 = [sb("beta%d" % i, [128, NCH]) for i in range(2)]
    gg2 = [sb("gg%d" % i, [128, NCH]) for i in range(2)]
    Rbd = sb("Rbd", [128, NCH, 128])
    E1 = sb("E1", [128, NCH, 128])
    MM2 = sb("MM2", [128, NCH, 128])
    expG = sb("expG", [128, NCH])
    glast = sb("glast", [128, NCH])
    decf = sb("decf", [128, NCH])
    kdsc = sb("kdsc", [128, NCH])
    bsc = sb("bsc", [128, NCH])
    Cb = [sb("Cb%d" % i, [128, NCH, 128], BF16) for i in range(2)]
    Bb = [sb("Bb%d" % i, [128, NCH, 128], BF16) for i in range(2)]
    Xb = [sb("Xb%d" % i, [128, NCH, 128], BF16) for i in range(2)]
    ufin = sb("ufin", [128, NCH, 64])
    Wbd2 = sb("Wbd2", [128, NCH, 128], BF16)
    Wt = sb("Wt", [128, NCH, 128], BF16)
    attn = sb("attn", [128, NCH, 128], BF16)
    attnT = sb("attnT", [128, NCH, 128], BF16)
    Kdec = sb("Kdec", [128, NCH, 128], BF16)
    Sst = sb("Sst", [128, 64])
    Sbf = sb("Sbf", [128, 64], BF16)
    vnew = sb("vnew", [128, 64], BF16)
    oB = sb("oB", [128, 64])
    otile = sb("otile", [128, NCH, 64])
    oss = sb("oss", [128, NCH])
    onbd = sb("onbd", [128, NCH, 128], BF16)

    for k_ in range(12):
        sc.add("dve", lambda e, k_=k_: e.tensor_scalar(diagw[:, k_, :], ident[:], convw[:, k_:k_ + 1], None, ALU.mult),
               (ident.b, convw.b), (diagw.b,))
    for t_ in Qx:
        sc.add("pool", lambda e, t_=t_: e.memset(t_[:], 0.0), (), (t_.b,))
    for t_, eng in ((Sst, "dve"), (Sbf, "dve"), (Wbd2, "pool"), (onbd, "pool"), (Qbd2[0], "pool"), (Kbd2[0], "pool"),
                    (Vbd2[0], "pool"), (Qbd2[1], "pool"), (Kbd2[1], "pool"), (Vbd2[1], "pool"), (xpad[0], "dve"), (xpad[1], "dve"), (xpad[2], "dve")):
        sc.add(eng, lambda e, t_=t_: e.memset(t_[:], 0.0), (), (t_.b,))

    def rmsnorm_tile(src_dram, col0, hdst):
        bufs = []
        for c in range(8):
            xb = xs[xs_i[0] % XR]
            k = "x%d" % (xs_i[0] % XR)
            xs_i[0] += 1
            sc.add("sp", lambda e, xb=xb, c=c: e.dma_start(out=xb[:], in_=src_dram[:, c, col0:col0 + TT]),
                   (), (xb.b,), dkey=k)
            bufs.append(xb)
        pss = ps()
        for c in range(8):
            sq = sqb[c % 3]
            sc.add("act", lambda e, sq=sq, xb=bufs[c]: e.activation(sq[:], xb[:], AF.Square), (bufs[c].b,), (sq.b,))
            sc.add("pe", lambda e, sq=sq, c=c: e.matmul(pss[:, 0:TT], ones[:], sq[:], start=(c == 0), stop=(c == 7)),
                   (sq.b, ones.b), (pss.b,))
        sc.add("act", lambda e: e.activation(rstd[:], pss[:, 0:TT], AF.Ln, bias=EPS, scale=1.0 / 1024.0),
               (pss.b,), (rstd.b,))
        sc.add("act", lambda e: e.activation(rstd[:], rstd[:], AF.Exp, scale=-0.5), (rstd.b,), (rstd.b,))
        for c in range(8):
            sc.add("dve", lambda e, c=c, xb=bufs[c]: e.scalar_tensor_tensor(
                hdst[:, c, :], xb[:], normw[:, c:c + 1], rstd[:], ALU.mult, ALU.mult),
                (bufs[c].b, rstd.b, normw.b), (hdst.b,))

    def proj_fm(blk):
        p = ps()
        hT = cur_hT[0]
        for c in range(8):
            sc.add("pe", lambda e, c=c, p=p, hT=hT: e.matmul(p[:, 0:TT], W1[:, c, blk * 128:(blk + 1) * 128], hT[:, c, :],
                                                            start=(c == 0), stop=(c == 7)), (W1b[blk], hT.b), (p.b,))
        return p

    stmp = [sb("stmp%d" % i, [128, TT]) for i in range(2)]
    stmp_i = [0]

    def sigmoid_to(src_ap, src_b, shape_cols=TT):
        t = stmp[stmp_i[0] % 2]
        stmp_i[0] += 1
        tv = t[:, 0:shape_cols]
        sc.add("act", lambda e: e.activation(tv, src_ap, AF.Exp, scale=-1.0), (src_b,), (t.b,))
        sc.add("act", lambda e: e.activation(tv, tv, AF.Ln, bias=1.0), (t.b,), (t.b,))
        sc.add("act", lambda e: e.activation(tv, tv, AF.Exp, scale=-1.0), (t.b,), (t.b,))
        return t

    def perm_view(t_ap_tile, g, tt):
        d = DIL[g]
        n = TT // d
        v = t_ap_tile[:, :].rearrange("p (r m) -> p r m", r=d)
        return v[:, :, tt * n:(tt + 1) * n]

    def src_view(ap, g):
        d = DIL[g]
        return ap.rearrange("p (m r) -> p r m", r=d)

    rope_i = [0]

    def prenorm(ti):
        col0 = ti * TT
        cb, sb_ = cosb[ti % 2], sinb[ti % 2]
        sc.add("sp", lambda e: e.dma_start(out=cb[:], in_=cos_d[:, col0:col0 + TT]), (), (cb.b,), dkey="cos%d" % (ti % 2))
        sc.add("sp", lambda e: e.dma_start(out=sb_[:], in_=sin_d[:, col0:col0 + TT]), (), (sb_.b,), dkey="sin%d" % (ti % 2))
        rmsnorm_tile(xT, col0, hT2[ti % 2])

    prenorm(0)
    for st in range(4):
        slot = st % 2
        def tileA(st, slot, tt):
            ti = st * TPS + tt
            col0 = ti * TT
            par = ti % 2
            Qbd, Kbd, Vbd, zbS, bastk, beta, gg = Qbd2[par], Kbd2[par], Vbd2[par], zbS2[par], bastk2[par], beta2[par], gg2[par]
            cb, sb_ = cosb[ti % 2], sinb[ti % 2]
            hT = hT2[ti % 2]
            cur_hT[0] = hT

            def dsa_qk(g, qk):
                p = proj_fm(3 * g + qk)
                qr = qraw[rope_i[0] % 2]
                t1 = rt1[rope_i[0] % 2]
                t2 = rt2[rope_i[0] % 2]
                rope_i[0] += 1
                sc.add("act", lambda e, p=p, qr=qr: e.copy(qr[:], p[:, 0:TT]), (p.b,), (qr.b,))
                p2 = ps()
                sc.add("pe", lambda e, p2=p2, qr=qr: e.matmul(p2[:, 0:TT], perm[:], qr[:], start=True, stop=True),
                       (perm.b, qr.b), (p2.b,))
                sc.add("dve", lambda e, p=p, t1=t1, cb=cb: e.tensor_tensor(t1[:], p[:, 0:TT], cb[:], ALU.mult),
                       (p.b, cb.b), (t1.b,))
                sc.add("dve", lambda e, p2=p2, t2=t2, sb_=sb_: e.tensor_tensor(t2[:], p2[:, 0:TT], sb_[:], ALU.mult),
                       (p2.b, sb_.b), (t2.b,))
                dst = Qt[g] if qk == 0 else Kt[g][slot]
                sc.add("pool", lambda e, dst=dst, t1=t1, t2=t2, g=g, tt=tt: e.tensor_tensor(
                    perm_view(dst, g, tt), src_view(t1[:, :], g), src_view(t2[:, :], g), ALU.add),
                    (t1.b, t2.b), (dst.b,))

            def dsa_v(g):
                p = proj_fm(3 * g + 2)
                sc.add("act", lambda e, p=p, g=g, tt=tt: e.copy(perm_view(Vt[g], g, tt), src_view(p[:, 0:TT], g)),
                       (p.b,), (Vt[g].b,))

            def z_a():
                p = proj_fm(ZA)
                sg = sigmoid_to(p[:, 0:TT], p.b)
                sc.add("dve", lambda e, p=p, tt=tt, sg=sg: e.tensor_tensor(zaS[:, tt * TT:(tt + 1) * TT], p[:, 0:TT], sg[:], ALU.mult),
                       (p.b, sg.b), (zaS.b,))

            def z_b():
                p = proj_fm(ZB)
                sg = sigmoid_to(p[:, 0:TT], p.b)
                sc.add("dve", lambda e, p=p, sg=sg: e.tensor_tensor(zbS[:], p[:, 0:TT], sg[:], ALU.mult), (p.b, sg.b), (zbS.b,))

            def gdn_in(j):
                blk = (GQ, GK, GV)[j]
                p = proj_fm(blk)
                xp = xpad[j]
                ca = cacc[j % 2]
                sc.add("act", lambda e, xp=xp: e.copy(xp[:, 0:3], xp[:, TT:TT + 3]), (xp.b,), (xp.b,))
                sc.add("act", lambda e, xp=xp, p=p: e.copy(xp[:, 3:TT + 3], p[:, 0:TT]), (p.b, xp.b), (xp.b,))
                pc = ps()
                for tap in range(4):
                    sc.add("pe", lambda e, xp=xp, pc=pc, j=j, tap=tap: e.matmul(
                        pc[:, 0:TT], diagw[:, j * 4 + tap, :], xp[:, tap:tap + TT], start=(tap == 0), stop=(tap == 3)),
                        (diagw.b, xp.b), (pc.b,))
                if j < 2:
                    sg = sigmoid_to(pc[:, 0:TT], pc.b)
                    sc.add("dve", lambda e, ca=ca, pc=pc, sg=sg: e.tensor_tensor(ca[:], pc[:, 0:TT], sg[:], ALU.mult), (pc.b, sg.b), (ca.b,))
                    sc.add("act", lambda e, ca=ca: e.activation(gsq[:], ca[:], AF.Square), (ca.b,), (gsq.b,))
                    p3 = ps()
                    sc.add("pe", lambda e, p3=p3: e.matmul(p3[:, 0:TT], onesbd[:], gsq[:], start=True, stop=True),
                           (onesbd.b, gsq.b), (p3.b,))
                    scl = 64.0 if j == 0 else 1.0
                    sc.add("act", lambda e, p3=p3, scl=scl: e.activation(grn[:], p3[:, 0:TT], AF.Ln, bias=EPS * scl, scale=scl),
                           (p3.b,), (grn.b,))
                    sc.add("act", lambda e: e.activation(grn[:], grn[:], AF.Exp, scale=-0.5), (grn.b,), (grn.b,))
                    dstb = Qbd if j == 0 else Kbd
                    for h in range(2):
                        hs = slice(h * 64, (h + 1) * 64)
                        sc.add("dve", lambda e, dstb=dstb, hs=hs, h=h, ca=ca: e.tensor_tensor(
                            dstb[hs, :, h * 64:(h + 1) * 64], ca[hs, :].rearrange("p (c i) -> p c i", i=64),
                            grn[hs, :].rearrange("p (c i) -> p c i", i=64), ALU.mult), (ca.b, grn.b), (dstb.b,))
                else:
                    sg = sigmoid_to(pc[:, 0:TT], pc.b)
                    for h in range(2):
                        hs = slice(h * 64, (h + 1) * 64)
                        sc.add("dve", lambda e, hs=hs, h=h, pc=pc, sg=sg: e.tensor_tensor(
                            Vbd[hs, :, h * 64:(h + 1) * 64], pc[hs, 0:TT].rearrange("p (c i) -> p c i", i=64),
                            sg[hs, :].rearrange("p (c i) -> p c i", i=64), ALU.mult), (pc.b, sg.b), (Vbd.b,))

            def beta_part():
                pb = ps()
                for tb in range(TT // 128):
                    for c in range(8):
                        sc.add("pe", lambda e, tb=tb, c=c, pb=pb: e.matmul(
                            pb[:, tb * 4:(tb + 1) * 4], hT[:, c, tb * 128:(tb + 1) * 128], Wba[:, c, :],
                            start=(c == 0), stop=(c == 7)), (hT.b, Wba.b), (pb.b,))
                sc.add("dve", lambda e, pb=pb: e.tensor_copy(batok[:], pb[:, 0:(TT // 128) * 4].rearrange("p (b k) -> p b k", k=4)),
                       (pb.b,), (batok.b,))
                for h in range(2):
                    for cp in range(2):
                        sc.add("sp", lambda e, h=h, cp=cp: e.dma_start(
                            out=bastk[h * 64:(h + 1) * 64, cp:NCH:2, :], in_=batok[cp * 64:(cp + 1) * 64, :, 2 * h:2 * h + 2],
                            allow_slow_non_contiguous=True), (batok.b,), (bastk.b,), dkey="ba%d" % par)
                sc.add("act", lambda e: e.activation(beta[:], bastk[:, :, 0], AF.Exp, scale=-1.0), (bastk.b,), (beta.b,))
                sc.add("act", lambda e: e.activation(beta[:], beta[:], AF.Ln, bias=1.0), (beta.b,), (beta.b,))
                sc.add("act", lambda e: e.activation(beta[:], beta[:], AF.Exp, scale=-1.0), (beta.b,), (beta.b,))
                sc.add("act", lambda e: e.activation(gg[:], bastk[:, :, 1], AF.Exp, bias=dtb[:]), (bastk.b, dtb.b), (gg.b,))
                sc.add("act", lambda e: e.activation(gg[:], gg[:], AF.Ln, bias=1.0), (gg.b,), (gg.b,))
                sc.add("dve", lambda e: e.tensor_scalar(gg[:], gg[:], nalog[:, 0:1], None, ALU.mult), (gg.b, nalog.b), (gg.b,))

            dsa_qk(0, 0); z_a(); yield
            dsa_qk(0, 1); z_b(); yield
            dsa_v(0); gdn_in(0); yield
            if ti + 1 < NTT:
                prenorm(ti + 1)
            yield
            dsa_qk(1, 0); yield
            dsa_qk(1, 1); gdn_in(1); yield
            dsa_v(1); yield
            dsa_qk(2, 0); gdn_in(2); yield
            dsa_qk(2, 1); beta_part(); yield
            dsa_v(2); yield

        def tileB(st, slot, tt):
            ti = st * TPS + tt
            par = ti % 2
            Qbd, Kbd, Vbd, zbS, bastk, beta, gg = Qbd2[par], Kbd2[par], Vbd2[par], zbS2[par], bastk2[par], beta2[par], gg2[par]
            pgl = ps()
            sc.add("pe", lambda e, pgl=pgl: e.matmul(pgl[:, 0:NCH], onesbdf[:], gg[:], start=True, stop=True),
                   (onesbdf.b, gg.b), (pgl.b,))
            sc.add("pe", lambda e, pgl=pgl: e.matmul(pgl[:, NCH:2 * NCH], lbd[:], gg[:], start=True, stop=True),
                   (lbd.b, gg.b), (pgl.b,))
            sc.add("act", lambda e, pgl=pgl: e.copy(glast[:], pgl[:, 0:NCH]), (pgl.b,), (glast.b,))
            sc.add("act", lambda e, pgl=pgl: e.activation(decf[:], pgl[:, 0:NCH], AF.Exp), (pgl.b,), (decf.b,))
            sc.add("act", lambda e, pgl=pgl: e.activation(expG[:], pgl[:, NCH:2 * NCH], AF.Exp), (pgl.b,), (expG.b,))
            sc.add("dve", lambda e, pgl=pgl: e.tensor_tensor(kdsc[:], glast[:], pgl[:, NCH:2 * NCH], ALU.subtract),
                   (glast.b, pgl.b), (kdsc.b,))
            sc.add("act", lambda e: e.activation(kdsc[:], kdsc[:], AF.Exp), (kdsc.b,), (kdsc.b,))
            sc.add("dve", lambda e: e.tensor_tensor(bsc[:], beta[:], expG[:], ALU.mult), (beta.b, expG.b), (bsc.b,))
            sc.add("dve", lambda e: e.tensor_tensor(
                Rbd[:], uaug[:, 0:128].unsqueeze(1).broadcast_to([128, NCH, 128]),
                gg[:, :].unsqueeze(2).broadcast_to([128, NCH, 128]), ALU.mult), (uaug.b, gg.b), (Rbd.b,))
            pD = ps()
            for c in range(NCH):
                sc.add("pe", lambda e, c=c, pD=pD: e.matmul(pD[:, c * 128:(c + 1) * 128], lbd[:], Rbd[:, c, :], start=True, stop=True),
                       (lbd.b, Rbd.b), (pD.b,))
            sc.add("act", lambda e, pD=pD: e.activation(E1[:].rearrange("p c n -> p (c n)"), pD[:, 0:NCH * 128], AF.Exp), (pD.b,), (E1.b,))
            yield
            pKK = ps()
            pQK = ps()
            for c in range(NCH):
                sc.add("pe", lambda e, c=c, pKK=pKK: e.matmul(pKK[:, c * 128:(c + 1) * 128], Kbd[:, c, :], Kbd[:, c, :], start=True, stop=True),
                       (Kbd.b,), (pKK.b,))
            for c in range(NCH):
                sc.add("pe", lambda e, c=c, pQK=pQK: e.matmul(pQK[:, c * 128:(c + 1) * 128], Qbd[:, c, :], Kbd[:, c, :], start=True, stop=True),
                       (Qbd.b, Kbd.b), (pQK.b,))
            sc.add("dve", lambda e: e.tensor_tensor(MM2[:], E1[:], beta[:, :].unsqueeze(2).broadcast_to([128, NCH, 128]), ALU.mult),
                   (E1.b, beta.b), (MM2.b,))
            sc.add("pool", lambda e: e.tensor_tensor(MM2[:], MM2[:], slneg[:, :].unsqueeze(1).broadcast_to([128, NCH, 128]), ALU.mult),
                   (MM2.b, slneg.b), (MM2.b,))
            sc.add("dve", lambda e, pKK=pKK: e.tensor_tensor(Cb[0][:].rearrange("p c n -> p (c n)"), pKK[:, 0:NCH * 128],
                                                           MM2[:].rearrange("p c n -> p (c n)"), ALU.mult), (pKK.b, MM2.b), (Cb[0].b,))
            sc.add("pool", lambda e: e.tensor_tensor(E1[:], E1[:], li[:, :].unsqueeze(1).broadcast_to([128, NCH, 128]), ALU.mult),
                   (E1.b, li.b), (E1.b,))
            sc.add("dve", lambda e, pQK=pQK: e.tensor_tensor(attn[:].rearrange("p c n -> p (c n)"), pQK[:, 0:NCH * 128],
                                                           E1[:].rearrange("p c n -> p (c n)"), ALU.mult), (pQK.b, E1.b), (attn.b,))
            yield
            pT1 = ps(); pT2 = ps(); pT3 = ps(); pT4 = ps()
            for c in range(NCH):
                for (pt, src_) in ((pT1, Cb[0]), (pT2, attn), (pT3, Kbd), (pT4, Vbd)):
                    sc.add("pe", lambda e, c=c, pt=pt, src_=src_: e.transpose(
                        pt[:].bitcast(BF16)[:, c * 128:(c + 1) * 128], src_[:, c, :], ident[:]), (src_.b, ident.b), (pt.b,))
            sc.add("act", lambda e: e.copy(Bb[0][:].rearrange("p c n -> p (c n)"), pT1[:].bitcast(BF16)[:, 0:NCH * 128]), (pT1.b,), (Bb[0].b,))
            sc.add("act", lambda e: e.copy(attnT[:].rearrange("p c n -> p (c n)"), pT2[:].bitcast(BF16)[:, 0:NCH * 128]), (pT2.b,), (attnT.b,))
            pT3v = pT3[:].bitcast(BF16)[:, 0:NCH * 128].rearrange("p (c n) -> p c n", n=128)
            pT4v = pT4[:].bitcast(BF16)[:, 0:NCH * 128].rearrange("p (c n) -> p c n", n=128)
            sc.add("dve", lambda e, pT3v=pT3v: e.tensor_tensor(Kdec[:], pT3v, kdsc[:, :].unsqueeze(2).broadcast_to([128, NCH, 128]), ALU.mult),
                   (pT3.b, kdsc.b), (Kdec.b,))
            for h in range(2):
                hs = slice(h * 64, (h + 1) * 64)
                sc.add("dve", lambda e, h=h, hs=hs, pT3v=pT3v: e.tensor_tensor(
                    Xb[0][hs, :, 64:128], pT3v[hs, :, h * 64:(h + 1) * 64], bsc[hs, :].unsqueeze(2).broadcast_to([64, NCH, 64]), ALU.mult),
                    (pT3.b, bsc.b), (Xb[0].b,))
                sc.add("dve", lambda e, h=h, hs=hs, pT4v=pT4v: e.tensor_tensor(
                    Xb[0][hs, :, 0:64], pT4v[hs, :, h * 64:(h + 1) * 64], beta[hs, :].unsqueeze(2).broadcast_to([64, NCH, 64]), ALU.mult),
                    (pT4.b, beta.b), (Xb[0].b,))
            yield
            cur = 0
            for lvl in range(6):
                yield
                Bc, Cc, Xc = Bb[cur], Cb[cur], Xb[cur]
                Bn, Cn, Xn = Bb[1 - cur], Cb[1 - cur], Xb[1 - cur]
                pX = ps()
                for c in range(NCH):
                    sc.add("pe", lambda e, c=c, pX=pX, Bc=Bc, Xc=Xc: e.matmul(pX[:, c * 128:(c + 1) * 128], Bc[:, c, :], Xc[:, c, :], start=True, stop=True),
                           (Bc.b, Xc.b), (pX.b,))
                if lvl < 5:
                    sc.add("dve", lambda e, pX=pX, Xc=Xc, Xn=Xn: e.tensor_tensor(
                        Xn[:].rearrange("p c n -> p (c n)"), pX[:, 0:NCH * 128], Xc[:].rearrange("p c n -> p (c n)"), ALU.add),
                        (pX.b, Xc.b), (Xn.b,))
                    pB = ps()
                    for c in range(NCH):
                        sc.add("pe", lambda e, c=c, pB=pB, Bc=Bc, Cc=Cc: e.matmul(pB[:, c * 128:(c + 1) * 128], Cc[:, c, :], Bc[:, c, :], start=True, stop=True),
                               (Bc.b, Cc.b), (pB.b,))
                    sc.add("act", lambda e, pB=pB, Bn=Bn: e.copy(Bn[:].rearrange("p c n -> p (c n)"), pB[:, 0:NCH * 128]), (pB.b,), (Bn.b,))
                    if lvl < 4:
                        pC = ps()
                        for c in range(NCH):
                            sc.add("pe", lambda e, c=c, pC=pC, Bc=Bc, Cc=Cc: e.matmul(pC[:, c * 128:(c + 1) * 128], Bc[:, c, :], Cc[:, c, :], start=True, stop=True),
                                   (Bc.b, Cc.b), (pC.b,))
                        sc.add("act", lambda e, pC=pC, Cn=Cn: e.copy(Cn[:].rearrange("p c n -> p (c n)"), pC[:, 0:NCH * 128]), (pC.b,), (Cn.b,))
                    cur = 1 - cur
                else:
                    pXv = pX[:, 0:NCH * 128].rearrange("p (c n) -> p c n", n=128)
                    sc.add("dve", lambda e, pXv=pXv, Xc=Xc: e.tensor_tensor(ufin[:], pXv[:, :, 0:64], Xc[:, :, 0:64], ALU.add),
                           (pX.b, Xc.b), (ufin.b,))
                    for h in range(2):
                        hs = slice(h * 64, (h + 1) * 64)
                        sc.add("dve", lambda e, pXv=pXv, Xc=Xc, hs=hs, h=h: e.tensor_tensor(
                            Wbd2[hs, :, h * 64:(h + 1) * 64], pXv[hs, :, 64:128], Xc[hs, :, 64:128], ALU.add),
                            (pX.b, Xc.b), (Wbd2.b,))
            pT5 = ps()
            for c in range(NCH):
                sc.add("pe", lambda e, c=c, pT5=pT5: e.transpose(pT5[:].bitcast(BF16)[:, c * 128:(c + 1) * 128], Wbd2[:, c, :], ident[:]),
                       (Wbd2.b, ident.b), (pT5.b,))
            sc.add("act", lambda e, pT5=pT5: e.copy(Wt[:].rearrange("p c n -> p (c n)"), pT5[:].bitcast(BF16)[:, 0:NCH * 128]), (pT5.b,), (Wt.b,))
            yield
            for c in range(NCH):
                yield
                pw = ps()
                sc.add("pe", lambda e, c=c, pw=pw: e.matmul(pw[:, 0:64], Wt[:, c, :], Sbf[:], start=True, stop=True), (Wt.b, Sbf.b), (pw.b,))
                sc.add("dve", lambda e, c=c, pw=pw: e.tensor_tensor(vnew[:], ufin[:, c, :], pw[:, 0:64], ALU.subtract), (ufin.b, pw.b), (vnew.b,))
                po = ps()
                sc.add("pe", lambda e, c=c, po=po: e.matmul(po[:, 0:64], Qbd[:, c, :], Sbf[:], start=True, stop=True), (Qbd.b, Sbf.b), (po.b,))
                sc.add("pe", lambda e, c=c, po=po: e.matmul(po[:, 64:128], attnT[:, c, :], vnew[:], start=True, stop=True), (attnT.b, vnew.b), (po.b,))
                sc.add("pe", lambda e, c=c, po=po: e.matmul(po[:, 128:192], Kdec[:, c, :], vnew[:], start=True, stop=True), (Kdec.b, vnew.b), (po.b,))
                sc.add("dve", lambda e, c=c, po=po: e.scalar_tensor_tensor(Sst[:], Sst[:], decf[:, c:c + 1], po[:, 128:192], ALU.mult, ALU.add),
                       (Sst.b, decf.b, po.b), (Sst.b,))
                sc.add("act", lambda e: e.copy(Sbf[:], Sst[:]), (Sst.b,), (Sbf.b,))
                sc.add("act", lambda e, po=po: e.copy(oB[:], po[:, 64:128]), (po.b,), (oB.b,))
                sc.add("dve", lambda e, c=c, po=po: e.scalar_tensor_tensor(otile[:, c, :], po[:, 0:64], expG[:, c:c + 1], oB[:], ALU.mult, ALU.add),
                       (po.b, expG.b, oB.b), (otile.b,))
            yield
            sc.add("pool", lambda e: e.tensor_tensor(Rbd[:, :, 0:64], otile[:], otile[:], ALU.mult), (otile.b,), (Rbd.b,))
            sc.add("dve", lambda e: e.tensor_reduce(oss[:], Rbd[:, :, 0:64], AX.X, ALU.add), (Rbd.b,), (oss.b,))
            sc.add("act", lambda e: e.activation(oss[:], oss[:], AF.Ln, bias=EPS, scale=1.0 / 64.0), (oss.b,), (oss.b,))
            sc.add("act", lambda e: e.activation(oss[:], oss[:], AF.Exp, scale=-0.5), (oss.b,), (oss.b,))
            sc.add("dve", lambda e: e.tensor_tensor(otile[:], otile[:], oss[:, :].unsqueeze(2).broadcast_to([128, NCH, 64]), ALU.mult),
                   (otile.b, oss.b), (otile.b,))
            sc.add("pool", lambda e: e.tensor_tensor(otile[:], otile[:], gnw[:, :].unsqueeze(1).broadcast_to([128, NCH, 64]), ALU.mult),
                   (otile.b, gnw.b), (otile.b,))
            for h in range(2):
                hs = slice(h * 64, (h + 1) * 64)
                sc.add("act", lambda e, hs=hs, h=h: e.copy(onbd[hs, :, h * 64:(h + 1) * 64], otile[hs, :, :]), (otile.b,), (onbd.b,))
            pT6 = ps()
            for c in range(NCH):
                sc.add("pe", lambda e, c=c, pT6=pT6: e.transpose(pT6[:].bitcast(BF16)[:, c * 128:(c + 1) * 128], onbd[:, c, :], ident[:]),
                       (onbd.b, ident.b), (pT6.b,))
            for h in range(2):
                hs = slice(h * 64, (h + 1) * 64)
                sc.add("dve", lambda e, hs=hs, h=h, tt=tt, pT6=pT6: e.tensor_tensor(
                    gbT[hs, tt * TT:(tt + 1) * TT].rearrange("p (c i) -> p c i", i=64),
                    pT6[:].bitcast(BF16)[hs, 0:NCH * 128].rearrange("p (c n) -> p c n", n=128)[:, :, h * 64:(h + 1) * 64],
                    zbS[hs, :].rearrange("p (c i) -> p c i", i=64), ALU.mult), (pT6.b, zbS.b), (gbT.b,))
        prevB = None
        for tt in range(TPS):
            interleave(tileA(st, slot, tt), prevB)
            prevB = tileB(st, slot, tt)

        def w2_stage_pieces(stage):
            def flat(t_):
                return t_[:].rearrange("p b n -> p (b n)") if len(t_[:].shape) == 3 else t_[:]
            def gate(t_, c):
                return [(t_, flat(t_)[:, j * TT:(j + 1) * TT], wg[:, c, j * TT:(j + 1) * TT]) for j in range(2048 // TT)]
            def two(t_, src, i):
                return [(t_, flat(t_)[:, (c % 2) * 1024 + j * TT:(c % 2) * 1024 + (j + 1) * TT], src[:, c, j * TT:(j + 1) * TT])
                        for c in (2 * i, 2 * i + 1) for j in range(1024 // TT)]
            if stage == 0:
                return two(Vt[0], wua, 1) + two(Vt[1], wub, 0) + two(Vt[2], wub, 1)
            if stage == 1:
                return gate(Kt[0][0], 0) + gate(Kt[0][1], 1) + gate(Qt[0], 6) + two(Vs[0][0], wo, 0) + two(Vs[0][1], wo, 1)
            if stage == 2:
                return gate(Kt[1][0], 2) + gate(Kt[1][1], 3) + gate(Qt[1], 7) + two(Vs[1][0], wo, 2) + two(Vs[1][1], wo, 3)
            return gate(Kt[2][0], 4) + gate(Kt[2][1], 5) + two(Qt[2], wua, 0)

        def emit_pieces(q, n):
            for _ in range(min(n, len(q))):
                t_, dst_ap, src_ap = q.pop(0)
                stb, k = ring()
                sc.add("sp", lambda e, stb=stb, src_ap=src_ap: e.dma_start(out=stb[:, :], in_=src_ap), (), (stb.b,), dkey=k)
                cast_to(dst_ap, t_.b, stb[:, :], stb.b)

        def attention(st, slot):
            wq = []
            for g in range(3):
                for q4 in range(4):
                    pv = ps()
                    for j in range(4):
                        blk = q4 * 4 + j
                        sc.add("pe", lambda e, pv=pv, j=j, blk=blk, g=g: e.transpose(
                            pv[:].bitcast(BF16)[:, j * 128:(j + 1) * 128], Vt[g][:, blk * 128:(blk + 1) * 128], ident[:]),
                            (Vt[g].b, ident.b), (pv.b,))
                    sc.add("act", lambda e, pv=pv, q4=q4, g=g, slot=slot: e.copy(
                        Vs[g][slot][:, q4 * 4:(q4 + 1) * 4, :].rearrange("p b n -> p (b n)"), pv[:].bitcast(BF16)[:, 0:512]),
                        (pv.b,), (Vs[g][slot].b,))
                    yield
            units = [(g, blk) for g in range(3) for blk in range(16)]
            if st == 3:
                wq.extend(w2_stage_pieces(0))

            def stage_s(u):
                g, blk = units[u]
                d = DIL[g]
                nps = 16 // d
                r, n = blk // nps, blk % nps
                halves = []
                if n > 0:
                    halves.append((0, slot, blk - 1))
                elif st > 0:
                    halves.append((0, 1 - slot, r * nps + nps - 1))
                halves.append((1, slot, blk))
                nh = len(halves)
                pS = ps()
                Pb = Pt[u % 3]
                qx = Qx[u % 2]
                sc.add("pool", lambda e, qx=qx, g=g, blk=blk: e.tensor_copy(qx[0:64, 0:128], Qt[g][0:64, blk * 128:(blk + 1) * 128]),
                       (Qt[g].b,), (qx.b,))
                sc.add("act", lambda e, qx=qx, g=g, blk=blk: e.copy(qx[64:128, 128:256], Qt[g][64:128, blk * 128:(blk + 1) * 128]),
                       (Qt[g].b,), (qx.b,))
                for (hf, sl, kb) in halves:
                    sc.add("pe", lambda e, hf=hf, sl=sl, kb=kb, g=g, pS=pS, qx=qx: e.matmul(
                        pS[:, hf * 256:(hf + 1) * 256], Kt[g][sl][:, kb * 128:(kb + 1) * 128], qx[:, :],
                        start=True, stop=True), (Kt[g][sl].b, qx.b), (pS.b,))
                lo = halves[0][0] * 256
                sc.add("act", lambda e, lo=lo, Pb=Pb, pS=pS: e.activation(
                    Pb[:, lo:512], pS[:, lo:512], AF.Exp, scale=0.125), (pS.b,), (Pb.b,))
                sc.add("dve", lambda e, Pb=Pb, lo=lo: e.tensor_tensor(Pb[:, lo:512], Pb[:, lo:512], dmask[:, lo:512], ALU.mult),
                       (Pb.b, dmask.b), (Pb.b,))
                return (g, d, r, n, halves, nh, Pb)

            def stage_pv(ctx):
                g, d, r, n, halves, nh, Pb = ctx
                pO = ps()
                for h in range(2):
                    hs = slice(h * 64, (h + 1) * 64)
                    for k, (hf, sl, kb) in enumerate(halves):
                        sc.add("pe", lambda e, h=h, hs=hs, hf=hf, sl=sl, kb=kb, k=k, g=g, Pb=Pb, pO=pO, nh=nh: e.matmul(
                            pO[hs, 0:128], Vs[g][sl][:, kb, h * 64:(h + 1) * 64], Pb[:, hf * 256 + h * 128:hf * 256 + (h + 1) * 128],
                            start=(k == 0), stop=(k == nh - 1)), (Vs[g][sl].b, Pb.b), (pO.b,))
                    for k, (hf, sl, kb) in enumerate(halves):
                        sc.add("pe", lambda e, h=h, hs=hs, hf=hf, k=k, Pb=Pb, pO=pO, nh=nh: e.matmul(
                            pO[hs, 128:256], ones[:, 0:64], Pb[:, hf * 256 + h * 128:hf * 256 + (h + 1) * 128],
                            start=(k == 0), stop=(k == nh - 1)), (ones.b, Pb.b), (pO.b,))
                off = n * 128 * d + r
                dstv = ndacc[:, :, off:off + 127 * d + 1:d] if d > 1 else ndacc[:, :, off:off + 128]
                srcv = pO[:, 0:256].rearrange("p (a q) -> p a q", a=2)
                if g == 0:
                    sc.add("act", lambda e, dstv=dstv, srcv=srcv: e.copy(dstv, srcv), (pO.b,), (ndacc.b,))
                else:
                    sc.add("dve", lambda e, dstv=dstv, srcv=srcv: e.tensor_tensor(dstv, srcv, dstv, ALU.add), (pO.b, ndacc.b), (ndacc.b,))

            ctx = stage_s(0)
            for u in range(len(units)):
                nxt = stage_s(u + 1) if u + 1 < len(units) else None
                stage_pv(ctx)
                ctx = nxt
                if st == 3:
                    if u == 15:
                        wq.extend(w2_stage_pieces(1))
                    if u == 31:
                        wq.extend(w2_stage_pieces(2))
                    emit_pieces(wq, 3)
                yield
            sc.add("act", lambda e: e.activation(ndacc[:, 1, :], ndacc[:, 1, :], AF.Ln), (ndacc.b,), (ndacc.b,))
            sc.add("act", lambda e: e.activation(ndacc[:, 1, :], ndacc[:, 1, :], AF.Exp, scale=-1.0), (ndacc.b,), (ndacc.b,))
            sc.add("dve", lambda e: e.tensor_tensor(ndacc[:, 0, :], ndacc[:, 0, :], ndacc[:, 1, :], ALU.mult), (ndacc.b,), (ndacc.b,))
            sc.add("pool", lambda e: e.tensor_tensor(gaT[:], ndacc[:, 0, :], zaS[:], ALU.mult), (ndacc.b, zaS.b), (gaT.b,))
            if st == 3:
                wq.extend(w2_stage_pieces(3))
                emit_pieces(wq, len(wq))
            yield

        interleave(attention(st, slot), prevB)
        if st > 0:
            copy_out(st - 1)
        sc.add("pool", lambda e, st=st: e.dma_start(out=bin_[st][0:128, :], in_=gaT[:]), (gaT.b,), (bin_b[st],), dkey="bi")
        sc.add("pool", lambda e, st=st: e.dma_start(out=bin_[st][128:256, :], in_=gbT[:]), (gbT.b,), (bin_b[st],), dkey="bi")
        sc.add("pool", lambda e, st=st: e.collective_compute(
            "AllGather", ALU.bypass, replica_groups=[[0, 1, 2, 3], [4, 5, 6, 7]], ins=[bin_[st][:, :]], outs=[bout[st][:, :]]),
            (bin_b[st],), (bout_b[st],), dkey="cc%d" % st)

    copy_out(3)
    es2 = es
    Wg = T.__new__(T); Wg.b = Buf("Wg")
    def alias(src, shape_str, dt=None, **kw):
        ap = src[:]
        if dt is not None:
            ap = ap.bitcast(dt)
        return ap

    class A:
        def __init__(self, ap, name, base=None, share=False):
            self.ap = ap
            if share:
                self.b = base.b
            else:
                self.b = Buf(name)
                if base is not None:
                    sc.link(self.b, base.b)

        def __getitem__(self, k):
            return self.ap[k]

    wg_parts = [Kt[0][0], Kt[0][1], Kt[1][0], Kt[1][1], Kt[2][0], Kt[2][1], Qt[0], Qt[1]]
    Wgc = [A(p_[:], "Wg%d" % i, p_, True) for i, p_ in enumerate(wg_parts)]
    Wua = [A(t_[:], "wua%d" % i, t_, True) for i, t_ in enumerate((Qt[2], Vt[0]))]
    Wub = [A(t_[:], "wub%d" % i, t_, True) for i, t_ in enumerate((Vt[1], Vt[2]))]
    wo_parts = [Vs[0][0], Vs[0][1], Vs[1][0], Vs[1][1]]
    Woc = [A(t_[:].rearrange("p b n -> p (b n)"), "wo%d" % i, t_, True) for i, t_ in enumerate(wo_parts)]
    fnw = A(ndacc[:, 0, 0:1024], "fnw", ndacc)
    xown = [A(ndacc[:, 1, 0:1024], "xown0", ndacc), A(ndacc[:, 1, 1024:2048], "xown1", ndacc)]
    ybuf = A(ndacc[:, 0, 1024:2048], "ybuf", ndacc)
    ga_g = A(Vs[2][0][:].rearrange("p b n -> p (b n)")[:, 0:4 * 512].rearrange("p (r t) -> p r t", r=4), "ga_g", Vs[2][0])
    gb_g = A(Vs[2][1][:].rearrange("p b n -> p (b n)")[:, 0:4 * 512].rearrange("p (r t) -> p r t", r=4), "gb_g", Vs[2][1])
    hTo = hT2[0]
    cur_hT[0] = hTo
    sgA = rt1
    sgB = rt2
    merged = A(gbT[:].rearrange("p (c t) -> p c t", c=8), "merged", gbT)
    ssq2 = A(oss[:, 0:1], "ssq2", oss)
    junk = A(gaT[:].bitcast(F32), "junk", gaT)

    def load_w2(dsts, src, nchunk, ncols, per):
        for c in range(nchunk):
            d_ = dsts[c // per]
            base = (c % per) * ncols
            for j in range(ncols // TT):
                stb, k = ring()
                sc.add("sp", lambda e, stb=stb, c=c, j=j: e.dma_start(out=stb[:, :], in_=src[:, c, j * TT:(j + 1) * TT]), (), (stb.b,), dkey=k)
                cast_to(d_[:, base + j * TT: base + (j + 1) * TT], d_.b, stb[:, :], stb.b)

    sc.add("sp", lambda e: e.dma_start(out=fnw[:, :], in_=fnw_d), (), (fnw.b,), dkey="fnw")

    for st in range(4):
        def p2(st, half):
            tcol = half * TT
            for r in range(4):
                sc.add("sp", lambda e, st=st, r=r, tcol=tcol: e.dma_start(
                    out=ga_g[:, r, 0:TT], in_=gown[st][r * 256:r * 256 + 128, tcol:tcol + TT]),
                    (gown_b[st],), (ga_g.b,), dkey="gag")
                sc.add("sp", lambda e, st=st, r=r, tcol=tcol: e.dma_start(
                    out=gb_g[:, r, 0:TT], in_=gown[st][r * 256 + 128:r * 256 + 256, tcol:tcol + TT]),
                    (gown_b[st],), (gb_g.b,), dkey="gbg")
            rmsnorm_tile(xTo, st * 512 + tcol, hTo)
            for mb in range(8):
                k2 = mb % 2
                pa = ps()
                for c in range(8):
                    sc.add("pe", lambda e, c=c, pa=pa, mb=mb: e.matmul(pa[:, 0:TT], Wgc[c][:, mb * 128:(mb + 1) * 128], hTo[:, c, :],
                                                                     start=(c == 0), stop=(c == 7)), (Wgc[c].b, hTo.b), (pa.b,))
                sc.add("act", lambda e, pa=pa, k2=k2: e.activation(sgA[k2][:], pa[:, 0:TT], AF.Sigmoid), (pa.b,), (sgA[k2].b,))
                pb_ = ps()
                for c in range(8):
                    sc.add("pe", lambda e, c=c, pb_=pb_, mb=mb: e.matmul(pb_[:, 0:TT], Wgc[c][:, 1024 + mb * 128:1024 + (mb + 1) * 128], hTo[:, c, :],
                                                                      start=(c == 0), stop=(c == 7)), (Wgc[c].b, hTo.b), (pb_.b,))
                sc.add("act", lambda e, pb_=pb_, k2=k2: e.activation(sgB[k2][:], pb_[:, 0:TT], AF.Sigmoid), (pb_.b,), (sgB[k2].b,))
                pya = ps()
                for r in range(4):
                    sc.add("pe", lambda e, r=r, pya=pya, mb=mb: e.matmul(
                        pya[:, 0:TT], Wua[r // 2][:, (r % 2) * 1024 + mb * 128:(r % 2) * 1024 + (mb + 1) * 128], ga_g[:, r, 0:TT],
                        start=(r == 0), stop=(r == 3)), (Wua[r // 2].b, ga_g.b), (pya.b,))
                pyb = ps()
                for r in range(4):
                    sc.add("pe", lambda e, r=r, pyb=pyb, mb=mb: e.matmul(
                        pyb[:, 0:TT], Wub[r // 2][:, (r % 2) * 1024 + mb * 128:(r % 2) * 1024 + (mb + 1) * 128], gb_g[:, r, 0:TT],
                        start=(r == 0), stop=(r == 3)), (Wub[r // 2].b, gb_g.b), (pyb.b,))
                sc.add("dve", lambda e, pya=pya, k2=k2: e.tensor_tensor(sgA[k2][:], pya[:, 0:TT], sgA[k2][:], ALU.mult), (pya.b, sgA[k2].b), (sgA[k2].b,))
                sc.add("dve", lambda e, pyb=pyb, k2=k2: e.tensor_tensor(sgB[k2][:], pyb[:, 0:TT], sgB[k2][:], ALU.mult), (pyb.b, sgB[k2].b), (sgB[k2].b,))
                sc.add("pool", lambda e, k2=k2, mb=mb: e.tensor_tensor(merged[:, mb, :], sgA[k2][:], sgB[k2][:], ALU.add), (sgA[k2].b, sgB[k2].b), (merged.b,))
            for tb in range(TT // 128):
                row0 = st * 512 + tcol + tb * 128
                xw = xown[tb % 2]
                sc.add("sp", lambda e, xw=xw, row0=row0: e.dma_start(out=xw[:, :], in_=xo[row0:row0 + 128, :]), (), (xw.b,), dkey="xo%d" % (tb % 2))
                po2 = [ps(), ps()]
                for nh_ in range(2):
                    for c in range(8):
                        sc.add("pe", lambda e, c=c, nh_=nh_, tb=tb, po2=po2: e.matmul(
                            po2[nh_][:, 0:512], merged[:, c, tb * 128:(tb + 1) * 128],
                            Woc[c // 2][:, (c % 2) * 1024 + nh_ * 512:(c % 2) * 1024 + (nh_ + 1) * 512],
                            start=(c == 0), stop=(c == 7)), (merged.b, Woc[c // 2].b), (po2[nh_].b,))
                for nh_ in range(2):
                    sc.add("dve", lambda e, nh_=nh_, xw=xw, po2=po2: e.tensor_tensor(
                        ybuf[:, nh_ * 512:(nh_ + 1) * 512], po2[nh_][:, 0:512], xw[:, nh_ * 512:(nh_ + 1) * 512], ALU.add),
                        (po2[nh_].b, xw.b), (ybuf.b,))
                sc.add("act", lambda e: e.activation(junk[:, :], ybuf[:, :], AF.Square, accum_out=ssq2[:]), (ybuf.b,), (junk.b, ssq2.b))
                sc.add("act", lambda e: e.activation(ssq2[:], ssq2[:], AF.Ln, bias=EPS, scale=1.0 / 1024.0), (ssq2.b,), (ssq2.b,))
                sc.add("act", lambda e: e.activation(ssq2[:], ssq2[:], AF.Exp, scale=-0.5), (ssq2.b,), (ssq2.b,))
                sc.add("dve", lambda e: e.scalar_tensor_tensor(ybuf[:, :], ybuf[:, :], ssq2[:, 0:1], fnw[:, :], ALU.mult, ALU.mult),
                       (ybuf.b, ssq2.b, fnw.b), (ybuf.b,))
                sc.add("sp", lambda e, row0=row0: e.dma_start(out=out_d[row0:row0 + 128, :], in_=ybuf[:, :]), (ybuf.b,), (Buf(),), dkey="out")
        for half in range(2):
            p2(st, half)
    fin = Buf("fin")
    last_out = [o for o in sc.ops["sp"] if o.dkey == "out"][-1]
    op = sc.add("sp", None, (), ())
    op.deps = [last_out]
    sc.reorder(window=100)
    build_program.stats = (sc.est_total, {e: len(sc.ops[e]) for e in sc.ENGS})
    sc.emit(nc, es)
    es.close()
    return nc


def _consts():
    bf = ml_dtypes.bfloat16
    idx = np.arange(128)
    h = idx // 64
    i = idx % 64
    same = (h[:, None] == h[None, :])
    c = {}
    sw = h * 64 + (i + 32) % 64
    perm = np.zeros((128, 128), np.float32)
    perm[sw, idx] = 1.0
    c["perm"] = perm.astype(bf)
    c["ident"] = np.eye(128, dtype=np.float32).astype(bf)
    c["onesbd"] = same.astype(np.float32).astype(bf)
    c["onesbdf"] = same.astype(np.float32)
    c["ones"] = np.ones((128, 128), np.float32).astype(bf)
    c["lbd"] = (same & (i[:, None] <= i[None, :])).astype(np.float32)
    ua = np.zeros((128, 130), np.float32)
    ua[:, :128] = (same & (i[:, None] > i[None, :])).astype(np.float32)
    ua[:, 128] = 1.0
    c["uaug"] = ua
    c["slneg"] = -(same & (i[:, None] > i[None, :])).astype(np.float32)
    c["li"] = (same & (i[:, None] >= i[None, :])).astype(np.float32)
    j_ = np.arange(128)[:, None]
    q_ = np.arange(128)[None, :]
    prev = (j_ >= q_).astype(np.float32)
    cur = (j_ <= q_).astype(np.float32)
    dm = np.concatenate([prev, prev, cur, cur], axis=1)
    c["dmask"] = dm.astype(bf)
    inv = (10000.0 ** (-np.arange(0, 64, 2, dtype=np.float32) / 64.0)).astype(np.float32)
    ang = (np.arange(S, dtype=np.float32)[:, None] * inv[None, :]).astype(np.float32)
    ang = np.concatenate([ang, ang], axis=-1)
    cos = np.cos(ang).astype(np.float32).T
    sin = np.sin(ang).astype(np.float32).T
    sgn = np.where(np.arange(64) < 32, -1.0, 1.0).astype(np.float32)[:, None]
    c["cosT"] = np.ascontiguousarray(np.concatenate([cos, cos], 0))
    c["sinT"] = np.ascontiguousarray(np.concatenate([sin * sgn, sin * sgn], 0))
    return c


def _chunk(w, nchunk):
    return np.ascontiguousarray(w.reshape(nchunk, 128, -1).transpose(1, 0, 2))


_NC = [None]


def kernel(x, norm_w, w_in, conv_w, a_log, dt_bias, gdn_norm_w, w_up_a, w_up_b, w_out, final_norm_w):
    x = np.asarray(x, np.float32)
    w_in0 = np.asarray(w_in, np.float32)[0]
    conv0 = np.asarray(conv_w, np.float32)[0]
    if _NC[0] is None:
        _NC[0] = build_program()
    nc = _NC[0]
    cst = _consts()
    shared = dict(cst)
    shared["wg"] = _chunk(w_in0[:, 7184:9232], 8)
    shared["wua"] = _chunk(np.asarray(w_up_a, np.float32)[0], 4)
    shared["wub"] = _chunk(np.asarray(w_up_b, np.float32)[0], 4)
    shared["wo"] = _chunk(np.asarray(w_out, np.float32)[0], 8)
    shared["normw"] = np.ascontiguousarray(np.asarray(norm_w, np.float32)[0].reshape(8, 128).T)
    shared["fnw"] = np.ascontiguousarray(np.broadcast_to(np.asarray(final_norm_w, np.float32)[None, :], (128, 1024)))
    shared["gnw"] = np.ascontiguousarray(np.broadcast_to(np.asarray(gdn_norm_w, np.float32)[0][None, :], (128, 64)))
    in_maps = []
    owns = []
    for core in range(8):
        b, c4 = core // 4, core % 4
        h0 = 2 * c4
        xTb = _chunk(np.ascontiguousarray(x[b].T), 8)
        own = np.concatenate([st * ST + c4 * 512 + np.arange(512) for st in range(4)])
        owns.append(own)
        cols = []
        for g in range(3):
            for t in range(3):
                s0 = g * 1536 + t * 512 + h0 * 64
                cols.append(np.arange(s0, s0 + 128))
        cols.append(np.arange(4608 + h0 * 64, 4608 + h0 * 64 + 128))
        for t in range(3):
            s0 = 5120 + t * 512 + h0 * 64
            cols.append(np.arange(s0, s0 + 128))
        cols.append(np.arange(6656 + h0 * 64, 6656 + h0 * 64 + 128))
        w1 = np.zeros((1024, NB1 * 128), np.float32)
        cc = np.concatenate(cols)
        w1[:, :14 * 128] = w_in0[:, cc]
        w1[:, 14 * 128 + 0] = w_in0[:, 7168 + h0]
        w1[:, 14 * 128 + 1] = w_in0[:, 7176 + h0]
        w1[:, 14 * 128 + 2] = w_in0[:, 7168 + h0 + 1]
        w1[:, 14 * 128 + 3] = w_in0[:, 7176 + h0 + 1]
        convw = np.zeros((128, 12), np.float32)
        for t in range(3):
            s0 = t * 512 + h0 * 64
            convw[:, t * 4:(t + 1) * 4] = conv0[:, s0:s0 + 128].T
        hh = np.arange(128) // 64
        m = dict(shared)
        m["xT"] = xTb
        m["xTo"] = np.ascontiguousarray(xTb[:, :, own])
        m["xo"] = np.ascontiguousarray(x[b][own])
        m["w1"] = _chunk(w1, 8)
        m["convw"] = convw
        m["alog"] = np.asarray(a_log, np.float32)[0][h0 + hh][:, None].copy()
        m["dtb"] = np.asarray(dt_bias, np.float32)[0][h0 + hh][:, None].copy()
        in_maps.append(m)
    res = run_bass_kernel_spmd(nc, in_maps, core_ids=list(range(8)))
    out = np.zeros((2, S, 1024), np.float32)
    for core in range(8):
        out[core // 4, owns[core]] = np.asarray(res.results[core]["out"], np.float32)
    return out
```

```python
from contextlib import ExitStack
import numpy as np
import ml_dtypes
import concourse.bass as bass
import concourse.mybir as mybir
from concourse.bass_utils import run_bass_kernel_spmd

F32 = mybir.dt.float32
BF16 = mybir.dt.bfloat16
ALU = mybir.AluOpType
AF = mybir.ActivationFunctionType
AX = mybir.AxisListType

S = 8192
TT = 256
NTT = S // TT
ST = 2048
TPS = ST // TT
NCH = TT // 64
DIL = (1, 4, 16)
EPS = 1e-6
NB1 = 15
(ZA, GQ, GK, GV, ZB, BA) = (9, 10, 11, 12, 13, 14)


class Buf:
    __slots__ = ("n", "psum")

    def __init__(self, n="", psum=False):
        self.n = n
        self.psum = psum


class Op:
    __slots__ = ("eng", "fn", "deps", "sig", "cnt", "dkey", "dcnt", "idx", "alldeps", "dur", "st", "fin", "nun", "users", "bl")


class _FakeEng:
    def __init__(self):
        self.rec = None

    def __getattr__(self, name):
        def f(*a, **k):
            self.rec = (name, a, k)
            return self
        return f


def _free_elems(ap):
    try:
        sh = list(ap.shape)
        n = 1
        for v in sh[1:]:
            n *= int(v)
        return n
    except Exception:
        return 0


def _estimate(op):
    if op.fn is None:
        return 0.0
    if op.dkey is not None:
        return 0.1
    fe = _FakeEng()
    try:
        op.fn(fe)
    except Exception:
        return 0.5
    name, a, k = fe.rec if fe.rec else ("", (), {})
    args = list(a) + list(k.values())
    n = max([_free_elems(x) for x in args if hasattr(x, "shape")] + [1])
    if op.eng == "pe":
        if name == "matmul":
            rhs = a[2] if len(a) > 2 else k.get("rhs")
            n = _free_elems(rhs)
            f32 = str(getattr(rhs, "dtype", "")).endswith("float32")
            small = 0.0
            try:
                if int(a[1].shape[0]) < 128 or _free_elems(a[1]) < 128:
                    small = 0.08
            except Exception:
                pass
            return 0.05 + small + n * (0.0016 if f32 else 0.0004)
        return 0.11
    if op.eng == "act":
        return 0.20 + n * 0.0008
    if op.eng == "dve":
        return 0.20 + n * 0.0010
    if op.eng == "pool":
        return 0.35 + n * 0.0015
    return 0.1


class Sched:
    ENGS = ("pe", "act", "dve", "pool", "sp")

    def __init__(self):
        self.ops = {e: [] for e in self.ENGS}
        self.last_w = {}
        self.readers = {}
        self.dma_n = {}
        self.nops = 0
        self.last_key = {}

    def add(self, eng, fn, r=(), w=(), dkey=None):
        op = Op()
        op.eng, op.fn, op.sig, op.cnt, op.dkey, op.dcnt = eng, fn, False, 0, dkey, 0
        op.idx = self.nops
        self.nops += 1
        if dkey is not None:
            self.dma_n[dkey] = self.dma_n.get(dkey, 0) + 1
            op.dcnt = self.dma_n[dkey]
        w = tuple(w) + tuple(b for b in r if b.psum and b not in w)
        deps = {}
        for b in r:
            lw = self.last_w.get(b)
            if lw is not None:
                deps[id(lw)] = lw
        for b in w:
            lw = self.last_w.get(b)
            if lw is not None:
                deps[id(lw)] = lw
            for o in self.readers.get(b, {}).values():
                deps[id(o)] = o
        op.alldeps = list(deps.values())
        if dkey is not None:
            lk = self.last_key.get(dkey)
            if lk is not None:
                op.alldeps.append(lk)
            self.last_key[dkey] = op
        op.deps = [d for d in deps.values()
                   if not (d.eng == "pe" and eng == "pe" and d.dkey is None and dkey is None)]
        for d in op.deps:
            d.sig = True
        rk = eng if dkey is None else ("dma", dkey)
        for b in r:
            self.readers.setdefault(b, {})[rk] = op
        for b in w:
            self.last_w[b] = op
            self.readers[b] = {}
        self.ops[eng].append(op)
        return op

    def link(self, new, old):
        lw = self.last_w.get(old)
        if lw is not None:
            self.last_w[new] = lw
        self.readers[new] = dict(self.readers.get(old, {}))

    def barrier(self):
        lasts = [self.ops[e][-1] for e in self.ENGS if self.ops[e]]
        b = Buf("barrier")
        for o in lasts:
            self.last_w.pop(b, None)
        for e in self.ENGS:
            op = self.add(e, None, (), ())
            op.deps = [o for o in lasts]
            op.alldeps = [o for o in lasts]
            for o in lasts:
                o.sig = True

    def reorder(self, window=40, lat=0.12, dma_lat=2.5, use_bl=True, bl_engs=("pe",)):
        allops = []
        for e in self.ENGS:
            allops.extend(self.ops[e])
        for op in allops:
            op.dur = _estimate(op)
            op.st = op.fin = None
            op.users = []
        for op in allops:
            op.nun = 0
        for e in self.ENGS:
            prev = None
            fence = None
            for op in self.ops[e]:
                if prev is not None and op.fn is None:
                    op.alldeps = list(op.alldeps) + [prev]
                if fence is not None:
                    op.alldeps = list(op.alldeps) + [fence]
                if op.fn is None:
                    fence = op
                prev = op
        for op in allops:
            seen = set()
            dd = []
            for d in op.alldeps:
                if id(d) not in seen:
                    seen.add(id(d))
                    dd.append(d)
            op.alldeps = dd
            op.nun = len(dd)
            for d in dd:
                d.users.append(op)
        eff = lambda o: ((100.0 if o.dkey.startswith('cc') else (8.0 if o.dkey.startswith('go') else dma_lat)) if o.dkey is not None else o.dur)
        for op in sorted(allops, key=lambda o: -o.idx):
            b = 0.0
            for u in op.users:
                if u.bl + lat > b:
                    b = u.bl + lat
            op.bl = b + eff(op)
        pend = {e: list(self.ops[e]) for e in self.ENGS}
        head = {e: 0 for e in self.ENGS}
        tfree = {e: 0.0 for e in self.ENGS}
        order = {e: [] for e in self.ENGS}
        remaining = len(allops)
        while remaining:
            best = None
            for e in self.ENGS:
                lst = pend[e]
                i = head[e]
                cnt = 0
                tf = tfree[e]
                cand = None
                while i < len(lst) and cnt < window:
                    op = lst[i]
                    i += 1
                    if op is None:
                        continue
                    cnt += 1
                    if op.nun:
                        continue
                    rt = tf
                    for d in op.alldeps:
                        v = d.fin + lat
                        if v > rt:
                            rt = v
                    if use_bl and e in bl_engs:
                        key = (rt if rt > tf + 0.02 else tf, -op.bl, op.idx)
                    else:
                        key = (rt, op.idx, 0)
                    if cand is None or key < cand[0]:
                        cand = (key, e, i - 1, op, rt)
                if cand is not None and (best is None or cand[0] < best[0]):
                    best = cand
            _, e, pos, op, rt = best
            rt = max(rt, tfree[e])
            op.st = rt
            busy = op.dur
            op.fin = rt + ((100.0 if op.dkey.startswith('cc') else (8.0 if op.dkey.startswith('go') else dma_lat)) if op.dkey is not None else busy)
            tfree[e] = rt + busy
            pend[e][pos] = None
            while head[e] < len(pend[e]) and pend[e][head[e]] is None:
                head[e] += 1
            order[e].append(op)
            for u in op.users:
                u.nun -= 1
            remaining -= 1
        for e in self.ENGS:
            self.ops[e] = order[e]
        self.est_total = max(tfree.values())

    def emit(self, nc, es):
        engobj = {"pe": nc.tensor, "act": nc.scalar, "dve": nc.vector, "pool": nc.gpsimd, "sp": nc.sync}
        sems = {e: es.enter_context(nc.semaphore("s_" + e)) for e in self.ENGS}
        dsems = {k: es.enter_context(nc.semaphore("d_%s" % (k,))) for k in self.dma_n}
        for e in self.ENGS:
            c = 0
            for op in self.ops[e]:
                if op.dkey is None and op.sig:
                    c += 1
                op.cnt = c
        block = es.enter_context(nc.Block())

        def run(e, eng):
            waited = {}
            for op in self.ops[e]:
                for d in op.deps:
                    if d.dkey is not None and d.dkey.startswith("cc"):
                        sem, val, key = dsems[d.dkey], 1, ("d", d.dkey)
                    elif d.dkey is not None:
                        sem, val, key = dsems[d.dkey], 16 * d.dcnt, ("d", d.dkey)
                    else:
                        sem, val, key = sems[d.eng], d.cnt, ("c", d.eng)
                    if waited.get(key, 0) >= val:
                        continue
                    waited[key] = val
                    eng.wait_ge(sem, val)
                if op.fn is None:
                    continue
                ins = op.fn(eng)
                if op.dkey is not None and op.dkey.startswith("cc"):
                    ins.then_inc(dsems[op.dkey])
                elif op.dkey is not None:
                    ins.then_inc(dsems[op.dkey], 16)
                elif op.sig:
                    ins.then_inc(sems[e], 1)

        @block.tensor
        def _(eng):
            run("pe", eng)

        @block.scalar
        def _(eng):
            run("act", eng)

        @block.vector
        def _(eng):
            run("dve", eng)

        @block.gpsimd
        def _(eng):
            run("pool", eng)

        @block.sync
        def _(eng):
            run("sp", eng)


class T:
    def __init__(self, nc, es, name, shape, dt, psum=False):
        self.t = es.enter_context((nc.psum_tensor if psum else nc.sbuf_tensor)("t_" + name, list(shape), dt))
        self.b = Buf(name, psum)

    def __getitem__(self, k):
        return self.t[k]


def build_program():
    nc = bass.Bass("TRN2", target_bir_lowering=False)
    es = ExitStack()
    sc = Sched()

    def dram_in(name, shape, dt=F32):
        return nc.dram_tensor(name, list(shape), dt, kind="ExternalInput").ap()

    xT = dram_in("xT", [128, 8, S])
    xTo = dram_in("xTo", [128, 8, ST])
    xo = dram_in("xo", [ST, 1024])
    w1 = dram_in("w1", [128, 8, NB1 * 128])
    wg = dram_in("wg", [128, 8, 2048])
    wua = dram_in("wua", [128, 4, 1024])
    wub = dram_in("wub", [128, 4, 1024])
    wo = dram_in("wo", [128, 8, 1024])
    normw_d = dram_in("normw", [128, 8])
    fnw_d = dram_in("fnw", [128, 1024])
    convw_d = dram_in("convw", [128, 12])
    alog_d = dram_in("alog", [128, 1])
    dtb_d = dram_in("dtb", [128, 1])
    gnw_d = dram_in("gnw", [128, 64])
    cos_d = dram_in("cosT", [128, S])
    sin_d = dram_in("sinT", [128, S])
    perm_d = dram_in("perm", [128, 128], BF16)
    ident_d = dram_in("ident", [128, 128], BF16)
    onesbd_d = dram_in("onesbd", [128, 128], BF16)
    onesbdf_d = dram_in("onesbdf", [128, 128])
    ones_d = dram_in("ones", [128, 128], BF16)
    lbd_d = dram_in("lbd", [128, 128])
    uaug_d = dram_in("uaug", [128, 130])
    slneg_d = dram_in("slneg", [128, 128])
    li_d = dram_in("li", [128, 128])
    dmask_d = dram_in("dmask", [128, 512], BF16)
    out_d = nc.dram_tensor("out", [ST, 1024], F32, kind="ExternalOutput").ap()
    bin0 = nc.dram_tensor("bin0", [256, ST], BF16)
    bout0 = nc.dram_tensor("bout0", [1024, ST], BF16)
    bin_ = [bin0] * 4
    bout = [bout0] * 4
    bb_, bo_ = Buf("bin"), Buf("bout")
    bin_b = [bb_] * 4
    bout_b = [bo_] * 4
    gown = [nc.dram_tensor("gown%d" % i, [1024, 512], BF16) for i in range(4)]
    gown_b = [Buf("gown%d" % i) for i in range(4)]
    cid_cache = {}

    def cidx(eng):
        if "v" not in cid_cache:
            cid_cache["v"] = eng.snap((eng.partition_id() % 4) * 512, min_val=0, max_val=1536)
        return cid_cache["v"]

    def copy_out(st):
        sc.add("sp", lambda e, st=st: e.dma_start(out=gown[st][:, :], in_=bout0[:, bass.ds(cidx(e), 512)]),
               (bo_,), (gown_b[st],), dkey="go%d" % st)

    def sb(name, shape, dt=F32):
        return T(nc, es, name, shape, dt)

    banks = [T(nc, es, "ps%d" % i, [128, 512], F32, psum=True) for i in range(8)]
    POOLS = {"ALL": [0, 1, 2, 3, 4, 5, 6, 7], "A": [0, 1, 2, 7], "B": [3, 4, 5, 6]}
    pool_i = {"ALL": 0, "A": 0, "B": 0}
    cur_pool = ["ALL"]

    def ps():
        k = cur_pool[0]
        lst = POOLS[k]
        b = banks[lst[pool_i[k] % len(lst)]]
        pool_i[k] += 1
        return b

    def step(gen, pool):
        cur_pool[0] = pool
        try:
            next(gen)
            return True
        except StopIteration:
            return False
        finally:
            cur_pool[0] = "ALL"

    def interleave(g1, g2):
        a1, a2 = g1 is not None, g2 is not None
        while a1 or a2:
            if a1:
                a1 = step(g1, "A")
            if a2:
                a2 = step(g2, "B")

    kid = [0]

    def load_const(dst, src, eng="sp"):
        kid[0] += 1
        sc.add(eng, lambda e, d=dst, s=src: e.dma_start(out=d[:], in_=s), (), (dst.b,), dkey="c%d" % kid[0])

    normw = sb("normw", [128, 8]); load_const(normw, normw_d)
    convw = sb("convw", [128, 12]); load_const(convw, convw_d)
    alog = sb("alog", [128, 1]); load_const(alog, alog_d)
    dtb = sb("dtb", [128, 1]); load_const(dtb, dtb_d)
    gnw = sb("gnw", [128, 64]); load_const(gnw, gnw_d)
    perm = sb("perm", [128, 128], BF16); load_const(perm, perm_d)
    ident = sb("ident", [128, 128], BF16); load_const(ident, ident_d)
    onesbd = sb("onesbd", [128, 128], BF16); load_const(onesbd, onesbd_d)
    onesbdf = sb("onesbdf", [128, 128]); load_const(onesbdf, onesbdf_d)
    ones = sb("ones", [128, 128], BF16); load_const(ones, ones_d)
    lbd = sb("lbd", [128, 128]); load_const(lbd, lbd_d)
    uaug = sb("uaug", [128, 130]); load_const(uaug, uaug_d)
    slneg = sb("slneg", [128, 128]); load_const(slneg, slneg_d)
    li = sb("li", [128, 128]); load_const(li, li_d)
    dmask = sb("dmask", [128, 512], BF16); load_const(dmask, dmask_d)
    nalog = sb("nalog", [128, 1])
    sc.add("act", lambda e: e.activation(nalog[:], alog[:], AF.Exp), (alog.b,), (nalog.b,))
    sc.add("dve", lambda e: e.tensor_scalar(nalog[:], nalog[:], -1.0, None, ALU.mult), (nalog.b,), (nalog.b,))

    XR = 12
    xs = [sb("xs%d" % i, [128, TT]) for i in range(XR)]
    xs_i = [0]

    def ring():
        b = xs[xs_i[0] % XR]
        k = "x%d" % (xs_i[0] % XR)
        xs_i[0] += 1
        return b, k

    cast_i = [0]

    def cast_to(dst_ap, dst_b, src_ap, src_b, nchunk=None):
        cast_i[0] += 1
        if nchunk is None:
            if cast_i[0] % 2 == 0:
                sc.add("dve", lambda e: e.tensor_copy(dst_ap, src_ap), (src_b,), (dst_b,))
            else:
                sc.add("act", lambda e: e.copy(dst_ap, src_ap), (src_b,), (dst_b,))
        else:
            sca = normw[:, nchunk:nchunk + 1]
            if cast_i[0] % 2 == 0:
                sc.add("dve", lambda e: e.tensor_scalar(dst_ap, src_ap, sca, None, ALU.mult), (src_b, normw.b), (dst_b,))
            else:
                sc.add("act", lambda e: e.activation(dst_ap, src_ap, AF.Copy, scale=sca), (src_b, normw.b), (dst_b,))

    W1 = sb("W1", [128, 8, (NB1 - 1) * 128], BF16)
    W1b = [Buf("W1b%d" % i) for i in range(NB1 - 1)]
    for j in range((NB1 - 1) * 128 // TT):
        for c in range(8):
            stb, k = ring()
            sc.add("sp", lambda e, stb=stb, j=j, c=c: e.dma_start(out=stb[:, :], in_=w1[:, c, j * TT:(j + 1) * TT]),
                   (), (stb.b,), dkey=k)
            for bi in (2 * j, 2 * j + 1):
                cast_i[0] += (bi == 2 * j)
                dst_ap = W1[:, c, bi * 128:(bi + 1) * 128]
                src_ap = stb[:, (bi - 2 * j) * 128:(bi - 2 * j + 1) * 128]
                sca = normw[:, c:c + 1]
                if cast_i[0] % 2 == 0:
                    sc.add("dve", lambda e, dst_ap=dst_ap, src_ap=src_ap, sca=sca: e.tensor_scalar(dst_ap, src_ap, sca, None, ALU.mult),
                           (stb.b, normw.b), (W1b[bi],))
                else:
                    sc.add("act", lambda e, dst_ap=dst_ap, src_ap=src_ap, sca=sca: e.activation(dst_ap, src_ap, AF.Copy, scale=sca),
                           (stb.b, normw.b), (W1b[bi],))
    Wba = sb("Wba", [128, 8, 4], BF16)
    stb, k = ring()
    stv = stb[:, 0:32].rearrange("p (a b) -> p a b", a=8)
    sc.add("sp", lambda e, stv=stv: e.dma_start(out=stv, in_=w1[:, :, BA * 128:BA * 128 + 4]), (), (stb.b,), dkey=k)
    sc.add("dve", lambda e, stv=stv: e.tensor_tensor(Wba[:], stv, normw[:, :].unsqueeze(2).broadcast_to([128, 8, 4]), ALU.mult),
           (stb.b, normw.b), (Wba.b,))

    sqb = [sb("sq%d" % i, [128, TT], BF16) for i in range(3)]
    rstd = sb("rstd", [128, TT])
    hT2 = [sb("hT%d" % i, [128, 8, TT], BF16) for i in range(2)]
    for t_ in hT2:
        t_.cb = [Buf("%s_c%d" % (t_.b.n, c)) for c in range(8)]
    cur_hT = [hT2[0]]
    cosb = [sb("cos%d" % i, [128, TT]) for i in range(2)]
    sinb = [sb("sin%d" % i, [128, TT]) for i in range(2)]
    Kt = [[sb("Kt%d_%d" % (g, s), [128, ST], BF16) for s in range(2)] for g in range(3)]
    Qt = [sb("Qt%d" % g, [128, ST], BF16) for g in range(3)]
    Vt = [sb("Vt%d" % g, [128, ST], BF16) for g in range(3)]
    Vs = [[sb("Vs%d_%d" % (g, s), [128, 16, 128], BF16) for s in range(2)] for g in range(3)]
    ndacc = sb("ndacc", [128, 2, ST])
    zaS = sb("zaS", [128, ST], BF16)
    gaT = sb("gaT", [128, ST], BF16)
    gbT = sb("gbT", [128, ST], BF16)
    qraw = [sb("qraw%d" % i, [128, TT], BF16) for i in range(2)]
    rt1 = [sb("rt1_%d" % i, [128, TT]) for i in range(2)]
    rt2 = [sb("rt2_%d" % i, [128, TT]) for i in range(2)]
    Pt = [sb("Pt%d" % i, [128, 512], BF16) for i in range(3)]
    Qx = [sb("Qx%d" % i, [128, 256], BF16) for i in range(2)]
    xpad = [sb("xpad%d" % i, [128, TT + 3], BF16) for i in range(3)]
    cacc = [sb("cacc%d" % i, [128, TT]) for i in range(2)]
    diagw = sb("diagw", [128, 12, 128], BF16)
    gsq = sb("gsq", [128, TT], BF16)
    grn = sb("grn", [128, TT])
    Qbd2 = [sb("Qbd%d" % i, [128, NCH, 128], BF16) for i in range(2)]
    Kbd2 = [sb("Kbd%d" % i, [128, NCH, 128], BF16) for i in range(2)]
    Vbd2 = [sb("Vbd%d" % i, [128, NCH, 128], BF16) for i in range(2)]
    zbS2 = [sb("zbS%d" % i, [128, TT], BF16) for i in range(2)]
    batok = sb("batok", [128, TT // 128, 4])
    bastk2 = [sb("bastk%d" % i, [128, NCH, 2]) for i in range(2)]
    beta2 = [sb("beta%d" % i, [128, NCH]) for i in range(2)]
    gg2 = [sb("gg%d" % i, [128, NCH]) for i in range(2)]
    Rbd = sb("Rbd", [128, NCH, 128])
    E1 = sb("E1", [128, NCH, 128])
    MM2 = sb("MM2", [128, NCH, 128])
    expG = sb("expG", [128, NCH])
    glast = sb("glast", [128, NCH])
    decf = sb("decf", [128, NCH])
    kdsc = sb("kdsc", [128, NCH])
    bsc = sb("bsc", [128, NCH])
    Cb = [sb("Cb%d" % i, [128, NCH, 128], BF16) for i in range(2)]
    Bb = [sb("Bb%d" % i, [128, NCH, 128], BF16) for i in range(2)]
    Xb = [sb("Xb%d" % i, [128, NCH, 128], BF16) for i in range(2)]
    ufin = sb("ufin", [128, NCH, 64])
    Wbd2 = sb("Wbd2", [128, NCH, 128], BF16)
    Wt = sb("Wt", [128, NCH, 128], BF16)
    attn = sb("attn", [128, NCH, 128], BF16)
    attnT = sb("attnT", [128, NCH, 128], BF16)
    Kdec = sb("Kdec", [128, NCH, 128], BF16)
    Sst = sb("Sst", [128, 64])
    Sbf = sb("Sbf", [128, 64], BF16)
    vnew = sb("vnew", [128, 64], BF16)
    oB = sb("oB", [128, 64])
    otile = sb("otile", [128, NCH, 64])
    oss = sb("oss", [128, NCH])
    onbd = sb("onbd", [128, NCH, 128], BF16)

    for k_ in range(12):
        sc.add("dve", lambda e, k_=k_: e.tensor_scalar(diagw[:, k_, :], ident[:], convw[:, k_:k_ + 1], None, ALU.mult),
               (ident.b, convw.b), (diagw.b,))
    for t_ in Qx:
        sc.add("pool", lambda e, t_=t_: e.memset(t_[:], 0.0), (), (t_.b,))
    for t_, eng in ((Sst, "dve"), (Sbf, "dve"), (Wbd2, "pool"), (onbd, "pool"), (Qbd2[0], "pool"), (Kbd2[0], "pool"),
                    (Vbd2[0], "pool"), (Qbd2[1], "pool"), (Kbd2[1], "pool"), (Vbd2[1], "pool"), (xpad[0], "dve"), (xpad[1], "dve"), (xpad[2], "dve")):
        sc.add(eng, lambda e, t_=t_: e.memset(t_[:], 0.0), (), (t_.b,))

    def rmsnorm_tile(src_dram, col0, hdst):
        bufs = []
        for c in range(8):
            xb = xs[xs_i[0] % XR]
            k = "x%d" % (xs_i[0] % XR)
            xs_i[0] += 1
            sc.add("sp", lambda e, xb=xb, c=c: e.dma_start(out=xb[:], in_=src_dram[:, c, col0:col0 + TT]),
                   (), (xb.b,), dkey=k)
            bufs.append(xb)
        pss = ps()
        for c in range(8):
            sq = sqb[c % 3]
            sc.add("act", lambda e, sq=sq, xb=bufs[c]: e.activation(sq[:], xb[:], AF.Square), (bufs[c].b,), (sq.b,))
            sc.add("pe", lambda e, sq=sq, c=c: e.matmul(pss[:, 0:TT], ones[:], sq[:], start=(c == 0), stop=(c == 7)),
                   (sq.b, ones.b), (pss.b,))
        sc.add("act", lambda e: e.activation(rstd[:], pss[:, 0:TT], AF.Ln, bias=EPS, scale=1.0 / 1024.0),
               (pss.b,), (rstd.b,))
        sc.add("act", lambda e: e.activation(rstd[:], rstd[:], AF.Exp, scale=-0.5), (rstd.b,), (rstd.b,))
        for c in range(8):
            sc.add("dve" if c % 2 == 0 else "pool", lambda e, c=c, xb=bufs[c]: e.tensor_tensor(
                hdst[:, c, :], xb[:], rstd[:], ALU.mult), (bufs[c].b, rstd.b), (hdst.cb[c],))

    def proj_fm(blk):
        p = ps()
        hT = cur_hT[0]
        for c in range(8):
            sc.add("pe", lambda e, c=c, p=p, hT=hT: e.matmul(p[:, 0:TT], W1[:, c, blk * 128:(blk + 1) * 128], hT[:, c, :],
                                                            start=(c == 0), stop=(c == 7)), (W1b[blk], hT.cb[c]), (p.b,))
        return p

    stmp = [sb("stmp%d" % i, [128, TT]) for i in range(2)]
    stmp_i = [0]

    def sigmoid_to(src_ap, src_b, shape_cols=TT):
        t = stmp[stmp_i[0] % 2]
        stmp_i[0] += 1
        tv = t[:, 0:shape_cols]
        sc.add("act", lambda e: e.activation(tv, src_ap, AF.Exp, scale=-1.0), (src_b,), (t.b,))
        sc.add("act", lambda e: e.activation(tv, tv, AF.Ln, bias=1.0), (t.b,), (t.b,))
        sc.add("act", lambda e: e.activation(tv, tv, AF.Exp, scale=-1.0), (t.b,), (t.b,))
        return t

    def perm_view(t_ap_tile, g, tt):
        d = DIL[g]
        n = TT // d
        v = t_ap_tile[:, :].rearrange("p (r m) -> p r m", r=d)
        return v[:, :, tt * n:(tt + 1) * n]

    def src_view(ap, g):
        d = DIL[g]
        return ap.rearrange("p (m r) -> p r m", r=d)

    rope_i = [0]

    def prenorm(ti):
        col0 = ti * TT
        cb, sb_ = cosb[ti % 2], sinb[ti % 2]
        sc.add("sp", lambda e: e.dma_start(out=cb[:], in_=cos_d[:, col0:col0 + TT]), (), (cb.b,), dkey="cos%d" % (ti % 2))
        sc.add("sp", lambda e: e.dma_start(out=sb_[:], in_=sin_d[:, col0:col0 + TT]), (), (sb_.b,), dkey="sin%d" % (ti % 2))
        rmsnorm_tile(xT, col0, hT2[ti % 2])

    prenorm(0)
    for st in range(4):
        slot = st % 2
        def tileA(st, slot, tt):
            ti = st * TPS + tt
            col0 = ti * TT
            par = ti % 2
            Qbd, Kbd, Vbd, zbS, bastk, beta, gg = Qbd2[par], Kbd2[par], Vbd2[par], zbS2[par], bastk2[par], beta2[par], gg2[par]
            cb, sb_ = cosb[ti % 2], sinb[ti % 2]
            hT = hT2[ti % 2]
            cur_hT[0] = hT

            def dsa_qk(g, qk):
                p = proj_fm(3 * g + qk)
                qr = qraw[rope_i[0] % 2]
                t1 = rt1[rope_i[0] % 2]
                t2 = rt2[rope_i[0] % 2]
                rope_i[0] += 1
                sc.add("act", lambda e, p=p, qr=qr: e.copy(qr[:], p[:, 0:TT]), (p.b,), (qr.b,))
                p2 = ps()
                sc.add("pe", lambda e, p2=p2, qr=qr: e.matmul(p2[:, 0:TT], perm[:], qr[:], start=True, stop=True),
                       (perm.b, qr.b), (p2.b,))
                sc.add("dve", lambda e, p=p, t1=t1, cb=cb: e.tensor_tensor(t1[:], p[:, 0:TT], cb[:], ALU.mult),
                       (p.b, cb.b), (t1.b,))
                sc.add("dve", lambda e, p2=p2, t2=t2, sb_=sb_: e.tensor_tensor(t2[:], p2[:, 0:TT], sb_[:], ALU.mult),
                       (p2.b, sb_.b), (t2.b,))
                dst = Qt[g] if qk == 0 else Kt[g][slot]
                sc.add("pool", lambda e, dst=dst, t1=t1, t2=t2, g=g, tt=tt: e.tensor_tensor(
                    perm_view(dst, g, tt), src_view(t1[:, :], g), src_view(t2[:, :], g), ALU.add),
                    (t1.b, t2.b), (dst.b,))

            def dsa_v(g):
                p = proj_fm(3 * g + 2)
                sc.add("act", lambda e, p=p, g=g, tt=tt: e.copy(perm_view(Vt[g], g, tt), src_view(p[:, 0:TT], g)),
                       (p.b,), (Vt[g].b,))

            def z_a():
                p = proj_fm(ZA)
                sg = sigmoid_to(p[:, 0:TT], p.b)
                sc.add("dve", lambda e, p=p, tt=tt, sg=sg: e.tensor_tensor(zaS[:, tt * TT:(tt + 1) * TT], p[:, 0:TT], sg[:], ALU.mult),
                       (p.b, sg.b), (zaS.b,))

            def z_b():
                p = proj_fm(ZB)
                sg = sigmoid_to(p[:, 0:TT], p.b)
                sc.add("dve", lambda e, p=p, sg=sg: e.tensor_tensor(zbS[:], p[:, 0:TT], sg[:], ALU.mult), (p.b, sg.b), (zbS.b,))

            def gdn_in(j):
                blk = (GQ, GK, GV)[j]
                p = proj_fm(blk)
                xp = xpad[j]
                ca = cacc[j % 2]
                sc.add("act", lambda e, xp=xp: e.copy(xp[:, 0:3], xp[:, TT:TT + 3]), (xp.b,), (xp.b,))
                sc.add("act", lambda e, xp=xp, p=p: e.copy(xp[:, 3:TT + 3], p[:, 0:TT]), (p.b, xp.b), (xp.b,))
                pc = ps()
                for tap in range(4):
                    sc.add("pe", lambda e, xp=xp, pc=pc, j=j, tap=tap: e.matmul(
                        pc[:, 0:TT], diagw[:, j * 4 + tap, :], xp[:, tap:tap + TT], start=(tap == 0), stop=(tap == 3)),
                        (diagw.b, xp.b), (pc.b,))
                if j < 2:
                    sg = sigmoid_to(pc[:, 0:TT], pc.b)
                    sc.add("dve", lambda e, ca=ca, pc=pc, sg=sg: e.tensor_tensor(ca[:], pc[:, 0:TT], sg[:], ALU.mult), (pc.b, sg.b), (ca.b,))
                    sc.add("act", lambda e, ca=ca: e.activation(gsq[:], ca[:], AF.Square), (ca.b,), (gsq.b,))
                    p3 = ps()
                    sc.add("pe", lambda e, p3=p3: e.matmul(p3[:, 0:TT], onesbd[:], gsq[:], start=True, stop=True),
                           (onesbd.b, gsq.b), (p3.b,))
                    scl = 64.0 if j == 0 else 1.0
                    sc.add("act", lambda e, p3=p3, scl=scl: e.activation(grn[:], p3[:, 0:TT], AF.Ln, bias=EPS * scl, scale=scl),
                           (p3.b,), (grn.b,))
                    sc.add("act", lambda e: e.activation(grn[:], grn[:], AF.Exp, scale=-0.5), (grn.b,), (grn.b,))
                    dstb = Qbd if j == 0 else Kbd
                    for h in range(2):
                        hs = slice(h * 64, (h + 1) * 64)
                        sc.add("dve", lambda e, dstb=dstb, hs=hs, h=h, ca=ca: e.tensor_tensor(
                            dstb[hs, :, h * 64:(h + 1) * 64], ca[hs, :].rearrange("p (c i) -> p c i", i=64),
                            grn[hs, :].rearrange("p (c i) -> p c i", i=64), ALU.mult), (ca.b, grn.b), (dstb.b,))
                else:
                    sg = sigmoid_to(pc[:, 0:TT], pc.b)
                    for h in range(2):
                        hs = slice(h * 64, (h + 1) * 64)
                        sc.add("dve", lambda e, hs=hs, h=h, pc=pc, sg=sg: e.tensor_tensor(
                            Vbd[hs, :, h * 64:(h + 1) * 64], pc[hs, 0:TT].rearrange("p (c i) -> p c i", i=64),
                            sg[hs, :].rearrange("p (c i) -> p c i", i=64), ALU.mult), (pc.b, sg.b), (Vbd.b,))

            def beta_part():
                pb = ps()
                for tb in range(TT // 128):
                    for c in range(8):
                        sc.add("pe", lambda e, tb=tb, c=c, pb=pb: e.matmul(
                            pb[:, tb * 4:(tb + 1) * 4], hT[:, c, tb * 128:(tb + 1) * 128], Wba[:, c, :],
                            start=(c == 0), stop=(c == 7)), (hT.cb[c], Wba.b), (pb.b,))
                sc.add("dve", lambda e, pb=pb: e.tensor_copy(batok[:], pb[:, 0:(TT // 128) * 4].rearrange("p (b k) -> p b k", k=4)),
                       (pb.b,), (batok.b,))
                for h in range(2):
                    for cp in range(2):
                        sc.add("sp", lambda e, h=h, cp=cp: e.dma_start(
                            out=bastk[h * 64:(h + 1) * 64, cp:NCH:2, :], in_=batok[cp * 64:(cp + 1) * 64, :, 2 * h:2 * h + 2],
                            allow_slow_non_contiguous=True), (batok.b,), (bastk.b,), dkey="ba%d" % par)
                sc.add("act", lambda e: e.activation(beta[:], bastk[:, :, 0], AF.Exp, scale=-1.0), (bastk.b,), (beta.b,))
                sc.add("act", lambda e: e.activation(beta[:], beta[:], AF.Ln, bias=1.0), (beta.b,), (beta.b,))
                sc.add("act", lambda e: e.activation(beta[:], beta[:], AF.Exp, scale=-1.0), (beta.b,), (beta.b,))
                sc.add("act", lambda e: e.activation(gg[:], bastk[:, :, 1], AF.Exp, bias=dtb[:]), (bastk.b, dtb.b), (gg.b,))
                sc.add("act", lambda e: e.activation(gg[:], gg[:], AF.Ln, bias=1.0), (gg.b,), (gg.b,))
                sc.add("dve", lambda e: e.tensor_scalar(gg[:], gg[:], nalog[:, 0:1], None, ALU.mult), (gg.b, nalog.b), (gg.b,))

            dsa_qk(0, 0); z_a(); yield
            dsa_qk(0, 1); z_b(); yield
            dsa_v(0); gdn_in(0); yield
            if ti + 1 < NTT:
                prenorm(ti + 1)
            yield
            dsa_qk(1, 0); yield
            dsa_qk(1, 1); gdn_in(1); yield
            dsa_v(1); yield
            dsa_qk(2, 0); gdn_in(2); yield
            dsa_qk(2, 1); beta_part(); yield
            dsa_v(2); yield

        def tileB(st, slot, tt):
            ti = st * TPS + tt
            par = ti % 2
            Qbd, Kbd, Vbd, zbS, bastk, beta, gg = Qbd2[par], Kbd2[par], Vbd2[par], zbS2[par], bastk2[par], beta2[par], gg2[par]
            pgl = ps()
            sc.add("pe", lambda e, pgl=pgl: e.matmul(pgl[:, 0:NCH], onesbdf[:], gg[:], start=True, stop=True),
                   (onesbdf.b, gg.b), (pgl.b,))
            sc.add("pe", lambda e, pgl=pgl: e.matmul(pgl[:, NCH:2 * NCH], lbd[:], gg[:], start=True, stop=True),
                   (lbd.b, gg.b), (pgl.b,))
            sc.add("act", lambda e, pgl=pgl: e.copy(glast[:], pgl[:, 0:NCH]), (pgl.b,), (glast.b,))
            sc.add("act", lambda e, pgl=pgl: e.activation(decf[:], pgl[:, 0:NCH], AF.Exp), (pgl.b,), (decf.b,))
            sc.add("act", lambda e, pgl=pgl: e.activation(expG[:], pgl[:, NCH:2 * NCH], AF.Exp), (pgl.b,), (expG.b,))
            sc.add("dve", lambda e, pgl=pgl: e.tensor_tensor(kdsc[:], glast[:], pgl[:, NCH:2 * NCH], ALU.subtract),
                   (glast.b, pgl.b), (kdsc.b,))
            sc.add("act", lambda e: e.activation(kdsc[:], kdsc[:], AF.Exp), (kdsc.b,), (kdsc.b,))
            sc.add("dve", lambda e: e.tensor_tensor(bsc[:], beta[:], expG[:], ALU.mult), (beta.b, expG.b), (bsc.b,))
            sc.add("dve", lambda e: e.tensor_tensor(
                Rbd[:], uaug[:, 0:128].unsqueeze(1).broadcast_to([128, NCH, 128]),
                gg[:, :].unsqueeze(2).broadcast_to([128, NCH, 128]), ALU.mult), (uaug.b, gg.b), (Rbd.b,))
            pD = ps()
            for c in range(NCH):
                sc.add("pe", lambda e, c=c, pD=pD: e.matmul(pD[:, c * 128:(c + 1) * 128], lbd[:], Rbd[:, c, :], start=True, stop=True),
                       (lbd.b, Rbd.b), (pD.b,))
            sc.add("act", lambda e, pD=pD: e.activation(E1[:].rearrange("p c n -> p (c n)"), pD[:, 0:NCH * 128], AF.Exp), (pD.b,), (E1.b,))
            yield
            pKK = ps()
            pQK = ps()
            for c in range(NCH):
                sc.add("pe", lambda e, c=c, pKK=pKK: e.matmul(pKK[:, c * 128:(c + 1) * 128], Kbd[:, c, :], Kbd[:, c, :], start=True, stop=True),
                       (Kbd.b,), (pKK.b,))
            for c in range(NCH):
                sc.add("pe", lambda e, c=c, pQK=pQK: e.matmul(pQK[:, c * 128:(c + 1) * 128], Qbd[:, c, :], Kbd[:, c, :], start=True, stop=True),
                       (Qbd.b, Kbd.b), (pQK.b,))
            sc.add("dve", lambda e: e.tensor_tensor(MM2[:], E1[:], beta[:, :].unsqueeze(2).broadcast_to([128, NCH, 128]), ALU.mult),
                   (E1.b, beta.b), (MM2.b,))
            sc.add("pool", lambda e: e.tensor_tensor(MM2[:], MM2[:], slneg[:, :].unsqueeze(1).broadcast_to([128, NCH, 128]), ALU.mult),
                   (MM2.b, slneg.b), (MM2.b,))
            sc.add("dve", lambda e, pKK=pKK: e.tensor_tensor(Cb[0][:].rearrange("p c n -> p (c n)"), pKK[:, 0:NCH * 128],
                                                           MM2[:].rearrange("p c n -> p (c n)"), ALU.mult), (pKK.b, MM2.b), (Cb[0].b,))
            sc.add("pool", lambda e: e.tensor_tensor(E1[:], E1[:], li[:, :].unsqueeze(1).broadcast_to([128, NCH, 128]), ALU.mult),
                   (E1.b, li.b), (E1.b,))
            sc.add("dve", lambda e, pQK=pQK: e.tensor_tensor(attn[:].rearrange("p c n -> p (c n)"), pQK[:, 0:NCH * 128],
                                                           E1[:].rearrange("p c n -> p (c n)"), ALU.mult), (pQK.b, E1.b), (attn.b,))
            yield
            pT1 = ps(); pT2 = ps(); pT3 = ps(); pT4 = ps()
            for c in range(NCH):
                for (pt, src_) in ((pT1, Cb[0]), (pT2, attn), (pT3, Kbd), (pT4, Vbd)):
                    sc.add("pe", lambda e, c=c, pt=pt, src_=src_: e.transpose(
                        pt[:].bitcast(BF16)[:, c * 128:(c + 1) * 128], src_[:, c, :], ident[:]), (src_.b, ident.b), (pt.b,))
            sc.add("act", lambda e: e.copy(Bb[0][:].rearrange("p c n -> p (c n)"), pT1[:].bitcast(BF16)[:, 0:NCH * 128]), (pT1.b,), (Bb[0].b,))
            sc.add("act", lambda e: e.copy(attnT[:].rearrange("p c n -> p (c n)"), pT2[:].bitcast(BF16)[:, 0:NCH * 128]), (pT2.b,), (attnT.b,))
            pT3v = pT3[:].bitcast(BF16)[:, 0:NCH * 128].rearrange("p (c n) -> p c n", n=128)
            pT4v = pT4[:].bitcast(BF16)[:, 0:NCH * 128].rearrange("p (c n) -> p c n", n=128)
            sc.add("dve", lambda e, pT3v=pT3v: e.tensor_tensor(Kdec[:], pT3v, kdsc[:, :].unsqueeze(2).broadcast_to([128, NCH, 128]), ALU.mult),
                   (pT3.b, kdsc.b), (Kdec.b,))
            for h in range(2):
                hs = slice(h * 64, (h + 1) * 64)
                sc.add("dve", lambda e, h=h, hs=hs, pT3v=pT3v: e.tensor_tensor(
                    Xb[0][hs, :, 64:128], pT3v[hs, :, h * 64:(h + 1) * 64], bsc[hs, :].unsqueeze(2).broadcast_to([64, NCH, 64]), ALU.mult),
                    (pT3.b, bsc.b), (Xb[0].b,))
                sc.add("dve", lambda e, h=h, hs=hs, pT4v=pT4v: e.tensor_tensor(
                    Xb[0][hs, :, 0:64], pT4v[hs, :, h * 64:(h + 1) * 64], beta[hs, :].unsqueeze(2).broadcast_to([64, NCH, 64]), ALU.mult),
                    (pT4.b, beta.b), (Xb[0].b,))
            yield
            cur = 0
            for lvl in range(6):
                yield
                Bc, Cc, Xc = Bb[cur], Cb[cur], Xb[cur]
                Bn, Cn, Xn = Bb[1 - cur], Cb[1 - cur], Xb[1 - cur]
                pX = ps()
                for c in range(NCH):
                    sc.add("pe", lambda e, c=c, pX=pX, Bc=Bc, Xc=Xc: e.matmul(pX[:, c * 128:(c + 1) * 128], Bc[:, c, :], Xc[:, c, :], start=True, stop=True),
                           (Bc.b, Xc.b), (pX.b,))
                if lvl < 5:
                    sc.add("dve", lambda e, pX=pX, Xc=Xc, Xn=Xn: e.tensor_tensor(
                        Xn[:].rearrange("p c n -> p (c n)"), pX[:, 0:NCH * 128], Xc[:].rearrange("p c n -> p (c n)"), ALU.add),
                        (pX.b, Xc.b), (Xn.b,))
                    pB = ps()
                    for c in range(NCH):
                        sc.add("pe", lambda e, c=c, pB=pB, Bc=Bc, Cc=Cc: e.matmul(pB[:, c * 128:(c + 1) * 128], Cc[:, c, :], Bc[:, c, :], start=True, stop=True),
                               (Bc.b, Cc.b), (pB.b,))
                    sc.add("act", lambda e, pB=pB, Bn=Bn: e.copy(Bn[:].rearrange("p c n -> p (c n)"), pB[:, 0:NCH * 128]), (pB.b,), (Bn.b,))
                    if lvl < 4:
                        pC = ps()
                        for c in range(NCH):
                            sc.add("pe", lambda e, c=c, pC=pC, Bc=Bc, Cc=Cc: e.matmul(pC[:, c * 128:(c + 1) * 128], Bc[:, c, :], Cc[:, c, :], start=True, stop=True),
                                   (Bc.b, Cc.b), (pC.b,))
                        sc.add("act", lambda e, pC=pC, Cn=Cn: e.copy(Cn[:].rearrange("p c n -> p (c n)"), pC[:, 0:NCH * 128]), (pC.b,), (Cn.b,))
                    cur = 1 - cur
                else:
                    pXv = pX[:, 0:NCH * 128].rearrange("p (c n) -> p c n", n=128)
                    sc.add("dve", lambda e, pXv=pXv, Xc=Xc: e.tensor_tensor(ufin[:], pXv[:, :, 0:64], Xc[:, :, 0:64], ALU.add),
                           (pX.b, Xc.b), (ufin.b,))
                    for h in range(2):
                        hs = slice(h * 64, (h + 1) * 64)
                        sc.add("dve", lambda e, pXv=pXv, Xc=Xc, hs=hs, h=h: e.tensor_tensor(
                            Wbd2[hs, :, h * 64:(h + 1) * 64], pXv[hs, :, 64:128], Xc[hs, :, 64:128], ALU.add),
                            (pX.b, Xc.b), (Wbd2.b,))
            pT5 = ps()
            for c in range(NCH):
                sc.add("pe", lambda e, c=c, pT5=pT5: e.transpose(pT5[:].bitcast(BF16)[:, c * 128:(c + 1) * 128], Wbd2[:, c, :], ident[:]),
                       (Wbd2.b, ident.b), (pT5.b,))
            sc.add("act", lambda e, pT5=pT5: e.copy(Wt[:].rearrange("p c n -> p (c n)"), pT5[:].bitcast(BF16)[:, 0:NCH * 128]), (pT5.b,), (Wt.b,))
            yield
            for c in range(NCH):
                yield
                pw = ps()
                sc.add("pe", lambda e, c=c, pw=pw: e.matmul(pw[:, 0:64], Wt[:, c, :], Sbf[:], start=True, stop=True), (Wt.b, Sbf.b), (pw.b,))
                sc.add("dve", lambda e, c=c, pw=pw: e.tensor_tensor(vnew[:], ufin[:, c, :], pw[:, 0:64], ALU.subtract), (ufin.b, pw.b), (vnew.b,))
                po = ps()
                sc.add("pe", lambda e, c=c, po=po: e.matmul(po[:, 0:64], Qbd[:, c, :], Sbf[:], start=True, stop=True), (Qbd.b, Sbf.b), (po.b,))
                sc.add("pe", lambda e, c=c, po=po: e.matmul(po[:, 64:128], attnT[:, c, :], vnew[:], start=True, stop=True), (attnT.b, vnew.b), (po.b,))
                sc.add("pe", lambda e, c=c, po=po: e.matmul(po[:, 128:192], Kdec[:, c, :], vnew[:], start=True, stop=True), (Kdec.b, vnew.b), (po.b,))
                sc.add("dve", lambda e, c=c, po=po: e.scalar_tensor_tensor(Sst[:], Sst[:], decf[:, c:c + 1], po[:, 128:192], ALU.mult, ALU.add),
                       (Sst.b, decf.b, po.b), (Sst.b,))
                sc.add("act", lambda e: e.copy(Sbf[:], Sst[:]), (Sst.b,), (Sbf.b,))
                sc.add("act", lambda e, po=po: e.copy(oB[:], po[:, 64:128]), (po.b,), (oB.b,))
                sc.add("dve", lambda e, c=c, po=po: e.scalar_tensor_tensor(otile[:, c, :], po[:, 0:64], expG[:, c:c + 1], oB[:], ALU.mult, ALU.add),
                       (po.b, expG.b, oB.b), (otile.b,))
            yield
            sc.add("pool", lambda e: e.tensor_tensor(Rbd[:, :, 0:64], otile[:], otile[:], ALU.mult), (otile.b,), (Rbd.b,))
            sc.add("dve", lambda e: e.tensor_reduce(oss[:], Rbd[:, :, 0:64], AX.X, ALU.add), (Rbd.b,), (oss.b,))
            sc.add("act", lambda e: e.activation(oss[:], oss[:], AF.Ln, bias=EPS, scale=1.0 / 64.0), (oss.b,), (oss.b,))
            sc.add("act", lambda e: e.activation(oss[:], oss[:], AF.Exp, scale=-0.5), (oss.b,), (oss.b,))
            sc.add("dve", lambda e: e.tensor_tensor(otile[:], otile[:], oss[:, :].unsqueeze(2).broadcast_to([128, NCH, 64]), ALU.mult),
                   (otile.b, oss.b), (otile.b,))
            sc.add("pool", lambda e: e.tensor_tensor(otile[:], otile[:], gnw[:, :].unsqueeze(1).broadcast_to([128, NCH, 64]), ALU.mult),
                   (otile.b, gnw.b), (otile.b,))
            for h in range(2):
                hs = slice(h * 64, (h + 1) * 64)
                sc.add("act", lambda e, hs=hs, h=h: e.copy(onbd[hs, :, h * 64:(h + 1) * 64], otile[hs, :, :]), (otile.b,), (onbd.b,))
            pT6 = ps()
            for c in range(NCH):
                sc.add("pe", lambda e, c=c, pT6=pT6: e.transpose(pT6[:].bitcast(BF16)[:, c * 128:(c + 1) * 128], onbd[:, c, :], ident[:]),
                       (onbd.b, ident.b), (pT6.b,))
            for h in range(2):
                hs = slice(h * 64, (h + 1) * 64)
                sc.add("dve", lambda e, hs=hs, h=h, tt=tt, pT6=pT6: e.tensor_tensor(
                    gbT[hs, tt * TT:(tt + 1) * TT].rearrange("p (c i) -> p c i", i=64),
                    pT6[:].bitcast(BF16)[hs, 0:NCH * 128].rearrange("p (c n) -> p c n", n=128)[:, :, h * 64:(h + 1) * 64],
                    zbS[hs, :].rearrange("p (c i) -> p c i", i=64), ALU.mult), (pT6.b, zbS.b), (gbT.b,))
        prevB = None
        for tt in range(TPS):
            interleave(tileA(st, slot, tt), prevB)
            prevB = tileB(st, slot, tt)

        def w2_stage_pieces(stage):
            def flat(t_):
                return t_[:].rearrange("p b n -> p (b n)") if len(t_[:].shape) == 3 else t_[:]
            def gate(t_, c):
                return [(t_, flat(t_)[:, j * TT:(j + 1) * TT], wg[:, c, j * TT:(j + 1) * TT], c) for j in range(2048 // TT)]
            def two(t_, src, i):
                return [(t_, flat(t_)[:, (c % 2) * 1024 + j * TT:(c % 2) * 1024 + (j + 1) * TT], src[:, c, j * TT:(j + 1) * TT], None)
                        for c in (2 * i, 2 * i + 1) for j in range(1024 // TT)]
            if stage == 0:
                return two(Vt[0], wua, 1) + two(Vt[1], wub, 0) + two(Vt[2], wub, 1)
            if stage == 1:
                return gate(Kt[0][0], 0) + gate(Kt[0][1], 1) + gate(Qt[0], 6) + two(Vs[0][0], wo, 0) + two(Vs[0][1], wo, 1)
            if stage == 2:
                return gate(Kt[1][0], 2) + gate(Kt[1][1], 3) + gate(Qt[1], 7) + two(Vs[1][0], wo, 2) + two(Vs[1][1], wo, 3)
            return gate(Kt[2][0], 4) + gate(Kt[2][1], 5) + two(Qt[2], wua, 0)

        def emit_pieces(q, n):
            for _ in range(min(n, len(q))):
                t_, dst_ap, src_ap, nchunk = q.pop(0)
                stb, k = ring()
                sc.add("sp", lambda e, stb=stb, src_ap=src_ap: e.dma_start(out=stb[:, :], in_=src_ap), (), (stb.b,), dkey=k)
                cast_to(dst_ap, t_.b, stb[:, :], stb.b, nchunk)

        def attention(st, slot):
            wq = []
            for g in range(3):
                for q4 in range(4):
                    pv = ps()
                    for j in range(4):
                        blk = q4 * 4 + j
                        sc.add("pe", lambda e, pv=pv, j=j, blk=blk, g=g: e.transpose(
                            pv[:].bitcast(BF16)[:, j * 128:(j + 1) * 128], Vt[g][:, blk * 128:(blk + 1) * 128], ident[:]),
                            (Vt[g].b, ident.b), (pv.b,))
                    sc.add("act", lambda e, pv=pv, q4=q4, g=g, slot=slot: e.copy(
                        Vs[g][slot][:, q4 * 4:(q4 + 1) * 4, :].rearrange("p b n -> p (b n)"), pv[:].bitcast(BF16)[:, 0:512]),
                        (pv.b,), (Vs[g][slot].b,))
                    yield
            units = [(g, blk) for g in range(3) for blk in range(16)]
            if st == 3:
                wq.extend(w2_stage_pieces(0))

            def stage_s(u):
                g, blk = units[u]
                d = DIL[g]
                nps = 16 // d
                r, n = blk // nps, blk % nps
                halves = []
                if n > 0:
                    halves.append((0, slot, blk - 1))
                elif st > 0:
                    halves.append((0, 1 - slot, r * nps + nps - 1))
                halves.append((1, slot, blk))
                nh = len(halves)
                pS = ps()
                Pb = Pt[u % 3]
                qx = Qx[u % 2]
                sc.add("pool", lambda e, qx=qx, g=g, blk=blk: e.tensor_copy(qx[0:64, 0:128], Qt[g][0:64, blk * 128:(blk + 1) * 128]),
                       (Qt[g].b,), (qx.b,))
                sc.add("act", lambda e, qx=qx, g=g, blk=blk: e.copy(qx[64:128, 128:256], Qt[g][64:128, blk * 128:(blk + 1) * 128]),
                       (Qt[g].b,), (qx.b,))
                for (hf, sl, kb) in halves:
                    sc.add("pe", lambda e, hf=hf, sl=sl, kb=kb, g=g, pS=pS, qx=qx: e.matmul(
                        pS[:, hf * 256:(hf + 1) * 256], Kt[g][sl][:, kb * 128:(kb + 1) * 128], qx[:, :],
                        start=True, stop=True), (Kt[g][sl].b, qx.b), (pS.b,))
                lo = halves[0][0] * 256
                sc.add("act", lambda e, lo=lo, Pb=Pb, pS=pS: e.activation(
                    Pb[:, lo:512], pS[:, lo:512], AF.Exp, scale=0.125), (pS.b,), (Pb.b,))
                sc.add("dve", lambda e, Pb=Pb, lo=lo: e.tensor_tensor(Pb[:, lo:512], Pb[:, lo:512], dmask[:, lo:512], ALU.mult),
                       (Pb.b, dmask.b), (Pb.b,))
                return (g, d, r, n, halves, nh, Pb)

            def stage_pv(ctx):
                g, d, r, n, halves, nh, Pb = ctx
                pO = ps()
                for h in range(2):
                    hs = slice(h * 64, (h + 1) * 64)
                    for k, (hf, sl, kb) in enumerate(halves):
                        sc.add("pe", lambda e, h=h, hs=hs, hf=hf, sl=sl, kb=kb, k=k, g=g, Pb=Pb, pO=pO, nh=nh: e.matmul(
                            pO[hs, 0:128], Vs[g][sl][:, kb, h * 64:(h + 1) * 64], Pb[:, hf * 256 + h * 128:hf * 256 + (h + 1) * 128],
                            start=(k == 0), stop=(k == nh - 1)), (Vs[g][sl].b, Pb.b), (pO.b,))
                    for k, (hf, sl, kb) in enumerate(halves):
                        sc.add("pe", lambda e, h=h, hs=hs, hf=hf, k=k, Pb=Pb, pO=pO, nh=nh: e.matmul(
                            pO[hs, 128:256], ones[:, 0:64], Pb[:, hf * 256 + h * 128:hf * 256 + (h + 1) * 128],
                            start=(k == 0), stop=(k == nh - 1)), (ones.b, Pb.b), (pO.b,))
                off = n * 128 * d + r
                dstv = ndacc[:, :, off:off + 127 * d + 1:d] if d > 1 else ndacc[:, :, off:off + 128]
                srcv = pO[:, 0:256].rearrange("p (a q) -> p a q", a=2)
                if g == 0:
                    sc.add("act", lambda e, dstv=dstv, srcv=srcv: e.copy(dstv, srcv), (pO.b,), (ndacc.b,))
                else:
                    sc.add("dve", lambda e, dstv=dstv, srcv=srcv: e.tensor_tensor(dstv, srcv, dstv, ALU.add), (pO.b, ndacc.b), (ndacc.b,))

            ctx = stage_s(0)
            for u in range(len(units)):
                nxt = stage_s(u + 1) if u + 1 < len(units) else None
                stage_pv(ctx)
                ctx = nxt
                if st == 3:
                    if u == 15:
                        wq.extend(w2_stage_pieces(1))
                    if u == 31:
                        wq.extend(w2_stage_pieces(2))
                    emit_pieces(wq, 3)
                yield
            sc.add("act", lambda e: e.activation(ndacc[:, 1, :], ndacc[:, 1, :], AF.Ln), (ndacc.b,), (ndacc.b,))
            sc.add("act", lambda e: e.activation(ndacc[:, 1, :], ndacc[:, 1, :], AF.Exp, scale=-1.0), (ndacc.b,), (ndacc.b,))
            sc.add("dve", lambda e: e.tensor_tensor(ndacc[:, 0, :], ndacc[:, 0, :], ndacc[:, 1, :], ALU.mult), (ndacc.b,), (ndacc.b,))
            sc.add("pool", lambda e: e.tensor_tensor(gaT[:], ndacc[:, 0, :], zaS[:], ALU.mult), (ndacc.b, zaS.b), (gaT.b,))
            if st == 3:
                wq.extend(w2_stage_pieces(3))
                emit_pieces(wq, len(wq))
            yield

        interleave(attention(st, slot), prevB)
        if st > 0:
            copy_out(st - 1)
        sc.add("pool", lambda e, st=st: e.dma_start(out=bin_[st][0:128, :], in_=gaT[:]), (gaT.b,), (bin_b[st],), dkey="bi")
        sc.add("pool", lambda e, st=st: e.dma_start(out=bin_[st][128:256, :], in_=gbT[:]), (gbT.b,), (bin_b[st],), dkey="bi")
        sc.add("pool", lambda e, st=st: e.collective_compute(
            "AllGather", ALU.bypass, replica_groups=[[0, 1, 2, 3], [4, 5, 6, 7]], ins=[bin_[st][:, :]], outs=[bout[st][:, :]]),
            (bin_b[st],), (bout_b[st],), dkey="cc%d" % st)

    copy_out(3)
    es2 = es
    Wg = T.__new__(T); Wg.b = Buf("Wg")
    def alias(src, shape_str, dt=None, **kw):
        ap = src[:]
        if dt is not None:
            ap = ap.bitcast(dt)
        return ap

    class A:
        def __init__(self, ap, name, base=None, share=False):
            self.ap = ap
            if share:
                self.b = base.b
            else:
                self.b = Buf(name)
                if base is not None:
                    sc.link(self.b, base.b)

        def __getitem__(self, k):
            return self.ap[k]

    wg_parts = [Kt[0][0], Kt[0][1], Kt[1][0], Kt[1][1], Kt[2][0], Kt[2][1], Qt[0], Qt[1]]
    Wgc = [A(p_[:], "Wg%d" % i, p_, True) for i, p_ in enumerate(wg_parts)]
    Wua = [A(t_[:], "wua%d" % i, t_, True) for i, t_ in enumerate((Qt[2], Vt[0]))]
    Wub = [A(t_[:], "wub%d" % i, t_, True) for i, t_ in enumerate((Vt[1], Vt[2]))]
    wo_parts = [Vs[0][0], Vs[0][1], Vs[1][0], Vs[1][1]]
    Woc = [A(t_[:].rearrange("p b n -> p (b n)"), "wo%d" % i, t_, True) for i, t_ in enumerate(wo_parts)]
    fnw = A(ndacc[:, 0, 0:1024], "fnw", ndacc)
    xown = [A(ndacc[:, 1, 0:1024], "xown0", ndacc), A(ndacc[:, 1, 1024:2048], "xown1", ndacc)]
    ybuf = A(ndacc[:, 0, 1024:2048], "ybuf", ndacc)
    ga_g = A(Vs[2][0][:].rearrange("p b n -> p (b n)")[:, 0:4 * 512].rearrange("p (r t) -> p r t", r=4), "ga_g", Vs[2][0])
    gb_g = A(Vs[2][1][:].rearrange("p b n -> p (b n)")[:, 0:4 * 512].rearrange("p (r t) -> p r t", r=4), "gb_g", Vs[2][1])
    hTo = hT2[0]
    cur_hT[0] = hTo
    sgA = rt1
    sgB = rt2
    merged = A(gbT[:].rearrange("p (c t) -> p c t", c=8), "merged", gbT)
    ssq2 = A(oss[:, 0:1], "ssq2", oss)
    junk = A(gaT[:].bitcast(F32), "junk", gaT)

    def load_w2(dsts, src, nchunk, ncols, per):
        for c in range(nchunk):
            d_ = dsts[c // per]
            base = (c % per) * ncols
            for j in range(ncols // TT):
                stb, k = ring()
                sc.add("sp", lambda e, stb=stb, c=c, j=j: e.dma_start(out=stb[:, :], in_=src[:, c, j * TT:(j + 1) * TT]), (), (stb.b,), dkey=k)
                cast_to(d_[:, base + j * TT: base + (j + 1) * TT], d_.b, stb[:, :], stb.b)

    sc.add("sp", lambda e: e.dma_start(out=fnw[:, :], in_=fnw_d), (), (fnw.b,), dkey="fnw")

    for st in range(4):
        def p2(st, half):
            tcol = half * TT
            for r in range(4):
                sc.add("sp", lambda e, st=st, r=r, tcol=tcol: e.dma_start(
                    out=ga_g[:, r, 0:TT], in_=gown[st][r * 256:r * 256 + 128, tcol:tcol + TT]),
                    (gown_b[st],), (ga_g.b,), dkey="gag")
                sc.add("sp", lambda e, st=st, r=r, tcol=tcol: e.dma_start(
                    out=gb_g[:, r, 0:TT], in_=gown[st][r * 256 + 128:r * 256 + 256, tcol:tcol + TT]),
                    (gown_b[st],), (gb_g.b,), dkey="gbg")
            rmsnorm_tile(xTo, st * 512 + tcol, hTo)
            for mb in range(8):
                k2 = mb % 2
                pa = ps()
                for c in range(8):
                    sc.add("pe", lambda e, c=c, pa=pa, mb=mb: e.matmul(pa[:, 0:TT], Wgc[c][:, mb * 128:(mb + 1) * 128], hTo[:, c, :],
                                                                     start=(c == 0), stop=(c == 7)), (Wgc[c].b, hTo.cb[c]), (pa.b,))
                sc.add("act", lambda e, pa=pa, k2=k2: e.activation(sgA[k2][:], pa[:, 0:TT], AF.Sigmoid), (pa.b,), (sgA[k2].b,))
                pb_ = ps()
                for c in range(8):
                    sc.add("pe", lambda e, c=c, pb_=pb_, mb=mb: e.matmul(pb_[:, 0:TT], Wgc[c][:, 1024 + mb * 128:1024 + (mb + 1) * 128], hTo[:, c, :],
                                                                      start=(c == 0), stop=(c == 7)), (Wgc[c].b, hTo.cb[c]), (pb_.b,))
                sc.add("act", lambda e, pb_=pb_, k2=k2: e.activation(sgB[k2][:], pb_[:, 0:TT], AF.Sigmoid), (pb_.b,), (sgB[k2].b,))
                pya = ps()
                for r in range(4):
                    sc.add("pe", lambda e, r=r, pya=pya, mb=mb: e.matmul(
                        pya[:, 0:TT], Wua[r // 2][:, (r % 2) * 1024 + mb * 128:(r % 2) * 1024 + (mb + 1) * 128], ga_g[:, r, 0:TT],
                        start=(r == 0), stop=(r == 3)), (Wua[r // 2].b, ga_g.b), (pya.b,))
                pyb = ps()
                for r in range(4):
                    sc.add("pe", lambda e, r=r, pyb=pyb, mb=mb: e.matmul(
                        pyb[:, 0:TT], Wub[r // 2][:, (r % 2) * 1024 + mb * 128:(r % 2) * 1024 + (mb + 1) * 128], gb_g[:, r, 0:TT],
                        start=(r == 0), stop=(r == 3)), (Wub[r // 2].b, gb_g.b), (pyb.b,))
                sc.add("dve", lambda e, pya=pya, k2=k2: e.tensor_tensor(sgA[k2][:], pya[:, 0:TT], sgA[k2][:], ALU.mult), (pya.b, sgA[k2].b), (sgA[k2].b,))
                sc.add("dve", lambda e, pyb=pyb, k2=k2: e.tensor_tensor(sgB[k2][:], pyb[:, 0:TT], sgB[k2][:], ALU.mult), (pyb.b, sgB[k2].b), (sgB[k2].b,))
                sc.add("pool", lambda e, k2=k2, mb=mb: e.tensor_tensor(merged[:, mb, :], sgA[k2][:], sgB[k2][:], ALU.add), (sgA[k2].b, sgB[k2].b), (merged.b,))
            for tb in range(TT // 128):
                row0 = st * 512 + tcol + tb * 128
                xw = xown[tb % 2]
                sc.add("sp", lambda e, xw=xw, row0=row0: e.dma_start(out=xw[:, :], in_=xo[row0:row0 + 128, :]), (), (xw.b,), dkey="xo%d" % (tb % 2))
                po2 = [ps(), ps()]
                for nh_ in range(2):
                    for c in range(8):
                        sc.add("pe", lambda e, c=c, nh_=nh_, tb=tb, po2=po2: e.matmul(
                            po2[nh_][:, 0:512], merged[:, c, tb * 128:(tb + 1) * 128],
                            Woc[c // 2][:, (c % 2) * 1024 + nh_ * 512:(c % 2) * 1024 + (nh_ + 1) * 512],
                            start=(c == 0), stop=(c == 7)), (merged.b, Woc[c // 2].b), (po2[nh_].b,))
                for nh_ in range(2):
                    sc.add("dve", lambda e, nh_=nh_, xw=xw, po2=po2: e.tensor_tensor(
                        ybuf[:, nh_ * 512:(nh_ + 1) * 512], po2[nh_][:, 0:512], xw[:, nh_ * 512:(nh_ + 1) * 512], ALU.add),
                        (po2[nh_].b, xw.b), (ybuf.b,))
                sc.add("act", lambda e: e.activation(junk[:, :], ybuf[:, :], AF.Square, accum_out=ssq2[:]), (ybuf.b,), (junk.b, ssq2.b))
                sc.add("act", lambda e: e.activation(ssq2[:], ssq2[:], AF.Ln, bias=EPS, scale=1.0 / 1024.0), (ssq2.b,), (ssq2.b,))
                sc.add("act", lambda e: e.activation(ssq2[:], ssq2[:], AF.Exp, scale=-0.5), (ssq2.b,), (ssq2.b,))
                sc.add("dve", lambda e: e.scalar_tensor_tensor(ybuf[:, :], ybuf[:, :], ssq2[:, 0:1], fnw[:, :], ALU.mult, ALU.mult),
                       (ybuf.b, ssq2.b, fnw.b), (ybuf.b,))
                sc.add("sp", lambda e, row0=row0: e.dma_start(out=out_d[row0:row0 + 128, :], in_=ybuf[:, :]), (ybuf.b,), (Buf(),), dkey="out")
        for half in range(2):
            p2(st, half)
    fin = Buf("fin")
    last_out = [o for o in sc.ops["sp"] if o.dkey == "out"][-1]
    op = sc.add("sp", None, (), ())
    op.deps = [last_out]
    sc.reorder(window=100)
    build_program.stats = (sc.est_total, {e: len(sc.ops[e]) for e in sc.ENGS})
    sc.emit(nc, es)
    es.close()
    return nc


def _consts():
    bf = ml_dtypes.bfloat16
    idx = np.arange(128)
    h = idx // 64
    i = idx % 64
    same = (h[:, None] == h[None, :])
    c = {}
    sw = h * 64 + (i + 32) % 64
    perm = np.zeros((128, 128), np.float32)
    perm[sw, idx] = 1.0
    c["perm"] = perm.astype(bf)
    c["ident"] = np.eye(128, dtype=np.float32).astype(bf)
    c["onesbd"] = same.astype(np.float32).astype(bf)
    c["onesbdf"] = same.astype(np.float32)
    c["ones"] = np.ones((128, 128), np.float32).astype(bf)
    c["lbd"] = (same & (i[:, None] <= i[None, :])).astype(np.float32)
    ua = np.zeros((128, 130), np.float32)
    ua[:, :128] = (same & (i[:, None] > i[None, :])).astype(np.float32)
    ua[:, 128] = 1.0
    c["uaug"] = ua
    c["slneg"] = -(same & (i[:, None] > i[None, :])).astype(np.float32)
    c["li"] = (same & (i[:, None] >= i[None, :])).astype(np.float32)
    j_ = np.arange(128)[:, None]
    q_ = np.arange(128)[None, :]
    prev = (j_ >= q_).astype(np.float32)
    cur = (j_ <= q_).astype(np.float32)
    dm = np.concatenate([prev, prev, cur, cur], axis=1)
    c["dmask"] = dm.astype(bf)
    inv = (10000.0 ** (-np.arange(0, 64, 2, dtype=np.float32) / 64.0)).astype(np.float32)
    ang = (np.arange(S, dtype=np.float32)[:, None] * inv[None, :]).astype(np.float32)
    ang = np.concatenate([ang, ang], axis=-1)
    cos = np.cos(ang).astype(np.float32).T
    sin = np.sin(ang).astype(np.float32).T
    sgn = np.where(np.arange(64) < 32, -1.0, 1.0).astype(np.float32)[:, None]
    c["cosT"] = np.ascontiguousarray(np.concatenate([cos, cos], 0))
    c["sinT"] = np.ascontiguousarray(np.concatenate([sin * sgn, sin * sgn], 0))
    return c


def _chunk(w, nchunk):
    return np.ascontiguousarray(w.reshape(nchunk, 128, -1).transpose(1, 0, 2))


_NC = [None]


def kernel(x, norm_w, w_in, conv_w, a_log, dt_bias, gdn_norm_w, w_up_a, w_up_b, w_out, final_norm_w):
    x = np.asarray(x, np.float32)
    w_in0 = np.asarray(w_in, np.float32)[0]
    conv0 = np.asarray(conv_w, np.float32)[0]
    if _NC[0] is None:
        _NC[0] = build_program()
    nc = _NC[0]
    cst = _consts()
    shared = dict(cst)
    shared["wg"] = _chunk(w_in0[:, 7184:9232], 8)
    shared["wua"] = _chunk(np.asarray(w_up_a, np.float32)[0], 4)
    shared["wub"] = _chunk(np.asarray(w_up_b, np.float32)[0], 4)
    shared["wo"] = _chunk(np.asarray(w_out, np.float32)[0], 8)
    shared["normw"] = np.ascontiguousarray(np.asarray(norm_w, np.float32)[0].reshape(8, 128).T)
    shared["fnw"] = np.ascontiguousarray(np.broadcast_to(np.asarray(final_norm_w, np.float32)[None, :], (128, 1024)))
    shared["gnw"] = np.ascontiguousarray(np.broadcast_to(np.asarray(gdn_norm_w, np.float32)[0][None, :], (128, 64)))
    in_maps = []
    owns = []
    for core in range(8):
        b, c4 = core // 4, core % 4
        h0 = 2 * c4
        xTb = _chunk(np.ascontiguousarray(x[b].T), 8)
        own = np.concatenate([st * ST + c4 * 512 + np.arange(512) for st in range(4)])
        owns.append(own)
        cols = []
        for g in range(3):
            for t in range(3):
                s0 = g * 1536 + t * 512 + h0 * 64
                cols.append(np.arange(s0, s0 + 128))
        cols.append(np.arange(4608 + h0 * 64, 4608 + h0 * 64 + 128))
        for t in range(3):
            s0 = 5120 + t * 512 + h0 * 64
            cols.append(np.arange(s0, s0 + 128))
        cols.append(np.arange(6656 + h0 * 64, 6656 + h0 * 64 + 128))
        w1 = np.zeros((1024, NB1 * 128), np.float32)
        cc = np.concatenate(cols)
        w1[:, :14 * 128] = w_in0[:, cc]
        w1[:, 14 * 128 + 0] = w_in0[:, 7168 + h0]
        w1[:, 14 * 128 + 1] = w_in0[:, 7176 + h0]
        w1[:, 14 * 128 + 2] = w_in0[:, 7168 + h0 + 1]
        w1[:, 14 * 128 + 3] = w_in0[:, 7176 + h0 + 1]
        convw = np.zeros((128, 12), np.float32)
        for t in range(3):
            s0 = t * 512 + h0 * 64
            convw[:, t * 4:(t + 1) * 4] = conv0[:, s0:s0 + 128].T
        hh = np.arange(128) // 64
        m = dict(shared)
        m["xT"] = xTb
        m["xTo"] = np.ascontiguousarray(xTb[:, :, own])
        m["xo"] = np.ascontiguousarray(x[b][own])
        m["w1"] = _chunk(w1, 8)
        m["convw"] = convw
        m["alog"] = np.asarray(a_log, np.float32)[0][h0 + hh][:, None].copy()
        m["dtb"] = np.asarray(dt_bias, np.float32)[0][h0 + hh][:, None].copy()
        in_maps.append(m)
    res = run_bass_kernel_spmd(nc, in_maps, core_ids=list(range(8)))
    out = np.zeros((2, S, 1024), np.float32)
    for core in range(8):
        out[core // 4, owns[core]] = np.asarray(res.results[core]["out"], np.float32)
    return out
```

```python
from contextlib import ExitStack
import numpy as np
import ml_dtypes
import concourse.bass as bass
import concourse.mybir as mybir
from concourse.bass_utils import run_bass_kernel_spmd

F32 = mybir.dt.float32
BF16 = mybir.dt.bfloat16
ALU = mybir.AluOpType
AF = mybir.ActivationFunctionType
AX = mybir.AxisListType

S = 8192
TT = 256
NTT = S // TT
ST = 2048
TPS = ST // TT
NCH = TT // 64
DIL = (1, 4, 16)
EPS = 1e-6
NB1 = 15
(ZA, GQ, GK, GV, ZB, BA) = (9, 10, 11, 12, 13, 14)


class Buf:
    __slots__ = ("n", "psum")

    def __init__(self, n="", psum=False):
        self.n = n
        self.psum = psum


class Op:
    __slots__ = ("eng", "fn", "deps", "sig", "cnt", "dkey", "dcnt", "idx", "alldeps", "dur", "st", "fin", "nun", "users", "bl")


class _FakeEng:
    def __init__(self):
        self.rec = None

    def __getattr__(self, name):
        def f(*a, **k):
            self.rec = (name, a, k)
            return self
        return f


def _free_elems(ap):
    try:
        sh = list(ap.shape)
        n = 1
        for v in sh[1:]:
            n *= int(v)
        return n
    except Exception:
        return 0


def _estimate(op):
    if op.fn is None:
        return 0.0
    if op.dkey is not None:
        return 0.1
    fe = _FakeEng()
    try:
        op.fn(fe)
    except Exception:
        return 0.5
    name, a, k = fe.rec if fe.rec else ("", (), {})
    args = list(a) + list(k.values())
    n = max([_free_elems(x) for x in args if hasattr(x, "shape")] + [1])
    if op.eng == "pe":
        if name == "matmul":
            rhs = a[2] if len(a) > 2 else k.get("rhs")
            n = _free_elems(rhs)
            f32 = str(getattr(rhs, "dtype", "")).endswith("float32")
            small = 0.0
            try:
                if int(a[1].shape[0]) < 128 or _free_elems(a[1]) < 128:
                    small = 0.08
            except Exception:
                pass
            return 0.05 + small + n * (0.0016 if f32 else 0.0004)
        return 0.11
    if op.eng == "act":
        return 0.20 + n * 0.0008
    if op.eng == "dve":
        return 0.20 + n * 0.0010
    if op.eng == "pool":
        return 0.35 + n * 0.0015
    return 0.1


class Sched:
    ENGS = ("pe", "act", "dve", "pool", "sp")

    def __init__(self):
        self.ops = {e: [] for e in self.ENGS}
        self.last_w = {}
        self.readers = {}
        self.dma_n = {}
        self.nops = 0
        self.last_key = {}

    def add(self, eng, fn, r=(), w=(), dkey=None):
        op = Op()
        op.eng, op.fn, op.sig, op.cnt, op.dkey, op.dcnt = eng, fn, False, 0, dkey, 0
        op.idx = self.nops
        self.nops += 1
        if dkey is not None:
            self.dma_n[dkey] = self.dma_n.get(dkey, 0) + 1
            op.dcnt = self.dma_n[dkey]
        w = tuple(w) + tuple(b for b in r if b.psum and b not in w)
        deps = {}
        for b in r:
            lw = self.last_w.get(b)
            if lw is not None:
                deps[id(lw)] = lw
        for b in w:
            lw = self.last_w.get(b)
            if lw is not None:
                deps[id(lw)] = lw
            for o in self.readers.get(b, {}).values():
                deps[id(o)] = o
        op.alldeps = list(deps.values())
        if dkey is not None:
            lk = self.last_key.get(dkey)
            if lk is not None:
                op.alldeps.append(lk)
            self.last_key[dkey] = op
        op.deps = [d for d in deps.values()
                   if not (d.eng == "pe" and eng == "pe" and d.dkey is None and dkey is None)]
        for d in op.deps:
            d.sig = True
        rk = eng if dkey is None else ("dma", dkey)
        for b in r:
            self.readers.setdefault(b, {})[rk] = op
        for b in w:
            self.last_w[b] = op
            self.readers[b] = {}
        self.ops[eng].append(op)
        return op

    def link(self, new, old):
        lw = self.last_w.get(old)
        if lw is not None:
            self.last_w[new] = lw
        self.readers[new] = dict(self.readers.get(old, {}))

    def barrier(self):
        lasts = [self.ops[e][-1] for e in self.ENGS if self.ops[e]]
        b = Buf("barrier")
        for o in lasts:
            self.last_w.pop(b, None)
        for e in self.ENGS:
            op = self.add(e, None, (), ())
            op.deps = [o for o in lasts]
            op.alldeps = [o for o in lasts]
            for o in lasts:
                o.sig = True

    def reorder(self, window=40, lat=0.12, dma_lat=2.5, use_bl=True, bl_engs=("pe",)):
        allops = []
        for e in self.ENGS:
            allops.extend(self.ops[e])
        for op in allops:
            op.dur = _estimate(op)
            op.st = op.fin = None
            op.users = []
        for op in allops:
            op.nun = 0
        for e in self.ENGS:
            prev = None
            fence = None
            for op in self.ops[e]:
                if prev is not None and op.fn is None:
                    op.alldeps = list(op.alldeps) + [prev]
                if fence is not None:
                    op.alldeps = list(op.alldeps) + [fence]
                if op.fn is None:
                    fence = op
                prev = op
        for op in allops:
            seen = set()
            dd = []
            for d in op.alldeps:
                if id(d) not in seen:
                    seen.add(id(d))
                    dd.append(d)
            op.alldeps = dd
            op.nun = len(dd)
            for d in dd:
                d.users.append(op)
        eff = lambda o: ((100.0 if o.dkey.startswith('cc') else (8.0 if o.dkey.startswith('go') else dma_lat)) if o.dkey is not None else o.dur)
        for op in sorted(allops, key=lambda o: -o.idx):
            b = 0.0
            for u in op.users:
                if u.bl + lat > b:
                    b = u.bl + lat
            op.bl = b + eff(op)
        pend = {e: list(self.ops[e]) for e in self.ENGS}
        head = {e: 0 for e in self.ENGS}
        tfree = {e: 0.0 for e in self.ENGS}
        order = {e: [] for e in self.ENGS}
        remaining = len(allops)
        while remaining:
            best = None
            for e in self.ENGS:
                lst = pend[e]
                i = head[e]
                cnt = 0
                tf = tfree[e]
                cand = None
                while i < len(lst) and cnt < window:
                    op = lst[i]
                    i += 1
                    if op is None:
                        continue
                    cnt += 1
                    if op.nun:
                        continue
                    rt = tf
                    for d in op.alldeps:
                        v = d.fin + lat
                        if v > rt:
                            rt = v
                    if use_bl and e in bl_engs:
                        key = (rt if rt > tf + 0.02 else tf, -op.bl, op.idx)
                    else:
                        key = (rt, op.idx, 0)
                    if cand is None or key < cand[0]:
                        cand = (key, e, i - 1, op, rt)
                if cand is not None and (best is None or cand[0] < best[0]):
                    best = cand
            _, e, pos, op, rt = best
            rt = max(rt, tfree[e])
            op.st = rt
            busy = op.dur
            op.fin = rt + ((100.0 if op.dkey.startswith('cc') else (8.0 if op.dkey.startswith('go') else dma_lat)) if op.dkey is not None else busy)
            tfree[e] = rt + busy
            pend[e][pos] = None
            while head[e] < len(pend[e]) and pend[e][head[e]] is None:
                head[e] += 1
            order[e].append(op)
            for u in op.users:
                u.nun -= 1
            remaining -= 1
        for e in self.ENGS:
            self.ops[e] = order[e]
        self.est_total = max(tfree.values())

    def emit(self, nc, es):
        engobj = {"pe": nc.tensor, "act": nc.scalar, "dve": nc.vector, "pool": nc.gpsimd, "sp": nc.sync}
        sems = {e: es.enter_context(nc.semaphore("s_" + e)) for e in self.ENGS}
        dsems = {k: es.enter_context(nc.semaphore("d_%s" % (k,))) for k in self.dma_n}
        for e in self.ENGS:
            c = 0
            for op in self.ops[e]:
                if op.dkey is None and op.sig:
                    c += 1
                op.cnt = c
        block = es.enter_context(nc.Block())

        def run(e, eng):
            waited = {}
            for op in self.ops[e]:
                for d in op.deps:
                    if d.dkey is not None and d.dkey.startswith("cc"):
                        sem, val, key = dsems[d.dkey], 1, ("d", d.dkey)
                    elif d.dkey is not None:
                        sem, val, key = dsems[d.dkey], 16 * d.dcnt, ("d", d.dkey)
                    else:
                        sem, val, key = sems[d.eng], d.cnt, ("c", d.eng)
                    if waited.get(key, 0) >= val:
                        continue
                    waited[key] = val
                    eng.wait_ge(sem, val)
                if op.fn is None:
                    continue
                ins = op.fn(eng)
                if op.dkey is not None and op.dkey.startswith("cc"):
                    ins.then_inc(dsems[op.dkey])
                elif op.dkey is not None:
                    ins.then_inc(dsems[op.dkey], 16)
                elif op.sig:
                    ins.then_inc(sems[e], 1)

        @block.tensor
        def _(eng):
            run("pe", eng)

        @block.scalar
        def _(eng):
            run("act", eng)

        @block.vector
        def _(eng):
            run("dve", eng)

        @block.gpsimd
        def _(eng):
            run("pool", eng)

        @block.sync
        def _(eng):
            run("sp", eng)


class T:
    def __init__(self, nc, es, name, shape, dt, psum=False):
        self.t = es.enter_context((nc.psum_tensor if psum else nc.sbuf_tensor)("t_" + name, list(shape), dt))
        self.b = Buf(name, psum)

    def __getitem__(self, k):
        return self.t[k]


def build_program():
    nc = bass.Bass("TRN2", target_bir_lowering=False)
    es = ExitStack()
    sc = Sched()

    def dram_in(name, shape, dt=F32):
        return nc.dram_tensor(name, list(shape), dt, kind="ExternalInput").ap()

    xT = dram_in("xT", [128, 8, S])
    xTo = dram_in("xTo", [128, 8, ST])
    xo = dram_in("xo", [ST, 1024])
    w1 = dram_in("w1", [128, 8, NB1 * 128])
    wg = dram_in("wg", [128, 8, 2048])
    wua = dram_in("wua", [128, 4, 1024])
    wub = dram_in("wub", [128, 4, 1024])
    wo = dram_in("wo", [128, 8, 1024])
    normw_d = dram_in("normw", [128, 8])
    fnw_d = dram_in("fnw", [128, 1024])
    convw_d = dram_in("convw", [128, 12])
    alog_d = dram_in("alog", [128, 1])
    dtb_d = dram_in("dtb", [128, 1])
    gnw_d = dram_in("gnw", [128, 64])
    cos_d = dram_in("cosT", [128, S])
    sin_d = dram_in("sinT", [128, S])
    perm_d = dram_in("perm", [128, 128], BF16)
    ident_d = dram_in("ident", [128, 128], BF16)
    onesbd_d = dram_in("onesbd", [128, 128], BF16)
    onesbdf_d = dram_in("onesbdf", [128, 128])
    ones_d = dram_in("ones", [128, 128], BF16)
    lbd_d = dram_in("lbd", [128, 128])
    uaug_d = dram_in("uaug", [128, 130])
    slneg_d = dram_in("slneg", [128, 128])
    li_d = dram_in("li", [128, 128])
    dmask_d = dram_in("dmask", [128, 512], BF16)
    out_d = nc.dram_tensor("out", [ST, 1024], F32, kind="ExternalOutput").ap()
    bin0 = nc.dram_tensor("bin0", [256, ST], BF16)
    bout0 = nc.dram_tensor("bout0", [1024, ST], BF16)
    bin_ = [bin0] * 4
    bout = [bout0] * 4
    bb_, bo_ = Buf("bin"), Buf("bout")
    bin_b = [bb_] * 4
    bout_b = [bo_] * 4
    gown = [nc.dram_tensor("gown%d" % i, [1024, 512], BF16) for i in range(4)]
    gown_b = [Buf("gown%d" % i) for i in range(4)]
    cid_cache = {}

    def cidx(eng):
        if "v" not in cid_cache:
            cid_cache["v"] = eng.snap((eng.partition_id() % 4) * 512, min_val=0, max_val=1536)
        return cid_cache["v"]

    def copy_out(st):
        sc.add("sp", lambda e, st=st: e.dma_start(out=gown[st][:, :], in_=bout0[:, bass.ds(cidx(e), 512)]),
               (bo_,), (gown_b[st],), dkey="go%d" % st)

    def sb(name, shape, dt=F32):
        return T(nc, es, name, shape, dt)

    banks = [T(nc, es, "ps%d" % i, [128, 512], F32, psum=True) for i in range(8)]
    POOLS = {"ALL": [0, 1, 2, 3, 4, 5, 6, 7], "A": [0, 1, 2, 7], "B": [3, 4, 5, 6]}
    pool_i = {"ALL": 0, "A": 0, "B": 0}
    cur_pool = ["ALL"]

    def ps():
        k = cur_pool[0]
        lst = POOLS[k]
        b = banks[lst[pool_i[k] % len(lst)]]
        pool_i[k] += 1
        return b

    def step(gen, pool):
        cur_pool[0] = pool
        try:
            next(gen)
            return True
        except StopIteration:
            return False
        finally:
            cur_pool[0] = "ALL"

    def interleave(g1, g2):
        a1, a2 = g1 is not None, g2 is not None
        while a1 or a2:
            if a1:
                a1 = step(g1, "A")
            if a2:
                a2 = step(g2, "B")

    kid = [0]

    def load_const(dst, src, eng="sp"):
        kid[0] += 1
        sc.add(eng, lambda e, d=dst, s=src: e.dma_start(out=d[:], in_=s), (), (dst.b,), dkey="c%d" % kid[0])

    normw = sb("normw", [128, 8]); load_const(normw, normw_d)
    convw = sb("convw", [128, 12]); load_const(convw, convw_d)
    alog = sb("alog", [128, 1]); load_const(alog, alog_d)
    dtb = sb("dtb", [128, 1]); load_const(dtb, dtb_d)
    gnw = sb("gnw", [128, 64]); load_const(gnw, gnw_d)
    perm = sb("perm", [128, 128], BF16); load_const(perm, perm_d)
    ident = sb("ident", [128, 128], BF16); load_const(ident, ident_d)
    onesbd = sb("onesbd", [128, 128], BF16); load_const(onesbd, onesbd_d)
    onesbdf = sb("onesbdf", [128, 128]); load_const(onesbdf, onesbdf_d)
    ones = sb("ones", [128, 128], BF16); load_const(ones, ones_d)
    lbd = sb("lbd", [128, 128]); load_const(lbd, lbd_d)
    uaug = sb("uaug", [128, 130]); load_const(uaug, uaug_d)
    slneg = sb("slneg", [128, 128]); load_const(slneg, slneg_d)
    li = sb("li", [128, 128]); load_const(li, li_d)
    dmask = sb("dmask", [128, 512], BF16); load_const(dmask, dmask_d)
    nalog = sb("nalog", [128, 1])
    sc.add("act", lambda e: e.activation(nalog[:], alog[:], AF.Exp), (alog.b,), (nalog.b,))
    sc.add("dve", lambda e: e.tensor_scalar(nalog[:], nalog[:], -1.0, None, ALU.mult), (nalog.b,), (nalog.b,))

    XR = 12
    xs = [sb("xs%d" % i, [128, TT]) for i in range(XR)]
    xs_i = [0]

    def ring():
        b = xs[xs_i[0] % XR]
        k = "x%d" % (xs_i[0] % XR)
        xs_i[0] += 1
        return b, k

    cast_i = [0]

    def cast_to(dst_ap, dst_b, src_ap, src_b):
        cast_i[0] += 1
        if cast_i[0] % 2 == 0:
            sc.add("dve", lambda e: e.tensor_copy(dst_ap, src_ap), (src_b,), (dst_b,))
        else:
            sc.add("act", lambda e: e.copy(dst_ap, src_ap), (src_b,), (dst_b,))

    W1 = sb("W1", [128, 8, (NB1 - 1) * 128], BF16)
    W1b = [Buf("W1b%d" % i) for i in range(NB1 - 1)]
    for j in range((NB1 - 1) * 128 // TT):
        for c in range(8):
            stb, k = ring()
            sc.add("sp", lambda e, stb=stb, j=j, c=c: e.dma_start(out=stb[:, :], in_=w1[:, c, j * TT:(j + 1) * TT]),
                   (), (stb.b,), dkey=k)
            for bi in (2 * j, 2 * j + 1):
                cast_i[0] += (bi == 2 * j)
                dst_ap = W1[:, c, bi * 128:(bi + 1) * 128]
                src_ap = stb[:, (bi - 2 * j) * 128:(bi - 2 * j + 1) * 128]
                if cast_i[0] % 2 == 0:
                    sc.add("dve", lambda e, dst_ap=dst_ap, src_ap=src_ap: e.tensor_copy(dst_ap, src_ap), (stb.b,), (W1b[bi],))
                else:
                    sc.add("act", lambda e, dst_ap=dst_ap, src_ap=src_ap: e.copy(dst_ap, src_ap), (stb.b,), (W1b[bi],))
    Wba = sb("Wba", [128, 8, 4], BF16)
    stb, k = ring()
    stv = stb[:, 0:32].rearrange("p (a b) -> p a b", a=8)
    sc.add("sp", lambda e, stv=stv: e.dma_start(out=stv, in_=w1[:, :, BA * 128:BA * 128 + 4]), (), (stb.b,), dkey=k)
    cast_to(Wba[:], Wba.b, stv, stb.b)

    sqb = [sb("sq%d" % i, [128, TT], BF16) for i in range(3)]
    rstd = sb("rstd", [128, TT])
    hT2 = [sb("hT%d" % i, [128, 8, TT], BF16) for i in range(2)]
    cur_hT = [hT2[0]]
    cosb = [sb("cos%d" % i, [128, TT]) for i in range(2)]
    sinb = [sb("sin%d" % i, [128, TT]) for i in range(2)]
    Kt = [[sb("Kt%d_%d" % (g, s), [128, ST], BF16) for s in range(2)] for g in range(3)]
    Qt = [sb("Qt%d" % g, [128, ST], BF16) for g in range(3)]
    Vt = [sb("Vt%d" % g, [128, ST], BF16) for g in range(3)]
    Vs = [[sb("Vs%d_%d" % (g, s), [128, 16, 128], BF16) for s in range(2)] for g in range(3)]
    ndacc = sb("ndacc", [128, 2, ST])
    zaS = sb("zaS", [128, ST], BF16)
    gaT = sb("gaT", [128, ST], BF16)
    gbT = sb("gbT", [128, ST], BF16)
    qraw = [sb("qraw%d" % i, [128, TT], BF16) for i in range(2)]
    rt1 = [sb("rt1_%d" % i, [128, TT]) for i in range(2)]
    rt2 = [sb("rt2_%d" % i, [128, TT]) for i in range(2)]
    Pt = [sb("Pt%d" % i, [128, 512], BF16) for i in range(3)]
    Qx = [sb("Qx%d" % i, [128, 256], BF16) for i in range(2)]
    xpad = [sb("xpad%d" % i, [128, TT + 3], BF16) for i in range(3)]
    cacc = [sb("cacc%d" % i, [128, TT]) for i in range(2)]
    diagw = sb("diagw", [128, 12, 128], BF16)
    gsq = sb("gsq", [128, TT], BF16)
    grn = sb("grn", [128, TT])
    Qbd2 = [sb("Qbd%d" % i, [128, NCH, 128], BF16) for i in range(2)]
    Kbd2 = [sb("Kbd%d" % i, [128, NCH, 128], BF16) for i in range(2)]
    Vbd2 = [sb("Vbd%d" % i, [128, NCH, 128], BF16) for i in range(2)]
    zbS2 = [sb("zbS%d" % i, [128, TT], BF16) for i in range(2)]
    batok = sb("batok", [128, TT // 128, 4])
    bastk2 = [sb("bastk%d" % i, [128, NCH, 2]) for i in range(2)]
    beta2 = [sb("beta%d" % i, [128, NCH]) for i in range(2)]
    gg2 = [sb("gg%d" % i, [128, NCH]) for i in range(2)]
    Rbd = sb("Rbd", [128, NCH, 128])
    E1 = sb("E1", [128, NCH, 128])
    MM2 = sb("MM2", [128, NCH, 128])
    expG = sb("expG", [128, NCH])
    glast = sb("glast", [128, NCH])
    decf = sb("decf", [128, NCH])
    kdsc = sb("kdsc", [128, NCH])
    bsc = sb("bsc", [128, NCH])
    Cb = [sb("Cb%d" % i, [128, NCH, 128], BF16) for i in range(2)]
    Bb = [sb("Bb%d" % i, [128, NCH, 128], BF16) for i in range(2)]
    Xb = [sb("Xb%d" % i, [128, NCH, 128], BF16) for i in range(2)]
    ufin = sb("ufin", [128, NCH, 64])
    Wbd2 = sb("Wbd2", [128, NCH, 128], BF16)
    Wt = sb("Wt", [128, NCH, 128], BF16)
    attn = sb("attn", [128, NCH, 128], BF16)
    attnT = sb("attnT", [128, NCH, 128], BF16)
    Kdec = sb("Kdec", [128, NCH, 128], BF16)
    Sst = sb("Sst", [128, 64])
    Sbf = sb("Sbf", [128, 64], BF16)
    vnew = sb("vnew", [128, 64], BF16)
    oB = sb("oB", [128, 64])
    otile = sb("otile", [128, NCH, 64])
    oss = sb("oss", [128, NCH])
    onbd = sb("onbd", [128, NCH, 128], BF16)

    for k_ in range(12):
        sc.add("dve", lambda e, k_=k_: e.tensor_scalar(diagw[:, k_, :], ident[:], convw[:, k_:k_ + 1], None, ALU.mult),
               (ident.b, convw.b), (diagw.b,))
    for t_ in Qx:
        sc.add("pool", lambda e, t_=t_: e.memset(t_[:], 0.0), (), (t_.b,))
    for t_, eng in ((Sst, "dve"), (Sbf, "dve"), (Wbd2, "pool"), (onbd, "pool"), (Qbd2[0], "pool"), (Kbd2[0], "pool"),
                    (Vbd2[0], "pool"), (Qbd2[1], "pool"), (Kbd2[1], "pool"), (Vbd2[1], "pool"), (xpad[0], "dve"), (xpad[1], "dve"), (xpad[2], "dve")):
        sc.add(eng, lambda e, t_=t_: e.memset(t_[:], 0.0), (), (t_.b,))

    def rmsnorm_tile(src_dram, col0, hdst):
        bufs = []
        for c in range(8):
            xb = xs[xs_i[0] % XR]
            k = "x%d" % (xs_i[0] % XR)
            xs_i[0] += 1
            sc.add("sp", lambda e, xb=xb, c=c: e.dma_start(out=xb[:], in_=src_dram[:, c, col0:col0 + TT]),
                   (), (xb.b,), dkey=k)
            bufs.append(xb)
        pss = ps()
        for c in range(8):
            sq = sqb[c % 3]
            sc.add("act", lambda e, sq=sq, xb=bufs[c]: e.activation(sq[:], xb[:], AF.Square), (bufs[c].b,), (sq.b,))
            sc.add("pe", lambda e, sq=sq, c=c: e.matmul(pss[:, 0:TT], ones[:], sq[:], start=(c == 0), stop=(c == 7)),
                   (sq.b, ones.b), (pss.b,))
        sc.add("act", lambda e: e.activation(rstd[:], pss[:, 0:TT], AF.Ln, bias=EPS, scale=1.0 / 1024.0),
               (pss.b,), (rstd.b,))
        sc.add("act", lambda e: e.activation(rstd[:], rstd[:], AF.Exp, scale=-0.5), (rstd.b,), (rstd.b,))
        for c in range(8):
            sc.add("dve", lambda e, c=c, xb=bufs[c]: e.scalar_tensor_tensor(
                hdst[:, c, :], xb[:], normw[:, c:c + 1], rstd[:], ALU.mult, ALU.mult),
                (bufs[c].b, rstd.b, normw.b), (hdst.b,))

    def proj_fm(blk):
        p = ps()
        hT = cur_hT[0]
        for c in range(8):
            sc.add("pe", lambda e, c=c, p=p, hT=hT: e.matmul(p[:, 0:TT], W1[:, c, blk * 128:(blk + 1) * 128], hT[:, c, :],
                                                            start=(c == 0), stop=(c == 7)), (W1b[blk], hT.b), (p.b,))
        return p

    stmp = [sb("stmp%d" % i, [128, TT]) for i in range(2)]
    stmp_i = [0]

    def sigmoid_to(src_ap, src_b, shape_cols=TT):
        t = stmp[stmp_i[0] % 2]
        stmp_i[0] += 1
        tv = t[:, 0:shape_cols]
        sc.add("act", lambda e: e.activation(tv, src_ap, AF.Exp, scale=-1.0), (src_b,), (t.b,))
        sc.add("act", lambda e: e.activation(tv, tv, AF.Ln, bias=1.0), (t.b,), (t.b,))
        sc.add("act", lambda e: e.activation(tv, tv, AF.Exp, scale=-1.0), (t.b,), (t.b,))
        return t

    def perm_view(t_ap_tile, g, tt):
        d = DIL[g]
        n = TT // d
        v = t_ap_tile[:, :].rearrange("p (r m) -> p r m", r=d)
        return v[:, :, tt * n:(tt + 1) * n]

    def src_view(ap, g):
        d = DIL[g]
        return ap.rearrange("p (m r) -> p r m", r=d)

    rope_i = [0]

    def prenorm(ti):
        col0 = ti * TT
        cb, sb_ = cosb[ti % 2], sinb[ti % 2]
        sc.add("sp", lambda e: e.dma_start(out=cb[:], in_=cos_d[:, col0:col0 + TT]), (), (cb.b,), dkey="cos%d" % (ti % 2))
        sc.add("sp", lambda e: e.dma_start(out=sb_[:], in_=sin_d[:, col0:col0 + TT]), (), (sb_.b,), dkey="sin%d" % (ti % 2))
        rmsnorm_tile(xT, col0, hT2[ti % 2])

    prenorm(0)
    for st in range(4):
        slot = st % 2
        def tileA(st, slot, tt):
            ti = st * TPS + tt
            col0 = ti * TT
            par = ti % 2
            Qbd, Kbd, Vbd, zbS, bastk, beta, gg = Qbd2[par], Kbd2[par], Vbd2[par], zbS2[par], bastk2[par], beta2[par], gg2[par]
            cb, sb_ = cosb[ti % 2], sinb[ti % 2]
            hT = hT2[ti % 2]
            cur_hT[0] = hT

            def dsa_qk(g, qk):
                p = proj_fm(3 * g + qk)
                qr = qraw[rope_i[0] % 2]
                t1 = rt1[rope_i[0] % 2]
                t2 = rt2[rope_i[0] % 2]
                rope_i[0] += 1
                sc.add("act", lambda e, p=p, qr=qr: e.copy(qr[:], p[:, 0:TT]), (p.b,), (qr.b,))
                p2 = ps()
                sc.add("pe", lambda e, p2=p2, qr=qr: e.matmul(p2[:, 0:TT], perm[:], qr[:], start=True, stop=True),
                       (perm.b, qr.b), (p2.b,))
                sc.add("dve", lambda e, p=p, t1=t1, cb=cb: e.tensor_tensor(t1[:], p[:, 0:TT], cb[:], ALU.mult),
                       (p.b, cb.b), (t1.b,))
                sc.add("dve", lambda e, p2=p2, t2=t2, sb_=sb_: e.tensor_tensor(t2[:], p2[:, 0:TT], sb_[:], ALU.mult),
                       (p2.b, sb_.b), (t2.b,))
                dst = Qt[g] if qk == 0 else Kt[g][slot]
                sc.add("pool", lambda e, dst=dst, t1=t1, t2=t2, g=g, tt=tt: e.tensor_tensor(
                    perm_view(dst, g, tt), src_view(t1[:, :], g), src_view(t2[:, :], g), ALU.add),
                    (t1.b, t2.b), (dst.b,))

            def dsa_v(g):
                p = proj_fm(3 * g + 2)
                sc.add("act", lambda e, p=p, g=g, tt=tt: e.copy(perm_view(Vt[g], g, tt), src_view(p[:, 0:TT], g)),
                       (p.b,), (Vt[g].b,))

            def z_a():
                p = proj_fm(ZA)
                sg = sigmoid_to(p[:, 0:TT], p.b)
                sc.add("dve", lambda e, p=p, tt=tt, sg=sg: e.tensor_tensor(zaS[:, tt * TT:(tt + 1) * TT], p[:, 0:TT], sg[:], ALU.mult),
                       (p.b, sg.b), (zaS.b,))

            def z_b():
                p = proj_fm(ZB)
                sg = sigmoid_to(p[:, 0:TT], p.b)
                sc.add("dve", lambda e, p=p, sg=sg: e.tensor_tensor(zbS[:], p[:, 0:TT], sg[:], ALU.mult), (p.b, sg.b), (zbS.b,))

            def gdn_in(j):
                blk = (GQ, GK, GV)[j]
                p = proj_fm(blk)
                xp = xpad[j]
                ca = cacc[j % 2]
                sc.add("act", lambda e, xp=xp: e.copy(xp[:, 0:3], xp[:, TT:TT + 3]), (xp.b,), (xp.b,))
                sc.add("act", lambda e, xp=xp, p=p: e.copy(xp[:, 3:TT + 3], p[:, 0:TT]), (p.b, xp.b), (xp.b,))
                pc = ps()
                for tap in range(4):
                    sc.add("pe", lambda e, xp=xp, pc=pc, j=j, tap=tap: e.matmul(
                        pc[:, 0:TT], diagw[:, j * 4 + tap, :], xp[:, tap:tap + TT], start=(tap == 0), stop=(tap == 3)),
                        (diagw.b, xp.b), (pc.b,))
                if j < 2:
                    sg = sigmoid_to(pc[:, 0:TT], pc.b)
                    sc.add("dve", lambda e, ca=ca, pc=pc, sg=sg: e.tensor_tensor(ca[:], pc[:, 0:TT], sg[:], ALU.mult), (pc.b, sg.b), (ca.b,))
                    sc.add("act", lambda e, ca=ca: e.activation(gsq[:], ca[:], AF.Square), (ca.b,), (gsq.b,))
                    p3 = ps()
                    sc.add("pe", lambda e, p3=p3: e.matmul(p3[:, 0:TT], onesbd[:], gsq[:], start=True, stop=True),
                           (onesbd.b, gsq.b), (p3.b,))
                    scl = 64.0 if j == 0 else 1.0
                    sc.add("act", lambda e, p3=p3, scl=scl: e.activation(grn[:], p3[:, 0:TT], AF.Ln, bias=EPS * scl, scale=scl),
                           (p3.b,), (grn.b,))
                    sc.add("act", lambda e: e.activation(grn[:], grn[:], AF.Exp, scale=-0.5), (grn.b,), (grn.b,))
                    dstb = Qbd if j == 0 else Kbd
                    for h in range(2):
                        hs = slice(h * 64, (h + 1) * 64)
                        sc.add("dve", lambda e, dstb=dstb, hs=hs, h=h, ca=ca: e.tensor_tensor(
                            dstb[hs, :, h * 64:(h + 1) * 64], ca[hs, :].rearrange("p (c i) -> p c i", i=64),
                            grn[hs, :].rearrange("p (c i) -> p c i", i=64), ALU.mult), (ca.b, grn.b), (dstb.b,))
                else:
                    sg = sigmoid_to(pc[:, 0:TT], pc.b)
                    for h in range(2):
                        hs = slice(h * 64, (h + 1) * 64)
                        sc.add("dve", lambda e, hs=hs, h=h, pc=pc, sg=sg: e.tensor_tensor(
                            Vbd[hs, :, h * 64:(h + 1) * 64], pc[hs, 0:TT].rearrange("p (c i) -> p c i", i=64),
                            sg[hs, :].rearrange("p (c i) -> p c i", i=64), ALU.mult), (pc.b, sg.b), (Vbd.b,))

            def beta_part():
                pb = ps()
                for tb in range(TT // 128):
                    for c in range(8):
                        sc.add("pe", lambda e, tb=tb, c=c, pb=pb: e.matmul(
                            pb[:, tb * 4:(tb + 1) * 4], hT[:, c, tb * 128:(tb + 1) * 128], Wba[:, c, :],
                            start=(c == 0), stop=(c == 7)), (hT.b, Wba.b), (pb.b,))
                sc.add("dve", lambda e, pb=pb: e.tensor_copy(batok[:], pb[:, 0:(TT // 128) * 4].rearrange("p (b k) -> p b k", k=4)),
                       (pb.b,), (batok.b,))
                for h in range(2):
                    for cp in range(2):
                        sc.add("sp", lambda e, h=h, cp=cp: e.dma_start(
                            out=bastk[h * 64:(h + 1) * 64, cp:NCH:2, :], in_=batok[cp * 64:(cp + 1) * 64, :, 2 * h:2 * h + 2],
                            allow_slow_non_contiguous=True), (batok.b,), (bastk.b,), dkey="ba%d" % par)
                sc.add("act", lambda e: e.activation(beta[:], bastk[:, :, 0], AF.Exp, scale=-1.0), (bastk.b,), (beta.b,))
                sc.add("act", lambda e: e.activation(beta[:], beta[:], AF.Ln, bias=1.0), (beta.b,), (beta.b,))
                sc.add("act", lambda e: e.activation(beta[:], beta[:], AF.Exp, scale=-1.0), (beta.b,), (beta.b,))
                sc.add("act", lambda e: e.activation(gg[:], bastk[:, :, 1], AF.Exp, bias=dtb[:]), (bastk.b, dtb.b), (gg.b,))
                sc.add("act", lambda e: e.activation(gg[:], gg[:], AF.Ln, bias=1.0), (gg.b,), (gg.b,))
                sc.add("dve", lambda e: e.tensor_scalar(gg[:], gg[:], nalog[:, 0:1], None, ALU.mult), (gg.b, nalog.b), (gg.b,))

            dsa_qk(0, 0); z_a(); yield
            dsa_qk(0, 1); z_b(); yield
            dsa_v(0); gdn_in(0); yield
            if ti + 1 < NTT:
                prenorm(ti + 1)
            yield
            dsa_qk(1, 0); yield
            dsa_qk(1, 1); gdn_in(1); yield
            dsa_v(1); yield
            dsa_qk(2, 0); gdn_in(2); yield
            dsa_qk(2, 1); beta_part(); yield
            dsa_v(2); yield

        def tileB(st, slot, tt):
            ti = st * TPS + tt
            par = ti % 2
            Qbd, Kbd, Vbd, zbS, bastk, beta, gg = Qbd2[par], Kbd2[par], Vbd2[par], zbS2[par], bastk2[par], beta2[par], gg2[par]
            pgl = ps()
            sc.add("pe", lambda e, pgl=pgl: e.matmul(pgl[:, 0:NCH], onesbdf[:], gg[:], start=True, stop=True),
                   (onesbdf.b, gg.b), (pgl.b,))
            sc.add("pe", lambda e, pgl=pgl: e.matmul(pgl[:, NCH:2 * NCH], lbd[:], gg[:], start=True, stop=True),
                   (lbd.b, gg.b), (pgl.b,))
            sc.add("act", lambda e, pgl=pgl: e.copy(glast[:], pgl[:, 0:NCH]), (pgl.b,), (glast.b,))
            sc.add("act", lambda e, pgl=pgl: e.activation(decf[:], pgl[:, 0:NCH], AF.Exp), (pgl.b,), (decf.b,))
            sc.add("act", lambda e, pgl=pgl: e.activation(expG[:], pgl[:, NCH:2 * NCH], AF.Exp), (pgl.b,), (expG.b,))
            sc.add("dve", lambda e, pgl=pgl: e.tensor_tensor(kdsc[:], glast[:], pgl[:, NCH:2 * NCH], ALU.subtract),
                   (glast.b, pgl.b), (kdsc.b,))
            sc.add("act", lambda e: e.activation(kdsc[:], kdsc[:], AF.Exp), (kdsc.b,), (kdsc.b,))
            sc.add("dve", lambda e: e.tensor_tensor(bsc[:], beta[:], expG[:], ALU.mult), (beta.b, expG.b), (bsc.b,))
            sc.add("dve", lambda e: e.tensor_tensor(
                Rbd[:], uaug[:, 0:128].unsqueeze(1).broadcast_to([128, NCH, 128]),
                gg[:, :].unsqueeze(2).broadcast_to([128, NCH, 128]), ALU.mult), (uaug.b, gg.b), (Rbd.b,))
            pD = ps()
            for c in range(NCH):
                sc.add("pe", lambda e, c=c, pD=pD: e.matmul(pD[:, c * 128:(c + 1) * 128], lbd[:], Rbd[:, c, :], start=True, stop=True),
                       (lbd.b, Rbd.b), (pD.b,))
            sc.add("act", lambda e, pD=pD: e.activation(E1[:].rearrange("p c n -> p (c n)"), pD[:, 0:NCH * 128], AF.Exp), (pD.b,), (E1.b,))
            yield
            pKK = ps()
            pQK = ps()
            for c in range(NCH):
                sc.add("pe", lambda e, c=c, pKK=pKK: e.matmul(pKK[:, c * 128:(c + 1) * 128], Kbd[:, c, :], Kbd[:, c, :], start=True, stop=True),
                       (Kbd.b,), (pKK.b,))
            for c in range(NCH):
                sc.add("pe", lambda e, c=c, pQK=pQK: e.matmul(pQK[:, c * 128:(c + 1) * 128], Qbd[:, c, :], Kbd[:, c, :], start=True, stop=True),
                       (Qbd.b, Kbd.b), (pQK.b,))
            sc.add("dve", lambda e: e.tensor_tensor(MM2[:], E1[:], beta[:, :].unsqueeze(2).broadcast_to([128, NCH, 128]), ALU.mult),
                   (E1.b, beta.b), (MM2.b,))
            sc.add("pool", lambda e: e.tensor_tensor(MM2[:], MM2[:], slneg[:, :].unsqueeze(1).broadcast_to([128, NCH, 128]), ALU.mult),
                   (MM2.b, slneg.b), (MM2.b,))
            sc.add("dve", lambda e, pKK=pKK: e.tensor_tensor(Cb[0][:].rearrange("p c n -> p (c n)"), pKK[:, 0:NCH * 128],
                                                           MM2[:].rearrange("p c n -> p (c n)"), ALU.mult), (pKK.b, MM2.b), (Cb[0].b,))
            sc.add("pool", lambda e: e.tensor_tensor(E1[:], E1[:], li[:, :].unsqueeze(1).broadcast_to([128, NCH, 128]), ALU.mult),
                   (E1.b, li.b), (E1.b,))
            sc.add("dve", lambda e, pQK=pQK: e.tensor_tensor(attn[:].rearrange("p c n -> p (c n)"), pQK[:, 0:NCH * 128],
                                                           E1[:].rearrange("p c n -> p (c n)"), ALU.mult), (pQK.b, E1.b), (attn.b,))
            yield
            pT1 = ps(); pT2 = ps(); pT3 = ps(); pT4 = ps()
            for c in range(NCH):
                for (pt, src_) in ((pT1, Cb[0]), (pT2, attn), (pT3, Kbd), (pT4, Vbd)):
                    sc.add("pe", lambda e, c=c, pt=pt, src_=src_: e.transpose(
                        pt[:].bitcast(BF16)[:, c * 128:(c + 1) * 128], src_[:, c, :], ident[:]), (src_.b, ident.b), (pt.b,))
            sc.add("act", lambda e: e.copy(Bb[0][:].rearrange("p c n -> p (c n)"), pT1[:].bitcast(BF16)[:, 0:NCH * 128]), (pT1.b,), (Bb[0].b,))
            sc.add("act", lambda e: e.copy(attnT[:].rearrange("p c n -> p (c n)"), pT2[:].bitcast(BF16)[:, 0:NCH * 128]), (pT2.b,), (attnT.b,))
            pT3v = pT3[:].bitcast(BF16)[:, 0:NCH * 128].rearrange("p (c n) -> p c n", n=128)
            pT4v = pT4[:].bitcast(BF16)[:, 0:NCH * 128].rearrange("p (c n) -> p c n", n=128)
            sc.add("dve", lambda e, pT3v=pT3v: e.tensor_tensor(Kdec[:], pT3v, kdsc[:, :].unsqueeze(2).broadcast_to([128, NCH, 128]), ALU.mult),
                   (pT3.b, kdsc.b), (Kdec.b,))
            for h in range(2):
                hs = slice(h * 64, (h + 1) * 64)
                sc.add("dve", lambda e, h=h, hs=hs, pT3v=pT3v: e.tensor_tensor(
                    Xb[0][hs, :, 64:128], pT3v[hs, :, h * 64:(h + 1) * 64], bsc[hs, :].unsqueeze(2).broadcast_to([64, NCH, 64]), ALU.mult),
                    (pT3.b, bsc.b), (Xb[0].b,))
                sc.add("dve", lambda e, h=h, hs=hs, pT4v=pT4v: e.tensor_tensor(
                    Xb[0][hs, :, 0:64], pT4v[hs, :, h * 64:(h + 1) * 64], beta[hs, :].unsqueeze(2).broadcast_to([64, NCH, 64]), ALU.mult),
                    (pT4.b, beta.b), (Xb[0].b,))
            yield
            cur = 0
            for lvl in range(6):
                yield
                Bc, Cc, Xc = Bb[cur], Cb[cur], Xb[cur]
                Bn, Cn, Xn = Bb[1 - cur], Cb[1 - cur], Xb[1 - cur]
                pX = ps()
                for c in range(NCH):
                    sc.add("pe", lambda e, c=c, pX=pX, Bc=Bc, Xc=Xc: e.matmul(pX[:, c * 128:(c + 1) * 128], Bc[:, c, :], Xc[:, c, :], start=True, stop=True),
                           (Bc.b, Xc.b), (pX.b,))
                if lvl < 5:
                    sc.add("dve", lambda e, pX=pX, Xc=Xc, Xn=Xn: e.tensor_tensor(
                        Xn[:].rearrange("p c n -> p (c n)"), pX[:, 0:NCH * 128], Xc[:].rearrange("p c n -> p (c n)"), ALU.add),
                        (pX.b, Xc.b), (Xn.b,))
                    pB = ps()
                    for c in range(NCH):
                        sc.add("pe", lambda e, c=c, pB=pB, Bc=Bc, Cc=Cc: e.matmul(pB[:, c * 128:(c + 1) * 128], Cc[:, c, :], Bc[:, c, :], start=True, stop=True),
                               (Bc.b, Cc.b), (pB.b,))
                    sc.add("act", lambda e, pB=pB, Bn=Bn: e.copy(Bn[:].rearrange("p c n -> p (c n)"), pB[:, 0:NCH * 128]), (pB.b,), (Bn.b,))
                    if lvl < 4:
                        pC = ps()
                        for c in range(NCH):
                            sc.add("pe", lambda e, c=c, pC=pC, Bc=Bc, Cc=Cc: e.matmul(pC[:, c * 128:(c + 1) * 128], Bc[:, c, :], Cc[:, c, :], start=True, stop=True),
                                   (Bc.b, Cc.b), (pC.b,))
                        sc.add("act", lambda e, pC=pC, Cn=Cn: e.copy(Cn[:].rearrange("p c n -> p (c n)"), pC[:, 0:NCH * 128]), (pC.b,), (Cn.b,))
                    cur = 1 - cur
                else:
                    pXv = pX[:, 0:NCH * 128].rearrange("p (c n) -> p c n", n=128)
                    sc.add("dve", lambda e, pXv=pXv, Xc=Xc: e.tensor_tensor(ufin[:], pXv[:, :, 0:64], Xc[:, :, 0:64], ALU.add),
                           (pX.b, Xc.b), (ufin.b,))
                    for h in range(2):
                        hs = slice(h * 64, (h + 1) * 64)
                        sc.add("dve", lambda e, pXv=pXv, Xc=Xc, hs=hs, h=h: e.tensor_tensor(
                            Wbd2[hs, :, h * 64:(h + 1) * 64], pXv[hs, :, 64:128], Xc[hs, :, 64:128], ALU.add),
                            (pX.b, Xc.b), (Wbd2.b,))
            pT5 = ps()
            for c in range(NCH):
                sc.add("pe", lambda e, c=c, pT5=pT5: e.transpose(pT5[:].bitcast(BF16)[:, c * 128:(c + 1) * 128], Wbd2[:, c, :], ident[:]),
                       (Wbd2.b, ident.b), (pT5.b,))
            sc.add("act", lambda e, pT5=pT5: e.copy(Wt[:].rearrange("p c n -> p (c n)"), pT5[:].bitcast(BF16)[:, 0:NCH * 128]), (pT5.b,), (Wt.b,))
            yield
            for c in range(NCH):
                yield
                pw = ps()
                sc.add("pe", lambda e, c=c, pw=pw: e.matmul(pw[:, 0:64], Wt[:, c, :], Sbf[:], start=True, stop=True), (Wt.b, Sbf.b), (pw.b,))
                sc.add("dve", lambda e, c=c, pw=pw: e.tensor_tensor(vnew[:], ufin[:, c, :], pw[:, 0:64], ALU.subtract), (ufin.b, pw.b), (vnew.b,))
                po = ps()
                sc.add("pe", lambda e, c=c, po=po: e.matmul(po[:, 0:64], Qbd[:, c, :], Sbf[:], start=True, stop=True), (Qbd.b, Sbf.b), (po.b,))
                sc.add("pe", lambda e, c=c, po=po: e.matmul(po[:, 64:128], attnT[:, c, :], vnew[:], start=True, stop=True), (attnT.b, vnew.b), (po.b,))
                sc.add("pe", lambda e, c=c, po=po: e.matmul(po[:, 128:192], Kdec[:, c, :], vnew[:], start=True, stop=True), (Kdec.b, vnew.b), (po.b,))
                sc.add("dve", lambda e, c=c, po=po: e.scalar_tensor_tensor(Sst[:], Sst[:], decf[:, c:c + 1], po[:, 128:192], ALU.mult, ALU.add),
                       (Sst.b, decf.b, po.b), (Sst.b,))
                sc.add("act", lambda e: e.copy(Sbf[:], Sst[:]), (Sst.b,), (Sbf.b,))
                sc.add("act", lambda e, po=po: e.copy(oB[:], po[:, 64:128]), (po.b,), (oB.b,))
                sc.add("dve", lambda e, c=c, po=po: e.scalar_tensor_tensor(otile[:, c, :], po[:, 0:64], expG[:, c:c + 1], oB[:], ALU.mult, ALU.add),
                       (po.b, expG.b, oB.b), (otile.b,))
            yield
            sc.add("pool", lambda e: e.tensor_tensor(Rbd[:, :, 0:64], otile[:], otile[:], ALU.mult), (otile.b,), (Rbd.b,))
            sc.add("dve", lambda e: e.tensor_reduce(oss[:], Rbd[:, :, 0:64], AX.X, ALU.add), (Rbd.b,), (oss.b,))
            sc.add("act", lambda e: e.activation(oss[:], oss[:], AF.Ln, bias=EPS, scale=1.0 / 64.0), (oss.b,), (oss.b,))
            sc.add("act", lambda e: e.activation(oss[:], oss[:], AF.Exp, scale=-0.5), (oss.b,), (oss.b,))
            sc.add("dve", lambda e: e.tensor_tensor(otile[:], otile[:], oss[:, :].unsqueeze(2).broadcast_to([128, NCH, 64]), ALU.mult),
                   (otile.b, oss.b), (otile.b,))
            sc.add("pool", lambda e: e.tensor_tensor(otile[:], otile[:], gnw[:, :].unsqueeze(1).broadcast_to([128, NCH, 64]), ALU.mult),
                   (otile.b, gnw.b), (otile.b,))
            for h in range(2):
                hs = slice(h * 64, (h + 1) * 64)
                sc.add("act", lambda e, hs=hs, h=h: e.copy(onbd[hs, :, h * 64:(h + 1) * 64], otile[hs, :, :]), (otile.b,), (onbd.b,))
            pT6 = ps()
            for c in range(NCH):
                sc.add("pe", lambda e, c=c, pT6=pT6: e.transpose(pT6[:].bitcast(BF16)[:, c * 128:(c + 1) * 128], onbd[:, c, :], ident[:]),
                       (onbd.b, ident.b), (pT6.b,))
            for h in range(2):
                hs = slice(h * 64, (h + 1) * 64)
                sc.add("dve", lambda e, hs=hs, h=h, tt=tt, pT6=pT6: e.tensor_tensor(
                    gbT[hs, tt * TT:(tt + 1) * TT].rearrange("p (c i) -> p c i", i=64),
                    pT6[:].bitcast(BF16)[hs, 0:NCH * 128].rearrange("p (c n) -> p c n", n=128)[:, :, h * 64:(h + 1) * 64],
                    zbS[hs, :].rearrange("p (c i) -> p c i", i=64), ALU.mult), (pT6.b, zbS.b), (gbT.b,))
        prevB = None
        for tt in range(TPS):
            interleave(tileA(st, slot, tt), prevB)
            prevB = tileB(st, slot, tt)

        def w2_stage_pieces(stage):
            def flat(t_):
                return t_[:].rearrange("p b n -> p (b n)") if len(t_[:].shape) == 3 else t_[:]
            def gate(t_, c):
                return [(t_, flat(t_)[:, j * TT:(j + 1) * TT], wg[:, c, j * TT:(j + 1) * TT]) for j in range(2048 // TT)]
            def two(t_, src, i):
                return [(t_, flat(t_)[:, (c % 2) * 1024 + j * TT:(c % 2) * 1024 + (j + 1) * TT], src[:, c, j * TT:(j + 1) * TT])
                        for c in (2 * i, 2 * i + 1) for j in range(1024 // TT)]
            if stage == 0:
                return two(Vt[0], wua, 1) + two(Vt[1], wub, 0) + two(Vt[2], wub, 1)
            if stage == 1:
                return gate(Kt[0][0], 0) + gate(Kt[0][1], 1) + gate(Qt[0], 6) + two(Vs[0][0], wo, 0) + two(Vs[0][1], wo, 1)
            if stage == 2:
                return gate(Kt[1][0], 2) + gate(Kt[1][1], 3) + gate(Qt[1], 7) + two(Vs[1][0], wo, 2) + two(Vs[1][1], wo, 3)
            return gate(Kt[2][0], 4) + gate(Kt[2][1], 5) + two(Qt[2], wua, 0)

        def emit_pieces(q, n):
            for _ in range(min(n, len(q))):
                t_, dst_ap, src_ap = q.pop(0)
                stb, k = ring()
                sc.add("sp", lambda e, stb=stb, src_ap=src_ap: e.dma_start(out=stb[:, :], in_=src_ap), (), (stb.b,), dkey=k)
                cast_to(dst_ap, t_.b, stb[:, :], stb.b)

        def attention(st, slot):
            wq = []
            for g in range(3):
                for q4 in range(4):
                    pv = ps()
                    for j in range(4):
                        blk = q4 * 4 + j
                        sc.add("pe", lambda e, pv=pv, j=j, blk=blk, g=g: e.transpose(
                            pv[:].bitcast(BF16)[:, j * 128:(j + 1) * 128], Vt[g][:, blk * 128:(blk + 1) * 128], ident[:]),
                            (Vt[g].b, ident.b), (pv.b,))
                    sc.add("act", lambda e, pv=pv, q4=q4, g=g, slot=slot: e.copy(
                        Vs[g][slot][:, q4 * 4:(q4 + 1) * 4, :].rearrange("p b n -> p (b n)"), pv[:].bitcast(BF16)[:, 0:512]),
                        (pv.b,), (Vs[g][slot].b,))
                    yield
            units = [(g, blk) for g in range(3) for blk in range(16)]
            if st == 3:
                wq.extend(w2_stage_pieces(0))

            def stage_s(u):
                g, blk = units[u]
                d = DIL[g]
                nps = 16 // d
                r, n = blk // nps, blk % nps
                halves = []
                if n > 0:
                    halves.append((0, slot, blk - 1))
                elif st > 0:
                    halves.append((0, 1 - slot, r * nps + nps - 1))
                halves.append((1, slot, blk))
                nh = len(halves)
                pS = ps()
                Pb = Pt[u % 3]
                qx = Qx[u % 2]
                sc.add("pool", lambda e, qx=qx, g=g, blk=blk: e.tensor_copy(qx[0:64, 0:128], Qt[g][0:64, blk * 128:(blk + 1) * 128]),
                       (Qt[g].b,), (qx.b,))
                sc.add("act", lambda e, qx=qx, g=g, blk=blk: e.copy(qx[64:128, 128:256], Qt[g][64:128, blk * 128:(blk + 1) * 128]),
                       (Qt[g].b,), (qx.b,))
                for (hf, sl, kb) in halves:
                    sc.add("pe", lambda e, hf=hf, sl=sl, kb=kb, g=g, pS=pS, qx=qx: e.matmul(
                        pS[:, hf * 256:(hf + 1) * 256], Kt[g][sl][:, kb * 128:(kb + 1) * 128], qx[:, :],
                        start=True, stop=True), (Kt[g][sl].b, qx.b), (pS.b,))
                lo = halves[0][0] * 256
                sc.add("act", lambda e, lo=lo, Pb=Pb, pS=pS: e.activation(
                    Pb[:, lo:512], pS[:, lo:512], AF.Exp, scale=0.125), (pS.b,), (Pb.b,))
                sc.add("dve", lambda e, Pb=Pb, lo=lo: e.tensor_tensor(Pb[:, lo:512], Pb[:, lo:512], dmask[:, lo:512], ALU.mult),
                       (Pb.b, dmask.b), (Pb.b,))
                return (g, d, r, n, halves, nh, Pb)

            def stage_pv(ctx):
                g, d, r, n, halves, nh, Pb = ctx
                pO = ps()
                for h in range(2):
                    hs = slice(h * 64, (h + 1) * 64)
                    for k, (hf, sl, kb) in enumerate(halves):
                        sc.add("pe", lambda e, h=h, hs=hs, hf=hf, sl=sl, kb=kb, k=k, g=g, Pb=Pb, pO=pO, nh=nh: e.matmul(
                            pO[hs, 0:128], Vs[g][sl][:, kb, h * 64:(h + 1) * 64], Pb[:, hf * 256 + h * 128:hf * 256 + (h + 1) * 128],
                            start=(k == 0), stop=(k == nh - 1)), (Vs[g][sl].b, Pb.b), (pO.b,))
                    for k, (hf, sl, kb) in enumerate(halves):
                        sc.add("pe", lambda e, h=h, hs=hs, hf=hf, k=k, Pb=Pb, pO=pO, nh=nh: e.matmul(
                            pO[hs, 128:256], ones[:, 0:64], Pb[:, hf * 256 + h * 128:hf * 256 + (h + 1) * 128],
                            start=(k == 0), stop=(k == nh - 1)), (ones.b, Pb.b), (pO.b,))
                off = n * 128 * d + r
                dstv = ndacc[:, :, off:off + 127 * d + 1:d] if d > 1 else ndacc[:, :, off:off + 128]
                srcv = pO[:, 0:256].rearrange("p (a q) -> p a q", a=2)
                if g == 0:
                    sc.add("act", lambda e, dstv=dstv, srcv=srcv: e.copy(dstv, srcv), (pO.b,), (ndacc.b,))
                else:
                    sc.add("dve", lambda e, dstv=dstv, srcv=srcv: e.tensor_tensor(dstv, srcv, dstv, ALU.add), (pO.b, ndacc.b), (ndacc.b,))

            ctx = stage_s(0)
            for u in range(len(units)):
                nxt = stage_s(u + 1) if u + 1 < len(units) else None
                stage_pv(ctx)
                ctx = nxt
                if st == 3:
                    if u == 15:
                        wq.extend(w2_stage_pieces(1))
                    if u == 31:
                        wq.extend(w2_stage_pieces(2))
                    emit_pieces(wq, 3)
                yield
            sc.add("act", lambda e: e.activation(ndacc[:, 1, :], ndacc[:, 1, :], AF.Ln), (ndacc.b,), (ndacc.b,))
            sc.add("act", lambda e: e.activation(ndacc[:, 1, :], ndacc[:, 1, :], AF.Exp, scale=-1.0), (ndacc.b,), (ndacc.b,))
            sc.add("dve", lambda e: e.tensor_tensor(ndacc[:, 0, :], ndacc[:, 0, :], ndacc[:, 1, :], ALU.mult), (ndacc.b,), (ndacc.b,))
            sc.add("pool", lambda e: e.tensor_tensor(gaT[:], ndacc[:, 0, :], zaS[:], ALU.mult), (ndacc.b, zaS.b), (gaT.b,))
            if st == 3:
                wq.extend(w2_stage_pieces(3))
                emit_pieces(wq, len(wq))
            yield

        interleave(attention(st, slot), prevB)
        if st > 0:
            copy_out(st - 1)
        sc.add("pool", lambda e, st=st: e.dma_start(out=bin_[st][0:128, :], in_=gaT[:]), (gaT.b,), (bin_b[st],), dkey="bi")
        sc.add("pool", lambda e, st=st: e.dma_start(out=bin_[st][128:256, :], in_=gbT[:]), (gbT.b,), (bin_b[st],), dkey="bi")
        sc.add("pool", lambda e, st=st: e.collective_compute(
            "AllGather", ALU.bypass, replica_groups=[[0, 1, 2, 3], [4, 5, 6, 7]], ins=[bin_[st][:, :]], outs=[bout[st][:, :]]),
            (bin_b[st],), (bout_b[st],), dkey="cc%d" % st)

    copy_out(3)
    es2 = es
    Wg = T.__new__(T); Wg.b = Buf("Wg")
    def alias(src, shape_str, dt=None, **kw):
        ap = src[:]
        if dt is not None:
            ap = ap.bitcast(dt)
        return ap

    class A:
        def __init__(self, ap, name, base=None, share=False):
            self.ap = ap
            if share:
                self.b = base.b
            else:
                self.b = Buf(name)
                if base is not None:
                    sc.link(self.b, base.b)

        def __getitem__(self, k):
            return self.ap[k]

    wg_parts = [Kt[0][0], Kt[0][1], Kt[1][0], Kt[1][1], Kt[2][0], Kt[2][1], Qt[0], Qt[1]]
    Wgc = [A(p_[:], "Wg%d" % i, p_, True) for i, p_ in enumerate(wg_parts)]
    Wua = [A(t_[:], "wua%d" % i, t_, True) for i, t_ in enumerate((Qt[2], Vt[0]))]
    Wub = [A(t_[:], "wub%d" % i, t_, True) for i, t_ in enumerate((Vt[1], Vt[2]))]
    wo_parts = [Vs[0][0], Vs[0][1], Vs[1][0], Vs[1][1]]
    Woc = [A(t_[:].rearrange("p b n -> p (b n)"), "wo%d" % i, t_, True) for i, t_ in enumerate(wo_parts)]
    fnw = A(ndacc[:, 0, 0:1024], "fnw", ndacc)
    xown = [A(ndacc[:, 1, 0:1024], "xown0", ndacc), A(ndacc[:, 1, 1024:2048], "xown1", ndacc)]
    ybuf = A(ndacc[:, 0, 1024:2048], "ybuf", ndacc)
    ga_g = A(Vs[2][0][:].rearrange("p b n -> p (b n)")[:, 0:4 * 512].rearrange("p (r t) -> p r t", r=4), "ga_g", Vs[2][0])
    gb_g = A(Vs[2][1][:].rearrange("p b n -> p (b n)")[:, 0:4 * 512].rearrange("p (r t) -> p r t", r=4), "gb_g", Vs[2][1])
    hTo = hT2[0]
    cur_hT[0] = hTo
    sgA = rt1
    sgB = rt2
    merged = A(gbT[:].rearrange("p (c t) -> p c t", c=8), "merged", gbT)
    ssq2 = A(oss[:, 0:1], "ssq2", oss)
    junk = A(gaT[:].bitcast(F32), "junk", gaT)

    def load_w2(dsts, src, nchunk, ncols, per):
        for c in range(nchunk):
            d_ = dsts[c // per]
            base = (c % per) * ncols
            for j in range(ncols // TT):
                stb, k = ring()
                sc.add("sp", lambda e, stb=stb, c=c, j=j: e.dma_start(out=stb[:, :], in_=src[:, c, j * TT:(j + 1) * TT]), (), (stb.b,), dkey=k)
                cast_to(d_[:, base + j * TT: base + (j + 1) * TT], d_.b, stb[:, :], stb.b)

    sc.add("sp", lambda e: e.dma_start(out=fnw[:, :], in_=fnw_d), (), (fnw.b,), dkey="fnw")

    for st in range(4):
        def p2(st, half):
            tcol = half * TT
            for r in range(4):
                sc.add("sp", lambda e, st=st, r=r, tcol=tcol: e.dma_start(
                    out=ga_g[:, r, 0:TT], in_=gown[st][r * 256:r * 256 + 128, tcol:tcol + TT]),
                    (gown_b[st],), (ga_g.b,), dkey="gag")
                sc.add("sp", lambda e, st=st, r=r, tcol=tcol: e.dma_start(
                    out=gb_g[:, r, 0:TT], in_=gown[st][r * 256 + 128:r * 256 + 256, tcol:tcol + TT]),
                    (gown_b[st],), (gb_g.b,), dkey="gbg")
            rmsnorm_tile(xTo, st * 512 + tcol, hTo)
            for mb in range(8):
                k2 = mb % 2
                pa = ps()
                for c in range(8):
                    sc.add("pe", lambda e, c=c, pa=pa, mb=mb: e.matmul(pa[:, 0:TT], Wgc[c][:, mb * 128:(mb + 1) * 128], hTo[:, c, :],
                                                                     start=(c == 0), stop=(c == 7)), (Wgc[c].b, hTo.b), (pa.b,))
                sc.add("act", lambda e, pa=pa, k2=k2: e.activation(sgA[k2][:], pa[:, 0:TT], AF.Sigmoid), (pa.b,), (sgA[k2].b,))
                pb_ = ps()
                for c in range(8):
                    sc.add("pe", lambda e, c=c, pb_=pb_, mb=mb: e.matmul(pb_[:, 0:TT], Wgc[c][:, 1024 + mb * 128:1024 + (mb + 1) * 128], hTo[:, c, :],
                                                                      start=(c == 0), stop=(c == 7)), (Wgc[c].b, hTo.b), (pb_.b,))
                sc.add("act", lambda e, pb_=pb_, k2=k2: e.activation(sgB[k2][:], pb_[:, 0:TT], AF.Sigmoid), (pb_.b,), (sgB[k2].b,))
                pya = ps()
                for r in range(4):
                    sc.add("pe", lambda e, r=r, pya=pya, mb=mb: e.matmul(
                        pya[:, 0:TT], Wua[r // 2][:, (r % 2) * 1024 + mb * 128:(r % 2) * 1024 + (mb + 1) * 128], ga_g[:, r, 0:TT],
                        start=(r == 0), stop=(r == 3)), (Wua[r // 2].b, ga_g.b), (pya.b,))
                pyb = ps()
                for r in range(4):
                    sc.add("pe", lambda e, r=r, pyb=pyb, mb=mb: e.matmul(
                        pyb[:, 0:TT], Wub[r // 2][:, (r % 2) * 1024 + mb * 128:(r % 2) * 1024 + (mb + 1) * 128], gb_g[:, r, 0:TT],
                        start=(r == 0), stop=(r == 3)), (Wub[r // 2].b, gb_g.b), (pyb.b,))
                sc.add("dve", lambda e, pya=pya, k2=k2: e.tensor_tensor(sgA[k2][:], pya[:, 0:TT], sgA[k2][:], ALU.mult), (pya.b, sgA[k2].b), (sgA[k2].b,))
                sc.add("dve", lambda e, pyb=pyb, k2=k2: e.tensor_tensor(sgB[k2][:], pyb[:, 0:TT], sgB[k2][:], ALU.mult), (pyb.b, sgB[k2].b), (sgB[k2].b,))
                sc.add("pool", lambda e, k2=k2, mb=mb: e.tensor_tensor(merged[:, mb, :], sgA[k2][:], sgB[k2][:], ALU.add), (sgA[k2].b, sgB[k2].b), (merged.b,))
            for tb in range(TT // 128):
                row0 = st * 512 + tcol + tb * 128
                xw = xown[tb % 2]
                sc.add("sp", lambda e, xw=xw, row0=row0: e.dma_start(out=xw[:, :], in_=xo[row0:row0 + 128, :]), (), (xw.b,), dkey="xo%d" % (tb % 2))
                po2 = [ps(), ps()]
                for nh_ in range(2):
                    for c in range(8):
                        sc.add("pe", lambda e, c=c, nh_=nh_, tb=tb, po2=po2: e.matmul(
                            po2[nh_][:, 0:512], merged[:, c, tb * 128:(tb + 1) * 128],
                            Woc[c // 2][:, (c % 2) * 1024 + nh_ * 512:(c % 2) * 1024 + (nh_ + 1) * 512],
                            start=(c == 0), stop=(c == 7)), (merged.b, Woc[c // 2].b), (po2[nh_].b,))
                for nh_ in range(2):
                    sc.add("dve", lambda e, nh_=nh_, xw=xw, po2=po2: e.tensor_tensor(
                        ybuf[:, nh_ * 512:(nh_ + 1) * 512], po2[nh_][:, 0:512], xw[:, nh_ * 512:(nh_ + 1) * 512], ALU.add),
                        (po2[nh_].b, xw.b), (ybuf.b,))
                sc.add("act", lambda e: e.activation(junk[:, :], ybuf[:, :], AF.Square, accum_out=ssq2[:]), (ybuf.b,), (junk.b, ssq2.b))
                sc.add("act", lambda e: e.activation(ssq2[:], ssq2[:], AF.Ln, bias=EPS, scale=1.0 / 1024.0), (ssq2.b,), (ssq2.b,))
                sc.add("act", lambda e: e.activation(ssq2[:], ssq2[:], AF.Exp, scale=-0.5), (ssq2.b,), (ssq2.b,))
                sc.add("dve", lambda e: e.scalar_tensor_tensor(ybuf[:, :], ybuf[:, :], ssq2[:, 0:1], fnw[:, :], ALU.mult, ALU.mult),
                       (ybuf.b, ssq2.b, fnw.b), (ybuf.b,))
                sc.add("sp", lambda e, row0=row0: e.dma_start(out=out_d[row0:row0 + 128, :], in_=ybuf[:, :]), (ybuf.b,), (Buf(),), dkey="out")
        for half in range(2):
            p2(st, half)
    fin = Buf("fin")
    last_out = [o for o in sc.ops["sp"] if o.dkey == "out"][-1]
    op = sc.add("sp", None, (), ())
    op.deps = [last_out]
    sc.reorder(window=200)
    build_program.stats = (sc.est_total, {e: len(sc.ops[e]) for e in sc.ENGS})
    sc.emit(nc, es)
    es.close()
    return nc


def _consts():
    bf = ml_dtypes.bfloat16
    idx = np.arange(128)
    h = idx // 64
    i = idx % 64
    same = (h[:, None] == h[None, :])
    c = {}
    sw = h * 64 + (i + 32) % 64
    perm = np.zeros((128, 128), np.float32)
    perm[sw, idx] = 1.0
    c["perm"] = perm.astype(bf)
    c["ident"] = np.eye(128, dtype=np.float32).astype(bf)
    c["onesbd"] = same.astype(np.float32).astype(bf)
    c["onesbdf"] = same.astype(np.float32)
    c["ones"] = np.ones((128, 128), np.float32).astype(bf)
    c["lbd"] = (same & (i[:, None] <= i[None, :])).astype(np.float32)
    ua = np.zeros((128, 130), np.float32)
    ua[:, :128] = (same & (i[:, None] > i[None, :])).astype(np.float32)
    ua[:, 128] = 1.0
    c["uaug"] = ua
    c["slneg"] = -(same & (i[:, None] > i[None, :])).astype(np.float32)
    c["li"] = (same & (i[:, None] >= i[None, :])).astype(np.float32)
    j_ = np.arange(128)[:, None]
    q_ = np.arange(128)[None, :]
    prev = (j_ >= q_).astype(np.float32)
    cur = (j_ <= q_).astype(np.float32)
    dm = np.concatenate([prev, prev, cur, cur], axis=1)
    c["dmask"] = dm.astype(bf)
    inv = (10000.0 ** (-np.arange(0, 64, 2, dtype=np.float32) / 64.0)).astype(np.float32)
    ang = (np.arange(S, dtype=np.float32)[:, None] * inv[None, :]).astype(np.float32)
    ang = np.concatenate([ang, ang], axis=-1)
    cos = np.cos(ang).astype(np.float32).T
    sin = np.sin(ang).astype(np.float32).T
    sgn = np.where(np.arange(64) < 32, -1.0, 1.0).astype(np.float32)[:, None]
    c["cosT"] = np.ascontiguousarray(np.concatenate([cos, cos], 0))
    c["sinT"] = np.ascontiguousarray(np.concatenate([sin * sgn, sin * sgn], 0))
    return c


def _chunk(w, nchunk):
    return np.ascontiguousarray(w.reshape(nchunk, 128, -1).transpose(1, 0, 2))


_NC = [None]


def kernel(x, norm_w, w_in, conv_w, a_log, dt_bias, gdn_norm_w, w_up_a, w_up_b, w_out, final_norm_w):
    x = np.asarray(x, np.float32)
    w_in0 = np.asarray(w_in, np.float32)[0]
    conv0 = np.asarray(conv_w, np.float32)[0]
    if _NC[0] is None:
        _NC[0] = build_program()
    nc = _NC[0]
    cst = _consts()
    shared = dict(cst)
    shared["wg"] = _chunk(w_in0[:, 7184:9232], 8)
    shared["wua"] = _chunk(np.asarray(w_up_a, np.float32)[0], 4)
    shared["wub"] = _chunk(np.asarray(w_up_b, np.float32)[0], 4)
    shared["wo"] = _chunk(np.asarray(w_out, np.float32)[0], 8)
    shared["normw"] = np.ascontiguousarray(np.asarray(norm_w, np.float32)[0].reshape(8, 128).T)
    shared["fnw"] = np.ascontiguousarray(np.broadcast_to(np.asarray(final_norm_w, np.float32)[None, :], (128, 1024)))
    shared["gnw"] = np.ascontiguousarray(np.broadcast_to(np.asarray(gdn_norm_w, np.float32)[0][None, :], (128, 64)))
    in_maps = []
    owns = []
    for core in range(8):
        b, c4 = core // 4, core % 4
        h0 = 2 * c4
        xTb = _chunk(np.ascontiguousarray(x[b].T), 8)
        own = np.concatenate([st * ST + c4 * 512 + np.arange(512) for st in range(4)])
        owns.append(own)
        cols = []
        for g in range(3):
            for t in range(3):
                s0 = g * 1536 + t * 512 + h0 * 64
                cols.append(np.arange(s0, s0 + 128))
        cols.append(np.arange(4608 + h0 * 64, 4608 + h0 * 64 + 128))
        for t in range(3):
            s0 = 5120 + t * 512 + h0 * 64
            cols.append(np.arange(s0, s0 + 128))
        cols.append(np.arange(6656 + h0 * 64, 6656 + h0 * 64 + 128))
        w1 = np.zeros((1024, NB1 * 128), np.float32)
        cc = np.concatenate(cols)
        w1[:, :14 * 128] = w_in0[:, cc]
        w1[:, 14 * 128 + 0] = w_in0[:, 7168 + h0]
        w1[:, 14 * 128 + 1] = w_in0[:, 7176 + h0]
        w1[:, 14 * 128 + 2] = w_in0[:, 7168 + h0 + 1]
        w1[:, 14 * 128 + 3] = w_in0[:, 7176 + h0 + 1]
        convw = np.zeros((128, 12), np.float32)
        for t in range(3):
            s0 = t * 512 + h0 * 64
            convw[:, t * 4:(t + 1) * 4] = conv0[:, s0:s0 + 128].T
        hh = np.arange(128) // 64
        m = dict(shared)
        m["xT"] = xTb
        m["xTo"] = np.ascontiguousarray(xTb[:, :, own])
        m["xo"] = np.ascontiguousarray(x[b][own])
        m["w1"] = _chunk(w1, 8)
        m["convw"] = convw
        m["alog"] = np.asarray(a_log, np.float32)[0][h0 + hh][:, None].copy()
        m["dtb"] = np.asarray(dt_bias, np.float32)[0][h0 + hh][:, None].copy()
        in_maps.append(m)
    res = run_bass_kernel_spmd(nc, in_maps, core_ids=list(range(8)))
    out = np.zeros((2, S, 1024), np.float32)
    for core in range(8):
        out[core // 4, owns[core]] = np.asarray(res.results[core]["out"], np.float32)
    return out
```
